# Optimizing a Trainium2 kernel written in Bass

```python
import jax, jax.numpy as jnp
from jax import lax
import numpy as np

D_MODEL = 2048
BATCH = 8
SEQ = 2048
DEPTH = 2

N_EVEN = (DEPTH + 1) // 2
N_ODD = DEPTH // 2
MLA_HEADS = 8
Q_LORA = 512
KV_LORA = 512
QK_NOPE = 128
QK_ROPE = 64
V_HEAD = 128
ROPE_BASE = 10000.0
Q_BLOCK = 128
SGU_GROUPS = 8
SGU_CH = 128
CHUNK = 128
CONV_DIM = D_MODEL
CONV_WIDTH = 3
D_FF = 4 * D_MODEL
EPS = 1e-6

MLA_OUT = MLA_HEADS * V_HEAD
SGU_OUT = SGU_GROUPS * SGU_CH
MIX_WIDTH = MLA_OUT + SGU_OUT
EVEN_IN = Q_LORA + KV_LORA + QK_ROPE + 2 * SGU_OUT

kernel_name = "hybrid_mla_sgu_shortconv_block"


def rms_norm(x, g):
    xf = x.astype(jnp.float32)
    y = xf * lax.rsqrt(jnp.mean(xf * xf, axis=-1, keepdims=True) + EPS)
    return (y * g.astype(jnp.float32)).astype(x.dtype)


def group_layer_norm(v, g):
    vf = v.astype(jnp.float32)
    mu = jnp.mean(vf, axis=-1, keepdims=True)
    var = jnp.mean(jnp.square(vf - mu), axis=-1, keepdims=True)
    return ((vf - mu) * lax.rsqrt(var + EPS) * g.astype(jnp.float32)).astype(v.dtype)


def rope_tables(positions):
    inv_freq = ROPE_BASE ** (-jnp.arange(0, QK_ROPE, 2, dtype=jnp.float32) / QK_ROPE)
    ang = positions.astype(jnp.float32)[..., None] * inv_freq
    return jnp.cos(ang), jnp.sin(ang)


def apply_rope(x, cos, sin):
    xf = x.astype(jnp.float32)
    x1, x2 = xf[..., : QK_ROPE // 2], xf[..., QK_ROPE // 2:]
    return jnp.concatenate([x1 * cos - x2 * sin, x2 * cos + x1 * sin], axis=-1).astype(x.dtype)


def mla_mixer(c_q, c_kv, k_rope_raw, cos, sin, q_norm, w_uq, kv_norm, w_ukv):
    B, S, _ = c_q.shape
    q = (rms_norm(c_q, q_norm) @ w_uq).reshape(B, S, MLA_HEADS, QK_NOPE + QK_ROPE)
    q_nope = q[..., :QK_NOPE]
    q_rope = apply_rope(q[..., QK_NOPE:], cos[:, :, None], sin[:, :, None])
    kv = (rms_norm(c_kv, kv_norm) @ w_ukv).reshape(B, S, MLA_HEADS, QK_NOPE + V_HEAD)
    k_nope, v = kv[..., :QK_NOPE], kv[..., QK_NOPE:]
    k_rope = apply_rope(k_rope_raw, cos, sin)
    scale = (QK_NOPE + QK_ROPE) ** -0.5
    outs = []
    for i in range(S // Q_BLOCK):
        q0, kend = i * Q_BLOCK, (i + 1) * Q_BLOCK
        s = (jnp.einsum('bqhd,bkhd->bhqk', q_nope[:, q0:kend], k_nope[:, :kend])
             + jnp.einsum('bqhr,bkr->bhqk', q_rope[:, q0:kend], k_rope[:, :kend]))
        s = s.astype(jnp.float32) * scale
        q_idx = q0 + jnp.arange(Q_BLOCK)
        mask = jnp.arange(kend)[None, :] <= q_idx[:, None]
        p = jax.nn.softmax(jnp.where(mask, s, -jnp.inf), axis=-1).astype(v.dtype)
        outs.append(jnp.einsum('bhqk,bkhd->bqhd', p, v[:, :kend]))
    o = jnp.concatenate(outs, axis=1)
    return o.reshape(B, S, MLA_OUT)


def sgu_mixer(uv, v_norm, w_s, b_s):
    B, S, _ = uv.shape
    uv = jax.nn.gelu(uv)
    u, v = uv[..., :SGU_OUT], uv[..., SGU_OUT:]
    v = group_layer_norm(v.reshape(B, S, SGU_GROUPS, SGU_CH), v_norm)
    v = v.reshape(B, S // CHUNK, CHUNK, SGU_GROUPS, SGU_CH)
    w = jnp.tril(w_s)
    y = jnp.einsum('gts,bcsgd->bctgd', w, v) + b_s.T[None, None, :, :, None]
    return u * y.reshape(B, S, SGU_OUT)


def short_conv_mixer(h, w_in, conv_w, w_out):
    proj = h @ w_in
    b_gate = proj[..., :CONV_DIM]
    c_gate = proj[..., CONV_DIM:2 * CONV_DIM]
    xin = proj[..., 2 * CONV_DIM:]
    z = c_gate * xin
    z = lax.conv_general_dilated(
        z, conv_w[:, None, :].astype(z.dtype), window_strides=(1,),
        padding=[(CONV_WIDTH - 1, 0)], dimension_numbers=('NWC', 'WIO', 'NWC'),
        feature_group_count=CONV_DIM)
    return (b_gate * z) @ w_out


def sqrelu_mlp(h, w1, w2):
    a = jax.nn.relu(h @ w1)
    return (a * a) @ w2


def setup_inputs(seed: int = 0) -> dict:
    key = jax.random.key(seed)
    ks = jax.random.split(key, 24)

    def nrm(k, shape, scale):
        return jax.random.normal(k, shape, jnp.float32) * scale

    def gain(k, shape):
        return 1.0 + 0.02 * jax.random.normal(k, shape, jnp.float32)

    x = jax.random.normal(ks[0], (BATCH, SEQ, D_MODEL), jnp.float32)
    offset = jax.random.randint(ks[1], (BATCH, 1), 0, 4096, dtype=jnp.int32)
    positions = offset + jnp.arange(SEQ, dtype=jnp.int32)[None, :]
    return {
        "x": x,
        "positions": positions,
        "e_norm_mix": gain(ks[2], (N_EVEN, D_MODEL)),
        "e_w_in": nrm(ks[3], (N_EVEN, D_MODEL, EVEN_IN), D_MODEL ** -0.5),
        "e_q_norm": gain(ks[4], (N_EVEN, Q_LORA)),
        "e_w_uq": nrm(ks[5], (N_EVEN, Q_LORA, MLA_HEADS * (QK_NOPE + QK_ROPE)), Q_LORA ** -0.5),
        "e_kv_norm": gain(ks[6], (N_EVEN, KV_LORA)),
        "e_w_ukv": nrm(ks[7], (N_EVEN, KV_LORA, MLA_HEADS * (QK_NOPE + V_HEAD)), KV_LORA ** -0.5),
        "e_v_norm": gain(ks[8], (N_EVEN, SGU_GROUPS, SGU_CH)),
        "e_sgu_w": nrm(ks[9], (N_EVEN, SGU_GROUPS, CHUNK, CHUNK), CHUNK ** -0.5),
        "e_sgu_b": 1.0 + nrm(ks[10], (N_EVEN, SGU_GROUPS, CHUNK), 0.1),
        "e_mla_out_norm": gain(ks[11], (N_EVEN, MLA_OUT)),
        "e_sgu_out_norm": gain(ks[12], (N_EVEN, SGU_OUT)),
        "e_w_out": nrm(ks[13], (N_EVEN, MIX_WIDTH, D_MODEL), MIX_WIDTH ** -0.5),
        "o_norm_mix": gain(ks[14], (N_ODD, D_MODEL)),
        "o_w_in": nrm(ks[15], (N_ODD, D_MODEL, 3 * CONV_DIM), D_MODEL ** -0.5),
        "o_conv_w": nrm(ks[16], (N_ODD, CONV_WIDTH, CONV_DIM), CONV_WIDTH ** -0.5),
        "o_w_out": nrm(ks[17], (N_ODD, CONV_DIM, D_MODEL), CONV_DIM ** -0.5),
        "mlp_norm": gain(ks[18], (DEPTH, D_MODEL)),
        "mlp_w1": nrm(ks[19], (DEPTH, D_MODEL, D_FF), D_MODEL ** -0.5),
        "mlp_w2": nrm(ks[20], (DEPTH, D_FF, D_MODEL), 0.5 * D_FF ** -0.5),
        "final_norm": gain(ks[21], (D_MODEL,)),
    }


def reference(x, positions, e_norm_mix, e_w_in, e_q_norm, e_w_uq, e_kv_norm, e_w_ukv,
              e_v_norm, e_sgu_w, e_sgu_b, e_mla_out_norm, e_sgu_out_norm, e_w_out,
              o_norm_mix, o_w_in, o_conv_w, o_w_out, mlp_norm, mlp_w1, mlp_w2, final_norm):
    cos, sin = rope_tables(positions)
    c1 = Q_LORA
    c2 = c1 + KV_LORA
    c3 = c2 + QK_ROPE
    for layer in range(DEPTH):
        i = layer // 2
        if layer % 2 == 0:
            h = rms_norm(x, e_norm_mix[i])
            proj = h @ e_w_in[i]
            a = mla_mixer(proj[..., :c1], proj[..., c1:c2], proj[..., c2:c3], cos, sin,
                          e_q_norm[i], e_w_uq[i], e_kv_norm[i], e_w_ukv[i])
            s = sgu_mixer(proj[..., c3:], e_v_norm[i], e_sgu_w[i], e_sgu_b[i])
            mixed = jnp.concatenate([rms_norm(a, e_mla_out_norm[i]),
                                     rms_norm(s, e_sgu_out_norm[i])], axis=-1)
            x = x + mixed @ e_w_out[i]
        else:
            x = x + short_conv_mixer(rms_norm(x, o_norm_mix[i]), o_w_in[i], o_conv_w[i], o_w_out[i])
        x = x + sqrelu_mlp(rms_norm(x, mlp_norm[layer]), mlp_w1[layer], mlp_w2[layer])
    return rms_norm(x, final_norm)
```

```python
import math
import numpy as np
import concourse.bass as bass
import concourse.mybir as mybir
from concourse.bass_utils import run_bass_kernel_spmd

F32 = mybir.dt.float32
BF16 = mybir.dt.bfloat16
I32 = mybir.dt.int32
AF = mybir.ActivationFunctionType
ALU = mybir.AluOpType
AX = mybir.AxisListType

D = 2048
SEQ = 2048
TH = 1024
NCORE = 8
EPS = 1e-6
RING = 8
SBUF_BASE = 16512
SBUF_TOP = 229344
SCALE = (128 + 64) ** -0.5
MAGIC = 12582912.0
C1 = 6.28125
C2 = 2.0 * math.pi - C1
NT_TILES = 371


class Buf:
    __slots__ = ("name", "w", "r", "dsem", "dcnt", "ring")

    def __init__(self, name, ring=False):
        self.name = name
        self.w = None
        self.r = {}
        self.dsem = None
        self.dcnt = 0
        self.ring = ring


class K:
    def __init__(self, nc):
        self.nc = nc
        self.eng = {"pe": nc.tensor, "act": nc.scalar, "dve": nc.vector,
                    "pool": nc.gpsimd, "sp": nc.sync}
        self.esem = {}
        self.ecnt = {}
        for e in ("pe", "act", "dve"):
            self.esem[e] = nc.alloc_semaphore(name="c_" + e)
            self.ecnt[e] = 0
        self.known = {e: {} for e in self.eng}
        self.prims = []
        self.ninst = {e: 0 for e in self.eng}

    def _wait(self, e, tok):
        if tok is None:
            return
        sem, val = tok
        if e == "pe" and sem is self.esem["pe"]:
            return
        key = id(sem)
        if self.known[e].get(key, 0) >= val:
            return
        self.eng[e].wait_ge(sem, val)
        self.known[e][key] = val
        self.ninst[e] += 1

    def _deps(self, e, reads, writes):
        for b in reads:
            self._wait(e, b.w)
        for b in writes:
            self._wait(e, b.w)
            for t in list(b.r.values()):
                self._wait(e, t)

    def _pend(self, e, reads, writes):
        out = {}
        toks = [b.w for b in reads]
        for b in writes:
            toks.append(b.w)
            toks.extend(b.r.values())
        for tok in toks:
            if tok is None:
                continue
            sem, val = tok
            if e == "pe" and sem is self.esem["pe"]:
                continue
            key = id(sem)
            if self.known[e].get(key, 0) >= val:
                continue
            if key in out and out[key][1] >= val:
                continue
            out[key] = tok
        return list(out.values())

    def _issue(self, e, pend, fn):
        emb = pend.pop() if pend else None
        for sem, val in pend:
            self.eng[e].wait_ge(sem, val)
            self.known[e][id(sem)] = val
            self.ninst[e] += 1
        ins = fn(self.eng[e])
        if emb is not None:
            ins._wait_ge(emb[0], emb[1])
            self.known[e][id(emb[0])] = emb[1]
        return ins

    @staticmethod
    def _commit(tok, reads, writes):
        key = id(tok[0])
        for b in reads:
            old = b.r.get(key)
            if old is None or old[1] < tok[1]:
                b.r[key] = tok
        for b in writes:
            b.w = tok
            b.r = {}

    def op(self, e, fn, reads=(), writes=()):
        ins = self._issue(e, self._pend(e, reads, writes), fn)
        self.ecnt[e] += 1
        ins.then_inc(self.esem[e], 1)
        tok = (self.esem[e], self.ecnt[e])
        self._commit(tok, reads, writes)
        self.ninst[e] += 1
        return tok

    def mm(self, fns, reads=(), writes=(), per=None):
        e = "pe"
        ins = None
        for i, fn in enumerate(fns):
            r_i = (list(reads) if i == 0 else []) + (list(per[i]) if per is not None else [])
            ins = self._issue(e, self._pend(e, r_i, writes if i == 0 else ()), fn)
        self.ecnt[e] += 1
        ins.then_inc(self.esem[e], 1)
        tok = (self.esem[e], self.ecnt[e])
        if per is not None:
            reads = list(reads) + [b for p in per for b in p]
        self._commit(tok, reads, writes)
        self.ninst[e] += len(fns)
        return tok

    def dma(self, q, pairs, prim, reads=(), writes=()):
        if prim.dsem is None:
            prim.dsem = self.nc.alloc_semaphore(name="d_" + prim.name)
            self.prims.append(prim)
        if prim.dcnt:
            self._wait(q, (prim.dsem, prim.dcnt))
        self._deps(q, reads, writes)
        for out_ap, in_ap in pairs:
            ins = self.eng[q].dma_start(out=out_ap, in_=in_ap)
            prim.dcnt += 16
            ins.then_inc(prim.dsem, 16)
            self.ninst[q] += 1
        tok = (prim.dsem, prim.dcnt)
        self._commit(tok, reads, writes)
        return tok

    @staticmethod
    def handoff(old, new):
        acc = {}
        for b in old:
            toks = list(b.r.values()) + ([b.w] if b.w is not None else [])
            for t in toks:
                key = id(t[0])
                if key not in acc or acc[key][1] < t[1]:
                    acc[key] = t
        for b in new:
            b.w = None
            b.r = dict(acc)

    def barrier(self):
        toks = [(self.esem[x], self.ecnt[x]) for x in ("pe", "act", "dve") if self.ecnt[x]]
        toks += [(b.dsem, b.dcnt) for b in self.prims if b.dcnt and not b.ring]
        for e in ("pe", "act", "dve", "sp"):
            for t in toks:
                self._wait(e, t)

    def final(self):
        for b in self.prims:
            if b.dcnt:
                self._wait("sp", (b.dsem, b.dcnt))
        for x in ("pe", "act", "dve"):
            self._wait("sp", (self.esem[x], self.ecnt[x]))


class WStream:
    def __init__(self, k, nc, wdram, base_off, nt):
        self.k, self.nc, self.w, self.nt = k, nc, wdram, nt
        self.slots = [nc.alloc_sbuf_tensor_at(f"ws{i}", [128, 2048], BF16, offset=base_off + i * 4096)
                      for i in range(RING)]
        self.bufs = [Buf(f"ws{i}", ring=True) for i in range(RING)]
        self.specs = []
        self.consumed = 0
        self.released = 0
        self.issued = 0

    def pump(self):
        while self.issued < min(2 * self.nt, self.released + RING):
            s = self.issued % RING
            self.k.dma("pool", [(self.slots[s][:], self.w[self.issued % self.nt])], self.bufs[s],
                       writes=[self.bufs[s]])
            self.issued += 1

    def next(self, spec):
        if self.consumed < self.nt:
            self.specs.append(spec)
        assert self.consumed < 2 * self.nt
        assert self.issued > self.consumed, (self.issued, self.consumed)
        s = self.consumed % RING
        self.consumed += 1
        return self.slots[s], self.bufs[s]

    def release(self, n=1):
        self.released += n
        assert self.released <= self.consumed
        self.pump()


class Prog:
    def __init__(self, dbg=None, nt_hint=None):
        self.dbg = dbg
        nc = bass.Bass("TRN2", target_bir_lowering=False)
        self.nc = nc
        self.k = K(nc)
        self.nt_hint = nt_hint
        self.build()

    def sb(self, name, shape, dt, off):
        return self.nc.alloc_sbuf_tensor_at(name, shape, dt, offset=SBUF_BASE + off)

    def bank(self, rot):
        i = rot[0]
        rot.append(rot.pop(0))
        return self.ps[i], self.psb[i]

    def build(self):
        nc, k = self.nc, self.k
        NT = self.nt_hint
        self.xT = nc.dram_tensor("xT", [16, 128, SEQ], F32, kind="ExternalInput").ap()
        self.posd = nc.dram_tensor("pos", [1, SEQ], I32, kind="ExternalInput").ap()
        self.wd = nc.dram_tensor("wstream", [NT, 128, 2048], F32, kind="ExternalInput").ap()
        self.cstd = nc.dram_tensor("cst", [128, 192], F32, kind="ExternalInput").ap()
        self.trid = nc.dram_tensor("tri", [128, 128], F32, kind="ExternalInput").ap()
        self.swd = nc.dram_tensor("swT", [128, 1024], F32, kind="ExternalInput").ap()
        self.sbd = nc.dram_tensor("sgub", [1, 1024], F32, kind="ExternalInput").ap()
        self.gvd = nc.dram_tensor("gv", [1, 1024], F32, kind="ExternalInput").ap()
        self.outT = nc.dram_tensor("outT", [16, 128, SEQ], F32, kind="ExternalOutput").ap()
        self.kd = nc.dram_tensor("kd", [128, 8, TH], BF16, kind="Internal").ap()
        self.vd = nc.dram_tensor("vd", [128, 8, 1024], BF16, kind="Internal").ap()
        if self.dbg:
            self.dbgd = nc.dram_tensor("dbg", [16, 128, TH], F32, kind="ExternalOutput").ap()
        self.pp = [nc.alloc_psum_tensor(f"pp{i}", [128, 1024], F32) for i in range(4)]
        self.ps = [self.pp[i // 2][:, (i % 2) * 512:(i % 2 + 1) * 512] for i in range(8)]
        self.psb = [Buf(f"ps{i}") for i in range(8)]
        o = 0
        self.cst = self.sb("cst", [128, 192], F32, o); o += 768
        self.ones16 = self.sb("ones16", [128, 128], BF16, o); o += 256
        self.tri16 = self.sb("tri16", [128, 128], BF16, o); o += 256
        self.ones32 = self.sb("ones32", [1, 128], F32, o); o += 512
        self.sgub = self.sb("sgub", [1, 1024], F32, o); o += 4096
        self.swT16 = self.sb("swT16", [128, 8, 128], BF16, o); o += 2048
        self.gvb = self.sb("gvb", [128, 1024], F32, o); o += 4096
        self.Ctab = self.sb("Ctab", [64, TH], F32, o); o += 4096
        self.Stab = self.sb("Stab", [64, TH], F32, o); o += 4096
        self.krT = self.sb("krT", [64, SEQ], BF16, o); o += 4096
        self.carry = self.sb("carry", [128, 16, 2], F32, o); o += 128
        self.zb = self.sb("zb", [128, 1026], F32, o); o += 4128
        self.st = self.sb("st", [128, 2, 32], F32, o); o += 256
        self.rstd = self.sb("rstd", [128, TH], F32, o); o += 4096
        self.sq = [self.sb(f"sq{i}", [128, TH], BF16, o + 2048 * i) for i in range(2)]; o += 4096
        self.T0 = self.sb("T0", [128, TH], F32, o); T0o = o; o += 4096
        self.T1 = self.sb("T1", [128, TH], F32, o); T1o = o; o += 4096
        self.PT = [self.sb(f"PT{i}", [128, 512], BF16, T1o + 1024 * i) for i in range(4)]
        self.ws = WStream(k, nc, self.wd, SBUF_BASE + o, NT); o += RING * 4096
        XO = o; o += 65536
        HO = o; o += 32768
        BO = o; o += 32768
        assert SBUF_BASE + o <= SBUF_TOP, o
        self.xs = self.sb("xs", [128, 16, TH], F32, XO)
        self.posi = self.sb("posi", [64, TH], I32, BO)
        self.ang = self.sb("ang", [64, TH], F32, BO + 4096)
        self.tt = self.sb("tt", [64, TH], F32, BO + 8192)
        self.rr = self.sb("rr", [64, TH], F32, BO + 12288)
        self.cq32 = self.sb("cq32", [128, 4, TH], F32, XO)
        self.cqn = self.sb("cqn", [128, 4, TH], BF16, XO + 16384)
        self.ckvn = self.sb("ckvn", [128, 4, TH], BF16, XO + 24576)
        self.vn = self.sb("vn", [128, 8, 1024], BF16, XO + 32768)
        self.ug = [self.sb(f"ug{i}", [128, TH], F32, XO + 49152 + 4096 * i) for i in range(2)]
        self.kh = [self.sb(f"kh{i}", [128, SEQ], BF16, XO + 4096 * i) for i in range(2)]
        self.qn = [self.sb(f"qn{i}", [128, TH], BF16, XO + 8192 + 4096 * i) for i in range(2)]
        self.qr = [self.sb(f"qr{i}", [64, TH], BF16, XO + 8192 + 4096 * i + 2048) for i in range(2)]
        self.a32 = self.sb("a32", [128, 8, TH], F32, XO + 32768)
        self.hT = self.sb("hT", [128, 16, TH], BF16, HO)
        self.RB = self.sb("RB", [128, 16, TH], BF16, BO)
        self.s32hi = self.sb("s32hi", [128, 4, TH], F32, BO)
        B = Buf
        self.b_cst, self.b_ones, self.b_tri, self.b_sgub, self.b_sw, self.b_gv = (
            B("cst"), B("ones"), B("tri"), B("sgub"), B("sw"), B("gv"))
        self.b_C, self.b_S, self.b_kr, self.b_carry, self.b_zb = B("C"), B("S"), B("kr"), B("carry"), B("zb")
        self.b_st = [B("st0"), B("st1")]
        self.b_rstd = B("rstd")
        self.b_sq = [B("sq0"), B("sq1")]
        self.b_T0 = [B("T0a"), B("T0b")]
        self.b_T1 = [B("T1a"), B("T1b")]
        self.b_PT = [B(f"PT{i}") for i in range(4)]
        self.b_xs = [B(f"xs{i}") for i in range(16)]
        self.b_xsio = [B(f"xsio{i}") for i in range(4)]
        self.b_cq = self.b_xs[0:4]
        self.b_cqn, self.b_ckvn = self.b_xs[4], self.b_xs[6]
        self.b_cqn2, self.b_ckvn2 = self.b_xs[5], self.b_xs[7]
        self.b_vn = [self.b_xs[8 + i // 2] for i in range(8)]
        self.b_ug = [self.b_xs[12], self.b_xs[13]]
        self.b_kh = self.b_xs[0:2]
        self.b_qn = self.b_xs[2:4]
        self.b_qr = self.b_xs[2:4]
        self.b_a32 = self.b_xs[8:16]
        self.b_h = [B(f"h{i}") for i in range(16)]
        self.b_rb = [B(f"rb{i}") for i in range(16)]
        self.b_vd, self.b_kd, self.b_out, self.b_dbg = B("vd"), B("kd"), B("out"), B("dbg")
        self.b_vio = B("vio")

        self.init_consts()
        self.load_xs(0)
        for b in self.b_xsio:
            k._wait("pool", (b.dsem, b.dcnt))
        self.ws.pump()
        for h in range(2):
            self.half(h)
        assert self.ws.consumed == 2 * NT and len(self.ws.specs) == NT, (self.ws.consumed, len(self.ws.specs))
        k.final()

    def init_consts(self):
        k = self.k
        k.dma("sp", [(self.cst[:], self.cstd)], self.b_cst, writes=[self.b_cst])
        k.dma("sp", [(self.T1[:, 0:128], self.trid)], self.b_T1[0], writes=[self.b_T1[0]])
        k.dma("sp", [(self.T0[:], self.swd)], self.b_T0[0], writes=self.b_T0)
        k.dma("sp", [(self.sgub[:], self.sbd)], self.b_sgub, writes=[self.b_sgub])
        k.dma("sp", [(self.gvb[:], self.gvd.partition_broadcast(128))], self.b_gv, writes=[self.b_gv])
        k.op("dve", lambda e: e.memset(self.ones16[:], 1.0), writes=[self.b_ones])
        k.op("dve", lambda e: e.memset(self.ones32[:], 1.0), writes=[self.b_ones])
        k.op("dve", lambda e: e.tensor_copy(self.tri16[:], self.T1[:, 0:128]), reads=[self.b_T1[0]], writes=[self.b_tri])
        for g in range(8):
            k.op("dve", lambda e, g=g: e.tensor_tensor(self.swT16[:, g, :], self.T0[:, g * 128:(g + 1) * 128],
                                                      self.T1[:, 0:128], ALU.mult),
                 reads=self.b_T0 + [self.b_T1[0]], writes=[self.b_sw])
        k.op("dve", lambda e: e.memset(self.zb[:, 0:2], 0.0), writes=[self.b_zb])
        k.barrier()

    def gcol(self, base, i):
        return self.cst[:, base + i:base + i + 1]

    def rms_rstd(self, chunks, nfeat, banks=(6, 7)):
        k = self.k
        n = len(chunks)
        for c, (ap, bufs) in enumerate(chunks):
            sq, sqb = self.sq[c % 2], self.b_sq[c % 2]
            k.op("act", lambda e, ap=ap, sq=sq: e.activation(sq[:], ap, AF.Square), reads=bufs, writes=[sqb])
            for tg in range(2):
                pt, pb = self.ps[banks[tg]], self.psb[banks[tg]]
                k.mm([lambda e, pt=pt, sq=sq, tg=tg, c=c: e.matmul(pt[:], self.ones16[:], sq[:, tg * 512:(tg + 1) * 512],
                                                                 start=(c == 0), stop=(c == n - 1))],
                     reads=[sqb, self.b_ones], writes=[pb])
        for tg in range(2):
            pt, pb = self.ps[banks[tg]], self.psb[banks[tg]]
            k.op("act", lambda e, pt=pt, tg=tg: e.activation(self.rstd[:, tg * 512:(tg + 1) * 512], pt[:], AF.Sqrt,
                                                            bias=EPS, scale=1.0 / nfeat),
                 reads=[pb], writes=[self.b_rstd])
        k.op("dve", lambda e: e.reciprocal(self.rstd[:], self.rstd[:]), reads=[self.b_rstd], writes=[self.b_rstd])

    def norm_apply(self, src, srcbufs, gbase, i, dst, dstbufs):
        self.k.op("dve", lambda e: e.scalar_tensor_tensor(dst, src, self.gcol(gbase, i), self.rstd[:], ALU.mult, ALU.mult),
                  reads=list(srcbufs) + [self.b_rstd, self.b_cst], writes=list(dstbufs))

    def rope_evac(self, pa, pab, pbk, pbb, tg, out_ap, outbufs):
        k = self.k
        ta, tb = self.T0[0:64, 0:512], self.T0[0:64, 512:1024]
        k.op("dve", lambda e: e.tensor_tensor(ta, pa[0:64, :], self.Ctab[:, tg * 512:(tg + 1) * 512], ALU.mult),
             reads=[pab, self.b_C], writes=[self.b_T0[0]])
        k.op("dve", lambda e: e.tensor_tensor(tb, pbk[0:64, :], self.Stab[:, tg * 512:(tg + 1) * 512], ALU.mult),
             reads=[pbb, self.b_S], writes=[self.b_T0[1]])
        k.op("dve", lambda e: e.tensor_tensor(out_ap, ta, tb, ALU.add),
             reads=self.b_T0, writes=outbufs)

    def tap(self, name, chunks):
        if self.dbg != name or getattr(self, "_tapped", False):
            return
        self._tapped = True
        k = self.k
        k.barrier()
        for c, (ap, bufs) in enumerate(chunks):
            p = ap.shape[0]
            n = ap.shape[-1]
            k.op("act", lambda e, ap=ap, p=p, n=n: e.activation(self.T0[0:p, 0:n], ap, AF.Copy),
                 reads=bufs, writes=self.b_T0)
            k.dma("sp", [(self.dbgd[c, 0:p, 0:n], self.T0[0:p, 0:n])], self.b_T0[0], reads=self.b_T0, writes=[self.b_dbg])
        k.barrier()

    def half(self, h):
        k = self.k
        t0 = h * TH
        if h > 0:
            self.load_xs(h)
        self.tables(h)
        self.rms_rstd([(self.xs[:, c, :], [self.b_xs[c]]) for c in range(16)], D)
        for c in range(16):
            self.norm_apply(self.xs[:, c, :], [self.b_xs[c]], 0, c, self.hT[:, c, :], [self.b_h[c]])
        self.tap("h0", [(self.hT[:, c, :], [self.b_h[c]]) for c in range(16)])
        self.l0_inproj(h)
        k.handoff(self.b_T1, self.b_PT)
        self.l0_attn(h)
        k.handoff(self.b_PT, self.b_T1)
        self.tap("mix", [(self.RB[:, c, :], [self.b_rb[c]]) for c in range(16)])
        self.out_proj("e_w_out", 0, xh=h, next_g=16)
        self.tap("x1", [(self.xs[:, c, :], [self.b_xs[c]]) for c in range(16)])
        self.mlp(0)
        self.tap("x2", [(self.xs[:, c, :], [self.b_xs[c]]) for c in range(16)])
        self.rms_rstd([(self.xs[:, c, :], [self.b_xs[c]]) for c in range(16)], D)
        for c in range(16):
            self.norm_apply(self.xs[:, c, :], [self.b_xs[c]], 32, c, self.hT[:, c, :], [self.b_h[c]])
        self.l1_conv(h)
        self.out_proj("o_w_out", 0, next_g=48)
        self.tap("x3", [(self.xs[:, c, :], [self.b_xs[c]]) for c in range(16)])
        self.mlp(1)
        self.rms_rstd([(self.xs[:, c, :], [self.b_xs[c]]) for c in range(16)], D)
        for c in range(16):
            T, tb = (self.T0, self.b_T0) if c % 2 == 0 else (self.T1, self.b_T1)
            self.norm_apply(self.xs[:, c, :], [self.b_xs[c]], 64, c, T[:], tb)
            k.dma("sp", [(self.outT[c, :, t0:t0 + TH], T[:])], tb[0], reads=tb, writes=[self.b_out])

    def load_xs(self, h):
        t0 = h * TH
        for b in range(4):
            self.k.dma("sp", [(self.xs[:, c, :], self.xT[c, :, t0:t0 + TH]) for c in range(4 * b, 4 * b + 4)],
                       self.b_xsio[b], writes=self.b_xs[4 * b:4 * b + 4])

    def tables(self, h):
        k = self.k
        t0 = h * TH
        k.dma("sp", [(self.posi[:], self.posd[:, t0:t0 + TH].partition_broadcast(64))], self.b_rb[0], writes=self.b_rb[0:2])
        k.op("dve", lambda e: e.tensor_copy(self.ang[:], self.posi[:]), reads=self.b_rb[0:8], writes=self.b_rb[0:8])
        k.op("dve", lambda e: e.tensor_scalar(self.ang[:], self.ang[:], self.cst[0:64, 152:153], None, ALU.mult),
             reads=self.b_rb[0:8] + [self.b_cst], writes=self.b_rb[0:8])
        for which in range(2):
            if which == 1:
                k.op("dve", lambda e: e.tensor_scalar(self.ang[:], self.ang[:], math.pi / 2, None, ALU.add),
                     reads=self.b_rb[0:8], writes=self.b_rb[0:8])
            k.op("dve", lambda e: e.tensor_scalar(self.tt[:], self.ang[:], 1.0 / (2 * math.pi), MAGIC, ALU.mult, ALU.add),
                 reads=self.b_rb[0:8], writes=self.b_rb[0:8])
            k.op("dve", lambda e: e.tensor_scalar(self.tt[:], self.tt[:], MAGIC, None, ALU.subtract), reads=self.b_rb[0:8], writes=self.b_rb[0:8])
            k.op("dve", lambda e: e.scalar_tensor_tensor(self.rr[:], self.tt[:], -C1, self.ang[:], ALU.mult, ALU.add),
                 reads=self.b_rb[0:8], writes=self.b_rb[0:8])
            k.op("dve", lambda e: e.scalar_tensor_tensor(self.rr[:], self.tt[:], -C2, self.rr[:], ALU.mult, ALU.add),
                 reads=self.b_rb[0:8], writes=self.b_rb[0:8])
            k.op("dve", lambda e: e.tensor_scalar(self.rr[:], self.rr[:], -math.pi, math.pi, ALU.max, ALU.min),
                 reads=self.b_rb[0:8], writes=self.b_rb[0:8])
            if which == 0:
                k.op("act", lambda e: e.activation(self.Stab[:], self.rr[:], AF.Sin), reads=self.b_rb[0:8], writes=[self.b_S])
                k.op("dve", lambda e: e.tensor_scalar(self.Stab[:], self.Stab[:], self.cst[0:64, 153:154], None, ALU.mult),
                     reads=[self.b_S, self.b_cst], writes=[self.b_S])
            else:
                k.op("act", lambda e: e.activation(self.Ctab[:], self.rr[:], AF.Sin), reads=self.b_rb[0:8], writes=[self.b_C])

    def l0_inproj(self, h):
        k = self.k
        t0 = h * TH
        rot = [0, 1, 2, 3, 4, 5]
        ar = np.arange
        for which, cbase, gbase, dst, dbase in ((0, 0, 80, self.cqn, 4), (1, 512, 84, self.ckvn, 6)):
            for cc in range(4):
                spec = [("e_w_in", 0, kc * 128, cbase + cc * 128 + ar(128)) for kc in range(16)]
                sl, slb = self.ws.next(spec)
                for tg in range(2):
                    pt, pb = self.bank(rot)
                    k.mm([lambda e, kc=kc, pt=pt, sl=sl, tg=tg: e.matmul(pt[:], sl[:, kc * 128:(kc + 1) * 128],
                                                                        self.hT[:, kc, tg * 512:(tg + 1) * 512],
                                                                        start=(kc == 0), stop=(kc == 15)) for kc in range(16)],
                         reads=[slb], per=[[self.b_h[kc]] for kc in range(16)], writes=[pb])
                    k.op("act", lambda e, pt=pt, cc=cc, tg=tg: e.activation(self.cq32[:, cc, tg * 512:(tg + 1) * 512], pt[:], AF.Copy),
                         reads=[pb], writes=[self.b_cq[cc]])
                self.ws.release()
            self.rms_rstd([(self.cq32[:, cc, :], [self.b_cq[cc]]) for cc in range(4)], 512)
            for cc in range(4):
                self.norm_apply(self.cq32[:, cc, :], [self.b_cq[cc]], gbase, cc, dst[:, cc, :], [self.b_xs[dbase + cc // 2]])
        sw = np.concatenate([ar(32, 64), ar(0, 32)])
        spec = [("e_w_in", 0, kc * 128, 1024 + ar(64)) for kc in range(16)] + \
               [("e_w_in", 0, kc * 128, 1024 + sw) for kc in range(16)]
        sl, slb = self.ws.next(spec)
        for tg in range(2):
            pa, pab = self.bank(rot)
            pbk, pbb = self.bank(rot)
            for pp, ppb, base in ((pa, pab, 0), (pbk, pbb, 1024)):
                k.mm([lambda e, kc=kc, pp=pp, base=base, tg=tg: e.matmul(pp[0:64, :], sl[:, base + kc * 64:base + (kc + 1) * 64],
                                                                        self.hT[:, kc, tg * 512:(tg + 1) * 512],
                                                                        start=(kc == 0), stop=(kc == 15)) for kc in range(16)],
                     reads=[slb], per=[[self.b_h[kc]] for kc in range(16)], writes=[ppb])
            self.rope_evac(pa, pab, pbk, pbb, tg, self.krT[:, t0 + tg * 512:t0 + (tg + 1) * 512], [self.b_kr])
        self.ws.release()
        it = 0
        for cg in range(2):
            sls = []
            for kq in range(4):
                spec = [("e_w_in", 0, (kq * 4 + i) * 128, 2112 + cg * 512 + ar(512)) for i in range(4)]
                sls.append(self.ws.next(spec))
            for tc in range(8):
                pt, pb = self.bank(rot)
                k.mm([lambda e, kc=kc, pt=pt, tc=tc: e.matmul(pt[:], self.hT[:, kc, tc * 128:(tc + 1) * 128],
                                                             sls[kc // 4][0][:, (kc % 4) * 512:(kc % 4 + 1) * 512],
                                                             start=(kc == 0), stop=(kc == 15)) for kc in range(16)],
                     reads=[s[1] for s in sls], per=[[self.b_h[kc]] for kc in range(16)], writes=[pb])
                i2 = it % 2
                it += 1
                ta, tab = self.T0[:, i2 * 512:(i2 + 1) * 512], self.b_T0[i2]
                tb, tbb = self.T1[:, i2 * 512:(i2 + 1) * 512], self.b_T1[i2]
                st, stb = self.st[:, i2, :], self.b_st[i2]
                k.op("act", lambda e, ta=ta, pt=pt: e.activation(ta, pt[:], AF.Gelu_apprx_tanh), reads=[pb], writes=[tab])
                k.op("act", lambda e, ta=ta, tb=tb: e.activation(tb, ta, AF.Square), reads=[tab], writes=[tbb])
                k.op("dve", lambda e, ta=ta, st=st: e.tensor_reduce(st[:, 0:4], ta.rearrange("p (g d) -> p g d", g=4), AX.X, ALU.add),
                     reads=[tab], writes=[stb])
                k.op("dve", lambda e, tb=tb, st=st: e.tensor_reduce(st[:, 4:8], tb.rearrange("p (g d) -> p g d", g=4), AX.X, ALU.add),
                     reads=[tbb], writes=[stb])
                k.op("dve", lambda e, st=st: e.tensor_scalar(st[:, 8:12], st[:, 0:4], 1.0 / 128, None, ALU.mult), reads=[stb], writes=[stb])
                k.op("dve", lambda e, st=st: e.tensor_tensor(st[:, 12:16], st[:, 8:12], st[:, 8:12], ALU.mult), reads=[stb], writes=[stb])
                k.op("dve", lambda e, st=st: e.scalar_tensor_tensor(st[:, 16:20], st[:, 4:8], 1.0 / 128, st[:, 12:16], ALU.mult, ALU.subtract),
                     reads=[stb], writes=[stb])
                k.op("act", lambda e, st=st: e.activation(st[:, 20:24], st[:, 16:20], AF.Sqrt, bias=EPS, scale=1.0), reads=[stb], writes=[stb])
                k.op("dve", lambda e, st=st: e.reciprocal(st[:, 24:28], st[:, 20:24]), reads=[stb], writes=[stb])
                for g in range(4):
                    k.op("dve", lambda e, g=g, ta=ta, st=st: e.tensor_scalar(ta[:, g * 128:(g + 1) * 128], ta[:, g * 128:(g + 1) * 128],
                                                                             st[:, 8 + g:9 + g], st[:, 24 + g:25 + g], ALU.subtract, ALU.mult),
                         reads=[tab, stb], writes=[tab])
                k.op("dve", lambda e, ta=ta, tc=tc, cg=cg: e.tensor_tensor(self.vn[:, tc, cg * 512:(cg + 1) * 512], ta,
                                                                           self.gvb[:, cg * 512:(cg + 1) * 512], ALU.mult),
                     reads=[tab, self.b_gv], writes=[self.b_vn[tc]])
            self.ws.release(4)
        self.tap("vn", [(self.vn[:, c, :], [self.b_vn[c]]) for c in range(8)])
        yrot = [6, 7]
        for g in range(8):
            spec = [("e_w_in", 0, kc * 128, 1088 + g * 128 + ar(128)) for kc in range(16)]
            sl, slb = self.ws.next(spec)
            ug, ugb = self.ug[g % 2], self.b_ug[g % 2]
            for tg in range(2):
                pt, pb = self.bank(rot)
                k.mm([lambda e, kc=kc, pt=pt, tg=tg: e.matmul(pt[:], sl[:, kc * 128:(kc + 1) * 128],
                                                             self.hT[:, kc, tg * 512:(tg + 1) * 512],
                                                             start=(kc == 0), stop=(kc == 15)) for kc in range(16)],
                     reads=[slb], per=[[self.b_h[kc]] for kc in range(16)], writes=[pb])
                k.op("act", lambda e, pt=pt, ug=ug, tg=tg: e.activation(ug[:, tg * 512:(tg + 1) * 512], pt[:], AF.Gelu_apprx_tanh),
                     reads=[pb], writes=[ugb])
            self.ws.release()
            if g < 4:
                s32, s32b = self.cq32[:, g, :], [self.b_cq[g]]
            else:
                s32, s32b = self.s32hi[:, g - 4, :], [self.b_rb[2 * (g - 4)], self.b_rb[2 * (g - 4) + 1]]
            for tq in range(2):
                pt, pb = self.ps[yrot[tq]], self.psb[yrot[tq]]
                fns = []
                for c4 in range(4):
                    c = tq * 4 + c4
                    fns.append(lambda e, pt=pt, c=c, c4=c4, g=g: e.matmul(pt[:, c4 * 128:(c4 + 1) * 128],
                                                                        self.vn[:, c, g * 128:(g + 1) * 128], self.swT16[:, g, :],
                                                                        start=True, stop=False))
                    fns.append(lambda e, pt=pt, c4=c4, g=g: e.matmul(pt[:, c4 * 128:(c4 + 1) * 128],
                                                                   self.ones32[0:1, :], self.sgub[0:1, g * 128:(g + 1) * 128],
                                                                   start=False, stop=True))
                k.mm(fns, reads=self.b_vn + [self.b_sw, self.b_ones, self.b_sgub], writes=[pb])
                k.op("dve", lambda e, pt=pt, tq=tq, s32=s32, ug=ug: e.tensor_tensor(s32[:, tq * 512:(tq + 1) * 512], pt[:],
                                                                                    ug[:, tq * 512:(tq + 1) * 512], ALU.mult),
                     reads=[pb, ugb], writes=s32b)
        chunks = [(self.cq32[:, g, :], [self.b_cq[g]]) for g in range(4)] + \
                 [(self.s32hi[:, g, :], [self.b_rb[2 * g], self.b_rb[2 * g + 1]]) for g in range(4)]
        self.tap("s32", chunks)
        self.rms_rstd(chunks, 1024, banks=(4, 5))
        for g in range(8):
            self.norm_apply(chunks[g][0], chunks[g][1], 96, g, self.RB[:, 8 + g, :], [self.b_rb[8 + g]])

    def l0_attn(self, h):
        k = self.k
        t0 = h * TH
        ar = np.arange
        rot = [4, 5, 6, 7]
        Vall, vb = self.hT, self.b_h
        if h == 1:
            k.dma("sp", [(Vall[:, 0:8, :], self.vd)], self.b_vio, reads=[self.b_vd], writes=vb[0:8])
        for hg in range(2):
            cols = np.concatenate([hd * 256 + 128 + ar(128) for hd in range(hg * 4, hg * 4 + 4)])
            spec = [("e_w_ukv", 0, kc * 128, cols) for kc in range(4)]
            sl, slb = self.ws.next(spec)
            for tc in range(8):
                pt, pb = self.bank(rot)
                k.mm([lambda e, kc=kc, pt=pt, tc=tc: e.matmul(pt[:], self.ckvn[:, kc, tc * 128:(tc + 1) * 128],
                                                             sl[:, kc * 512:(kc + 1) * 512], start=(kc == 0), stop=(kc == 3))
                      for kc in range(4)], reads=[slb, self.b_xs[6], self.b_xs[7]], writes=[pb])
                k.op("act", lambda e, pt=pt, tc=tc, hg=hg: e.activation(Vall[:, h * 8 + tc, hg * 512:(hg + 1) * 512], pt[:], AF.Copy),
                     reads=[pb], writes=[vb[h * 8 + tc]])
            self.ws.release()
        if h == 0:
            k.dma("sp", [(self.vd, Vall[:, 0:8, :])], self.b_vio, reads=vb[0:8], writes=[self.b_vd])
        sw = np.concatenate([ar(32, 64), ar(0, 32)])
        orot = [0, 1]
        srot = [2, 3]
        pti = 0
        for hd in range(8):
            i2 = hd % 2
            kh, khb = self.kh[i2], self.b_kh[i2]
            qn, qnb = self.qn[i2], self.b_qn[i2]
            qr, qrb = self.qr[i2], self.b_qr[i2]
            if h == 1:
                k.dma("sp", [(kh[:, 0:TH], self.kd[:, hd, :])], khb, reads=[self.b_kd], writes=[khb])
            spec = [("e_w_uq", 0, kc * 128, hd * 192 + ar(128)) for kc in range(4)] + \
                   [("e_w_uq", 0, kc * 128, hd * 192 + 128 + ar(64)) for kc in range(4)] + \
                   [("e_w_uq", 0, kc * 128, hd * 192 + 128 + sw) for kc in range(4)] + \
                   [("e_w_ukv", 0, kc * 128, hd * 256 + ar(128)) for kc in range(4)]
            sl, slb = self.ws.next(spec)
            for tg in range(2):
                pt, pb = self.bank(rot)
                k.mm([lambda e, kc=kc, pt=pt, tg=tg: e.matmul(pt[:], sl[:, kc * 128:(kc + 1) * 128],
                                                             self.cqn[:, kc, tg * 512:(tg + 1) * 512], start=(kc == 0), stop=(kc == 3))
                      for kc in range(4)], reads=[slb, self.b_xs[4], self.b_xs[5]], writes=[pb])
                k.op("act", lambda e, pt=pt, tg=tg, qn=qn: e.activation(qn[:, tg * 512:(tg + 1) * 512], pt[:], AF.Copy),
                     reads=[pb], writes=[qnb])
                pa, pab = self.bank(rot)
                pbk, pbb = self.bank(rot)
                for pp, ppb, base in ((pa, pab, 512), (pbk, pbb, 768)):
                    k.mm([lambda e, kc=kc, pp=pp, base=base, tg=tg: e.matmul(pp[0:64, :], sl[:, base + kc * 64:base + (kc + 1) * 64],
                                                                            self.cqn[:, kc, tg * 512:(tg + 1) * 512],
                                                                            start=(kc == 0), stop=(kc == 3)) for kc in range(4)],
                         reads=[slb, self.b_xs[4], self.b_xs[5]], writes=[ppb])
                self.rope_evac(pa, pab, pbk, pbb, tg, qr[:, tg * 512:(tg + 1) * 512], [qrb])
                pt, pb = self.bank(rot)
                k.mm([lambda e, kc=kc, pt=pt, tg=tg: e.matmul(pt[:], sl[:, 1024 + kc * 128:1024 + (kc + 1) * 128],
                                                             self.ckvn[:, kc, tg * 512:(tg + 1) * 512], start=(kc == 0), stop=(kc == 3))
                      for kc in range(4)], reads=[slb, self.b_xs[6], self.b_xs[7]], writes=[pb])
                k.op("act", lambda e, pt=pt, tg=tg, kh=kh: e.activation(kh[:, t0 + tg * 512:t0 + (tg + 1) * 512], pt[:], AF.Copy),
                     reads=[pb], writes=[khb])
            self.ws.release()
            if h == 0:
                k.dma("sp", [(self.kd[:, hd, :], kh[:, 0:TH])], khb, reads=[khb], writes=[self.b_kd])
            for tg in range(2):
                Q0 = t0 + tg * 512
                nkb = (Q0 + 512) // 128
                po, pob = self.bank(orot)
                psm, psmb = self.bank(srot)

                def c0_of(j):
                    r = j - Q0 // 128
                    return 128 * r if r > 0 else 0

                def score(j):
                    pt, pb = self.bank(rot)
                    c0 = c0_of(j)
                    k.mm([lambda e: e.matmul(pt[:, c0:512], kh[:, j * 128:(j + 1) * 128], qn[:, tg * 512 + c0:(tg + 1) * 512],
                                             start=True, stop=False),
                          lambda e: e.matmul(pt[:, c0:512], self.krT[:, j * 128:(j + 1) * 128], qr[:, tg * 512 + c0:(tg + 1) * 512],
                                             start=False, stop=True)],
                         reads=[khb, qnb, qrb, self.b_kr], writes=[pb])
                    return pt, pb

                nxt = score(0)
                for j in range(nkb):
                    pt, pb = nxt
                    if j + 1 < nkb:
                        nxt = score(j + 1)
                    c0 = c0_of(j)
                    r = j - Q0 // 128
                    P, Pb = self.PT[pti % 4], self.b_PT[pti % 4]
                    pti += 1
                    k.op("act", lambda e, pt=pt, P=P, c0=c0: e.activation(P[:, c0:512], pt[:, c0:512], AF.Exp, scale=SCALE),
                         reads=[pb], writes=[Pb])
                    if r >= 0:
                        k.op("dve", lambda e, P=P, c0=c0: e.tensor_tensor(P[:, c0:c0 + 128], P[:, c0:c0 + 128], self.tri16[:], ALU.mult),
                             reads=[Pb, self.b_tri], writes=[Pb])
                    k.mm([lambda e, P=P, c0=c0, j=j: e.matmul(po[:, c0:512], Vall[:, j, hd * 128:(hd + 1) * 128], P[:, c0:512],
                                                             start=(j == 0), stop=(j == nkb - 1))],
                         reads=[Pb, vb[j]], writes=[pob])
                    k.mm([lambda e, P=P, c0=c0, j=j: e.matmul(psm[:, c0:512], self.ones16[:], P[:, c0:512],
                                                             start=(j == 0), stop=(j == nkb - 1))],
                         reads=[Pb, self.b_ones], writes=[psmb])
                rs, rsb = self.T0[:, tg * 512:(tg + 1) * 512], self.b_T0[tg]
                k.op("dve", lambda e, rs=rs, psm=psm: e.reciprocal(rs, psm[:]), reads=[psmb], writes=[rsb])
                k.op("dve", lambda e, rs=rs, po=po, tg=tg, hd=hd: e.tensor_tensor(self.a32[:, hd, tg * 512:(tg + 1) * 512], po[:], rs, ALU.mult),
                     reads=[pob, rsb], writes=[self.b_a32[hd]])
        self.tap("a32", [(self.a32[:, c, :], [self.b_a32[c]]) for c in range(8)])
        self.rms_rstd([(self.a32[:, c, :], [self.b_a32[c]]) for c in range(8)], 1024, banks=(4, 5))
        for hd in range(8):
            self.norm_apply(self.a32[:, hd, :], [self.b_a32[hd]], 88, hd, self.RB[:, hd, :], [self.b_rb[hd]])

    def rms_mm(self, c, n, banks=(6, 7)):
        k = self.k
        sq, sqb = self.sq[c % 2], self.b_sq[c % 2]
        for tg in range(2):
            pt, pb = self.ps[banks[tg]], self.psb[banks[tg]]
            k.mm([lambda e: e.matmul(pt[:], self.ones16[:], sq[:, tg * 512:(tg + 1) * 512], start=(c == 0), stop=(c == n - 1))],
                 reads=[sqb, self.b_ones], writes=[pb])

    def rms_finish(self, nfeat, banks=(6, 7)):
        k = self.k
        for tg in range(2):
            pt, pb = self.ps[banks[tg]], self.psb[banks[tg]]
            k.op("act", lambda e: e.activation(self.rstd[:, tg * 512:(tg + 1) * 512], pt[:], AF.Sqrt, bias=EPS, scale=1.0 / nfeat),
                 reads=[pb], writes=[self.b_rstd])
        k.op("dve", lambda e: e.reciprocal(self.rstd[:], self.rstd[:]), reads=[self.b_rstd], writes=[self.b_rstd])

    def out_proj(self, wname, layer, xh=None, next_g=None):
        k = self.k
        prot = [0, 1, 2]
        ar = np.arange
        stg = [(self.T0, self.b_T0), (self.T1, self.b_T1)]

        def ld(j):
            T, tb = stg[j % 2]
            k.dma("sp", [(T[:], self.xT[j, :, xh * TH:(xh + 1) * TH])], tb[0], writes=tb)

        if xh is not None:
            ld(0)
        for j in range(16):
            if xh is not None and j + 1 < 16:
                ld(j + 1)
            spec = [(wname, layer, kc * 128, j * 128 + ar(128)) for kc in range(16)]
            sl, slb = self.ws.next(spec)
            p = prot[0]
            prot.append(prot.pop(0))
            for tg in range(2):
                pt, pb = self.ps[2 * p + tg], self.psb[2 * p + tg]
                k.mm([lambda e, kc=kc, pt=pt, tg=tg: e.matmul(pt[:], sl[:, kc * 128:(kc + 1) * 128],
                                                             self.RB[:, kc, tg * 512:(tg + 1) * 512], start=(kc == 0), stop=(kc == 15))
                      for kc in range(16)], reads=[slb], per=[[self.b_rb[kc]] for kc in range(16)], writes=[pb])
            pbs = [self.psb[2 * p], self.psb[2 * p + 1]]
            if xh is None:
                k.op("dve", lambda e, j=j, p=p: e.tensor_tensor(self.xs[:, j, :], self.xs[:, j, :], self.pp[p][:], ALU.add),
                     reads=pbs + [self.b_xs[j]], writes=[self.b_xs[j]])
            else:
                T, tb = stg[j % 2]
                k.op("dve", lambda e, j=j, p=p, T=T: e.tensor_tensor(self.xs[:, j, :], T[:], self.pp[p][:], ALU.add),
                     reads=pbs + tb, writes=[self.b_xs[j]])
            self.ws.release()
            if next_g is not None:
                if j >= 2:
                    self.rms_mm(j - 2, 16)
                k.op("act", lambda e, j=j: e.activation(self.hT[:, j, :], self.xs[:, j, :], AF.Copy, scale=self.gcol(next_g, j)),
                     reads=[self.b_xs[j], self.b_cst], writes=[self.b_h[j]])
                sq, sqb = self.sq[j % 2], self.b_sq[j % 2]
                k.op("act", lambda e, j=j, sq=sq: e.activation(sq[:], self.xs[:, j, :], AF.Square), reads=[self.b_xs[j]], writes=[sqb])
        if next_g is not None:
            self.rms_mm(14, 16)
            self.rms_mm(15, 16)
            self.rms_finish(D)

    def mlp(self, layer):
        k = self.k
        ar = np.arange
        prot = [0, 1, 2, 3]

        def pair():
            p = prot[0]
            prot.append(prot.pop(0))
            return p

        tmps = [(self.T0, self.b_T0), (self.T1, self.b_T1)]
        ti = 0
        for fb in range(16):
            ab = (fb % 4) * 4
            for fcl in range(4):
                spec = [("mlp_w1", layer, kc * 128, fb * 512 + fcl * 128 + ar(128)) for kc in range(16)]
                sl, slb = self.ws.next(spec)
                p = pair()
                for tg in range(2):
                    pt, pb = self.ps[2 * p + tg], self.psb[2 * p + tg]
                    k.mm([lambda e, kc=kc, pt=pt, tg=tg: e.matmul(pt[:], sl[:, kc * 128:(kc + 1) * 128],
                                                                 self.hT[:, kc, tg * 512:(tg + 1) * 512], start=(kc == 0), stop=(kc == 15))
                          for kc in range(16)], reads=[slb], per=[[self.b_h[kc]] for kc in range(16)], writes=[pb])
                self.ws.release()
                pbs = [self.psb[2 * p], self.psb[2 * p + 1]]
                tm, tmb = tmps[ti % 2]
                ti += 1
                c = ab + fcl
                k.op("act", lambda e, tm=tm, p=p: e.activation(tm[:], self.pp[p][:], AF.Relu), reads=pbs, writes=tmb)
                k.op("dve", lambda e, tm=tm: e.tensor_tensor(tm[:], tm[:], self.rstd[:], ALU.mult),
                     reads=tmb + [self.b_rstd], writes=tmb)
                k.op("act", lambda e, tm=tm, c=c: e.activation(self.RB[:, c, :], tm[:], AF.Square),
                     reads=tmb, writes=[self.b_rb[c]])
            for dg in range(4):
                spec = [("mlp_w2", layer, fb * 512 + fcl * 128, dg * 512 + ar(512)) for fcl in range(4)]
                sl, slb = self.ws.next(spec)
                for dcl in range(4):
                    j = dg * 4 + dcl
                    p = pair()
                    for tg in range(2):
                        pt, pb = self.ps[2 * p + tg], self.psb[2 * p + tg]
                        k.mm([lambda e, fcl=fcl, pt=pt, tg=tg, dcl=dcl: e.matmul(pt[:], sl[:, fcl * 512 + dcl * 128:fcl * 512 + (dcl + 1) * 128],
                                                                                self.RB[:, ab + fcl, tg * 512:(tg + 1) * 512],
                                                                                start=(fcl == 0), stop=(fcl == 3)) for fcl in range(4)],
                             reads=[slb], per=[[self.b_rb[ab + f]] for f in range(4)], writes=[pb])
                    k.op("dve", lambda e, j=j, p=p: e.tensor_tensor(self.xs[:, j, :], self.xs[:, j, :], self.pp[p][:], ALU.add),
                         reads=[self.psb[2 * p], self.psb[2 * p + 1], self.b_xs[j]], writes=[self.b_xs[j]])
                self.ws.release()

    def l1_conv(self, h):
        k = self.k
        ar = np.arange
        rot = [0, 1, 2, 3, 4, 5, 6, 7]
        for j in range(16):
            banks = {}
            for name, cb in (("c", 2048), ("x", 4096), ("b", 0)):
                spec = [("o_w_in", 0, kc * 128, cb + j * 128 + ar(128)) for kc in range(16)]
                sl, slb = self.ws.next(spec)
                for tg in range(2):
                    pt, pb = self.bank(rot)
                    banks[(name, tg)] = (pt, pb)
                    k.mm([lambda e, kc=kc, pt=pt, tg=tg, sl=sl: e.matmul(pt[:], sl[:, kc * 128:(kc + 1) * 128],
                                                                        self.hT[:, kc, tg * 512:(tg + 1) * 512], start=(kc == 0), stop=(kc == 15))
                          for kc in range(16)], reads=[slb], per=[[self.b_h[kc]] for kc in range(16)], writes=[pb])
                    if name == "c":
                        k.op("act", lambda e, pt=pt, tg=tg: e.activation(self.T0[:, tg * 512:(tg + 1) * 512], pt[:], AF.Copy),
                             reads=[pb], writes=[self.b_T0[tg]])
                    elif name == "x":
                        if tg == 0 and h == 1:
                            k.op("act", lambda e, j=j: e.activation(self.zb[:, 0:2], self.carry[:, j, :], AF.Copy),
                                 reads=[self.b_carry], writes=[self.b_zb])
                        k.op("dve", lambda e, pt=pt, tg=tg: e.tensor_tensor(self.zb[:, 2 + tg * 512:2 + (tg + 1) * 512],
                                                                           self.T0[:, tg * 512:(tg + 1) * 512], pt[:], ALU.mult),
                             reads=[pb, self.b_T0[tg]], writes=[self.b_zb])
                self.ws.release()
                if name == "x":
                    if h == 0:
                        k.op("act", lambda e, j=j: e.activation(self.carry[:, j, :], self.zb[:, 1024:1026], AF.Copy),
                             reads=[self.b_zb], writes=[self.b_carry])
                    k.op("dve", lambda e, j=j: e.tensor_scalar(self.T1[:], self.zb[:, 0:1024], self.gcol(104, j), None, ALU.mult),
                         reads=[self.b_zb, self.b_cst], writes=self.b_T1)
                    k.op("dve", lambda e, j=j: e.scalar_tensor_tensor(self.T1[:], self.zb[:, 1:1025], self.gcol(120, j), self.T1[:], ALU.mult, ALU.add),
                         reads=[self.b_zb, self.b_cst] + self.b_T1, writes=self.b_T1)
                    k.op("dve", lambda e, j=j: e.scalar_tensor_tensor(self.T1[:], self.zb[:, 2:1026], self.gcol(136, j), self.T1[:], ALU.mult, ALU.add),
                         reads=[self.b_zb, self.b_cst] + self.b_T1, writes=self.b_T1)
            for tg in range(2):
                pt, pb = banks[("b", tg)]
                k.op("dve", lambda e, pt=pt, tg=tg, j=j: e.tensor_tensor(self.RB[:, j, tg * 512:(tg + 1) * 512], pt[:],
                                                                        self.T1[:, tg * 512:(tg + 1) * 512], ALU.mult),
                     reads=[pb, self.b_T1[tg]], writes=[self.b_rb[j]])


_CACHE = {}


def _get_prog(dbg=None):
    key = ("prog", dbg)
    if key not in _CACHE:
        _CACHE[key] = Prog(dbg=dbg, nt_hint=NT_TILES)
    return _CACHE[key]


def _build_wstream(specs, W):
    nt = len(specs)
    ws = np.zeros((nt, 128, 2048), np.float32)
    for t, spec in enumerate(specs):
        off = 0
        for (name, layer, r0, cols) in spec:
            n = len(cols)
            c0, c1 = int(cols[0]), int(cols[-1])
            m = W[name][layer]
            if c1 - c0 == n - 1:
                ws[t, :, off:off + n] = m[r0:r0 + 128, c0:c1 + 1]
            else:
                ws[t, :, off:off + n] = m[r0:r0 + 128][:, cols]
            off += n
        assert off <= 2048
    return ws


def _fm(v):
    return np.ascontiguousarray(np.asarray(v, np.float32).reshape(-1, 128).T)


def kernel(x, positions, e_norm_mix, e_w_in, e_q_norm, e_w_uq, e_kv_norm, e_w_ukv,
           e_v_norm, e_sgu_w, e_sgu_b, e_mla_out_norm, e_sgu_out_norm, e_w_out,
           o_norm_mix, o_w_in, o_conv_w, o_w_out, mlp_norm, mlp_w1, mlp_w2, final_norm, _dbg=None):
    prog = _get_prog(_dbg)
    W = {"e_w_in": np.asarray(e_w_in), "e_w_uq": np.asarray(e_w_uq), "e_w_ukv": np.asarray(e_w_ukv),
         "e_w_out": np.asarray(e_w_out), "o_w_in": np.asarray(o_w_in), "o_w_out": np.asarray(o_w_out),
         "mlp_w1": np.asarray(mlp_w1), "mlp_w2": np.asarray(mlp_w2)}
    wstream = _build_wstream(prog.ws.specs, W)
    cst = np.zeros((128, 192), np.float32)
    cst[:, 0:16] = _fm(e_norm_mix[0])
    cst[:, 16:32] = _fm(mlp_norm[0])
    cst[:, 32:48] = _fm(o_norm_mix[0])
    cst[:, 48:64] = _fm(mlp_norm[1])
    cst[:, 64:80] = _fm(final_norm)
    cst[:, 80:84] = _fm(e_q_norm[0])
    cst[:, 84:88] = _fm(e_kv_norm[0])
    cst[:, 88:96] = _fm(e_mla_out_norm[0])
    cst[:, 96:104] = _fm(e_sgu_out_norm[0])
    for kk in range(3):
        cst[:, 104 + 16 * kk:120 + 16 * kk] = _fm(np.asarray(o_conv_w)[0, kk])
    inv_freq = (np.float32(10000.0) ** (-np.arange(0, 64, 2, dtype=np.float32) / np.float32(64))).astype(np.float32)
    cst[0:64, 152] = np.concatenate([inv_freq, inv_freq])
    cst[0:32, 153] = -1.0
    cst[32:64, 153] = 1.0
    tri = (np.arange(128)[:, None] <= np.arange(128)[None, :]).astype(np.float32)
    swT = np.ascontiguousarray(np.asarray(e_sgu_w, np.float32)[0].transpose(2, 0, 1)).reshape(128, 1024)
    sgub = np.ascontiguousarray(np.asarray(e_sgu_b, np.float32)[0].reshape(1, 1024))
    gv = np.ascontiguousarray(np.asarray(e_v_norm, np.float32)[0].reshape(1, 1024))
    x = np.asarray(x, np.float32)
    positions = np.asarray(positions, np.int32)
    in_maps = []
    for c in range(NCORE):
        xT = np.ascontiguousarray(x[c].T).reshape(16, 128, SEQ)
        in_maps.append({"xT": xT, "pos": np.ascontiguousarray(positions[c].reshape(1, SEQ)), "wstream": wstream,
                        "cst": cst, "tri": tri, "swT": swT, "sgub": sgub, "gv": gv})
    res = run_bass_kernel_spmd(prog.nc, in_maps, core_ids=list(range(NCORE)))
    out = np.empty((NCORE, SEQ, D), np.float32)
    for c in range(NCORE):
        out[c] = res.results[c]["outT"].reshape(D, SEQ).T
    if _dbg:
        return out, [res.results[c]["dbg"] for c in range(NCORE)]
    return out
```

```python
import math
import numpy as np
import concourse.bass as bass
import concourse.mybir as mybir
from concourse.bass_utils import run_bass_kernel_spmd

F32 = mybir.dt.float32
BF16 = mybir.dt.bfloat16
I32 = mybir.dt.int32
AF = mybir.ActivationFunctionType
ALU = mybir.AluOpType
AX = mybir.AxisListType

D = 2048
SEQ = 2048
TH = 1024
NCORE = 8
EPS = 1e-6
RING = 8
SBUF_BASE = 16512
SBUF_TOP = 229344
SCALE = (128 + 64) ** -0.5
MAGIC = 12582912.0
C1 = 6.28125
C2 = 2.0 * math.pi - C1
NT_TILES = 371


class Buf:
    __slots__ = ("name", "w", "r", "dsem", "dcnt", "ring")

    def __init__(self, name, ring=False):
        self.name = name
        self.w = None
        self.r = {}
        self.dsem = None
        self.dcnt = 0
        self.ring = ring


class K:
    def __init__(self, nc):
        self.nc = nc
        self.eng = {"pe": nc.tensor, "act": nc.scalar, "dve": nc.vector,
                    "pool": nc.gpsimd, "sp": nc.sync}
        self.esem = {}
        self.ecnt = {}
        for e in ("pe", "act", "dve"):
            self.esem[e] = nc.alloc_semaphore(name="c_" + e)
            self.ecnt[e] = 0
        self.known = {e: {} for e in self.eng}
        self.prims = []
        self.ninst = {e: 0 for e in self.eng}

    def _wait(self, e, tok):
        if tok is None:
            return
        sem, val = tok
        if e == "pe" and sem is self.esem["pe"]:
            return
        key = id(sem)
        if self.known[e].get(key, 0) >= val:
            return
        self.eng[e].wait_ge(sem, val)
        self.known[e][key] = val
        self.ninst[e] += 1

    def _deps(self, e, reads, writes):
        for b in reads:
            self._wait(e, b.w)
        for b in writes:
            self._wait(e, b.w)
            for t in list(b.r.values()):
                self._wait(e, t)

    def _pend(self, e, reads, writes):
        out = {}
        toks = [b.w for b in reads]
        for b in writes:
            toks.append(b.w)
            toks.extend(b.r.values())
        for tok in toks:
            if tok is None:
                continue
            sem, val = tok
            if e == "pe" and sem is self.esem["pe"]:
                continue
            key = id(sem)
            if self.known[e].get(key, 0) >= val:
                continue
            if key in out and out[key][1] >= val:
                continue
            out[key] = tok
        return list(out.values())

    def _issue(self, e, pend, fn):
        emb = pend.pop() if pend else None
        for sem, val in pend:
            self.eng[e].wait_ge(sem, val)
            self.known[e][id(sem)] = val
            self.ninst[e] += 1
        ins = fn(self.eng[e])
        if emb is not None:
            ins._wait_ge(emb[0], emb[1])
            self.known[e][id(emb[0])] = emb[1]
        return ins

    @staticmethod
    def _commit(tok, reads, writes):
        key = id(tok[0])
        for b in reads:
            old = b.r.get(key)
            if old is None or old[1] < tok[1]:
                b.r[key] = tok
        for b in writes:
            b.w = tok
            b.r = {}

    def op(self, e, fn, reads=(), writes=()):
        ins = self._issue(e, self._pend(e, reads, writes), fn)
        self.ecnt[e] += 1
        ins.then_inc(self.esem[e], 1)
        tok = (self.esem[e], self.ecnt[e])
        self._commit(tok, reads, writes)
        self.ninst[e] += 1
        return tok

    def mm(self, fns, reads=(), writes=(), per=None):
        e = "pe"
        ins = None
        for i, fn in enumerate(fns):
            r_i = (list(reads) if i == 0 else []) + (list(per[i]) if per is not None else [])
            ins = self._issue(e, self._pend(e, r_i, writes if i == 0 else ()), fn)
        self.ecnt[e] += 1
        ins.then_inc(self.esem[e], 1)
        tok = (self.esem[e], self.ecnt[e])
        if per is not None:
            reads = list(reads) + [b for p in per for b in p]
        self._commit(tok, reads, writes)
        self.ninst[e] += len(fns)
        return tok

    def dma(self, q, pairs, prim, reads=(), writes=()):
        if prim.dsem is None:
            prim.dsem = self.nc.alloc_semaphore(name="d_" + prim.name)
            self.prims.append(prim)
        if prim.dcnt:
            self._wait(q, (prim.dsem, prim.dcnt))
        self._deps(q, reads, writes)
        for out_ap, in_ap in pairs:
            ins = self.eng[q].dma_start(out=out_ap, in_=in_ap)
            prim.dcnt += 16
            ins.then_inc(prim.dsem, 16)
            self.ninst[q] += 1
        tok = (prim.dsem, prim.dcnt)
        self._commit(tok, reads, writes)
        return tok

    @staticmethod
    def handoff(old, new):
        acc = {}
        for b in old:
            toks = list(b.r.values()) + ([b.w] if b.w is not None else [])
            for t in toks:
                key = id(t[0])
                if key not in acc or acc[key][1] < t[1]:
                    acc[key] = t
        for b in new:
            b.w = None
            b.r = dict(acc)

    def barrier(self):
        toks = [(self.esem[x], self.ecnt[x]) for x in ("pe", "act", "dve") if self.ecnt[x]]
        toks += [(b.dsem, b.dcnt) for b in self.prims if b.dcnt and not b.ring]
        for e in ("pe", "act", "dve", "sp"):
            for t in toks:
                self._wait(e, t)

    def final(self):
        for b in self.prims:
            if b.dcnt:
                self._wait("sp", (b.dsem, b.dcnt))
        for x in ("pe", "act", "dve"):
            self._wait("sp", (self.esem[x], self.ecnt[x]))


class WStream:
    def __init__(self, k, nc, wdram, base_off, nt):
        self.k, self.nc, self.w, self.nt = k, nc, wdram, nt
        self.slots = [nc.alloc_sbuf_tensor_at(f"ws{i}", [128, 2048], BF16, offset=base_off + i * 4096)
                      for i in range(RING)]
        self.bufs = [Buf(f"ws{i}", ring=True) for i in range(RING)]
        self.specs = []
        self.consumed = 0
        self.released = 0
        self.issued = 0

    def pump(self):
        while self.issued < min(2 * self.nt, self.released + RING):
            s = self.issued % RING
            self.k.dma("pool", [(self.slots[s][:], self.w[self.issued % self.nt])], self.bufs[s],
                       writes=[self.bufs[s]])
            self.issued += 1

    def next(self, spec):
        if self.consumed < self.nt:
            self.specs.append(spec)
        assert self.consumed < 2 * self.nt
        assert self.issued > self.consumed, (self.issued, self.consumed)
        s = self.consumed % RING
        self.consumed += 1
        return self.slots[s], self.bufs[s]

    def release(self, n=1):
        self.released += n
        assert self.released <= self.consumed
        self.pump()


class Prog:
    def __init__(self, dbg=None, nt_hint=None):
        self.dbg = dbg
        nc = bass.Bass("TRN2", target_bir_lowering=False)
        self.nc = nc
        self.k = K(nc)
        self.nt_hint = nt_hint
        self.build()

    def sb(self, name, shape, dt, off):
        return self.nc.alloc_sbuf_tensor_at(name, shape, dt, offset=SBUF_BASE + off)

    def bank(self, rot):
        i = rot[0]
        rot.append(rot.pop(0))
        return self.ps[i], self.psb[i]

    def build(self):
        nc, k = self.nc, self.k
        NT = self.nt_hint
        self.xT = nc.dram_tensor("xT", [16, 128, SEQ], F32, kind="ExternalInput").ap()
        self.posd = nc.dram_tensor("pos", [1, SEQ], I32, kind="ExternalInput").ap()
        self.wd = nc.dram_tensor("wstream", [NT, 128, 2048], F32, kind="ExternalInput").ap()
        self.cstd = nc.dram_tensor("cst", [128, 192], F32, kind="ExternalInput").ap()
        self.trid = nc.dram_tensor("tri", [128, 128], F32, kind="ExternalInput").ap()
        self.swd = nc.dram_tensor("swT", [128, 1024], F32, kind="ExternalInput").ap()
        self.sbd = nc.dram_tensor("sgub", [1, 1024], F32, kind="ExternalInput").ap()
        self.gvd = nc.dram_tensor("gv", [1, 1024], F32, kind="ExternalInput").ap()
        self.outT = nc.dram_tensor("outT", [16, 128, SEQ], F32, kind="ExternalOutput").ap()
        self.kd = nc.dram_tensor("kd", [128, 8, TH], BF16, kind="Internal").ap()
        self.vd = nc.dram_tensor("vd", [128, 8, 1024], BF16, kind="Internal").ap()
        if self.dbg:
            self.dbgd = nc.dram_tensor("dbg", [16, 128, TH], F32, kind="ExternalOutput").ap()
        self.pp = [nc.alloc_psum_tensor(f"pp{i}", [128, 1024], F32) for i in range(4)]
        self.ps = [self.pp[i // 2][:, (i % 2) * 512:(i % 2 + 1) * 512] for i in range(8)]
        self.psb = [Buf(f"ps{i}") for i in range(8)]
        o = 0
        self.cst = self.sb("cst", [128, 192], F32, o); o += 768
        self.ones16 = self.sb("ones16", [128, 128], BF16, o); o += 256
        self.tri16 = self.sb("tri16", [128, 128], BF16, o); o += 256
        self.ones32 = self.sb("ones32", [1, 128], F32, o); o += 512
        self.sgub = self.sb("sgub", [1, 1024], F32, o); o += 4096
        self.swT16 = self.sb("swT16", [128, 8, 128], BF16, o); o += 2048
        self.gvb = self.sb("gvb", [128, 1024], F32, o); o += 4096
        self.Ctab = self.sb("Ctab", [64, TH], F32, o); o += 4096
        self.Stab = self.sb("Stab", [64, TH], F32, o); o += 4096
        self.krT = self.sb("krT", [64, SEQ], BF16, o); o += 4096
        self.carry = self.sb("carry", [128, 16, 2], F32, o); o += 128
        self.zb = self.sb("zb", [128, 1026], F32, o); o += 4128
        self.st = self.sb("st", [128, 2, 32], F32, o); o += 256
        self.rstd = self.sb("rstd", [128, TH], F32, o); o += 4096
        self.sq = [self.sb(f"sq{i}", [128, TH], BF16, o + 2048 * i) for i in range(2)]; o += 4096
        self.T0 = self.sb("T0", [128, TH], F32, o); T0o = o; o += 4096
        self.T1 = self.sb("T1", [128, TH], F32, o); T1o = o; o += 4096
        self.PT = [self.sb(f"PT{i}", [128, 512], BF16, T1o + 1024 * i) for i in range(4)]
        self.ws = WStream(k, nc, self.wd, SBUF_BASE + o, NT); o += RING * 4096
        XO = o; o += 65536
        HO = o; o += 32768
        BO = o; o += 32768
        assert SBUF_BASE + o <= SBUF_TOP, o
        self.xs = self.sb("xs", [128, 16, TH], F32, XO)
        self.posi = self.sb("posi", [64, TH], I32, BO)
        self.ang = self.sb("ang", [64, TH], F32, BO + 4096)
        self.tt = self.sb("tt", [64, TH], F32, BO + 8192)
        self.rr = self.sb("rr", [64, TH], F32, BO + 12288)
        self.cq32 = self.sb("cq32", [128, 4, TH], F32, XO)
        self.cqn = self.sb("cqn", [128, 4, TH], BF16, XO + 16384)
        self.ckvn = self.sb("ckvn", [128, 4, TH], BF16, XO + 24576)
        self.vn = self.sb("vn", [128, 8, 1024], BF16, XO + 32768)
        self.ug = [self.sb(f"ug{i}", [128, TH], F32, XO + 49152 + 4096 * i) for i in range(2)]
        self.kh = [self.sb(f"kh{i}", [128, SEQ], BF16, XO + 4096 * i) for i in range(2)]
        self.qn = [self.sb(f"qn{i}", [128, TH], BF16, XO + 8192 + 4096 * i) for i in range(2)]
        self.qr = [self.sb(f"qr{i}", [64, TH], BF16, XO + 8192 + 4096 * i + 2048) for i in range(2)]
        self.a32 = self.sb("a32", [128, 8, TH], F32, XO + 32768)
        self.hT = self.sb("hT", [128, 16, TH], BF16, HO)
        self.RB = self.sb("RB", [128, 16, TH], BF16, BO)
        self.s32hi = self.sb("s32hi", [128, 4, TH], F32, BO)
        B = Buf
        self.b_cst, self.b_ones, self.b_tri, self.b_sgub, self.b_sw, self.b_gv = (
            B("cst"), B("ones"), B("tri"), B("sgub"), B("sw"), B("gv"))
        self.b_C, self.b_S, self.b_kr, self.b_carry, self.b_zb = B("C"), B("S"), B("kr"), B("carry"), B("zb")
        self.b_st = [B("st0"), B("st1")]
        self.b_rstd = B("rstd")
        self.b_sq = [B("sq0"), B("sq1")]
        self.b_T0 = [B("T0a"), B("T0b")]
        self.b_T1 = [B("T1a"), B("T1b")]
        self.b_PT = [B(f"PT{i}") for i in range(4)]
        self.b_xs = [B(f"xs{i}") for i in range(16)]
        self.b_xsio = [B(f"xsio{i}") for i in range(4)]
        self.b_cq = self.b_xs[0:4]
        self.b_cqn, self.b_ckvn = self.b_xs[4], self.b_xs[6]
        self.b_cqn2, self.b_ckvn2 = self.b_xs[5], self.b_xs[7]
        self.b_vn = [self.b_xs[8 + i // 2] for i in range(8)]
        self.b_ug = [self.b_xs[12], self.b_xs[13]]
        self.b_kh = self.b_xs[0:2]
        self.b_qn = self.b_xs[2:4]
        self.b_qr = self.b_xs[2:4]
        self.b_a32 = self.b_xs[8:16]
        self.b_h = [B(f"h{i}") for i in range(16)]
        self.b_rb = [B(f"rb{i}") for i in range(16)]
        self.b_vd, self.b_kd, self.b_out, self.b_dbg = B("vd"), B("kd"), B("out"), B("dbg")
        self.b_vio = B("vio")

        self.init_consts()
        self.load_xs(0)
        for b in self.b_xsio:
            k._wait("pool", (b.dsem, b.dcnt))
        self.ws.pump()
        for h in range(2):
            self.half(h)
        assert self.ws.consumed == 2 * NT and len(self.ws.specs) == NT, (self.ws.consumed, len(self.ws.specs))
        k.final()

    def init_consts(self):
        k = self.k
        k.dma("sp", [(self.cst[:], self.cstd)], self.b_cst, writes=[self.b_cst])
        k.dma("sp", [(self.T1[:, 0:128], self.trid)], self.b_T1[0], writes=[self.b_T1[0]])
        k.dma("sp", [(self.T0[:], self.swd)], self.b_T0[0], writes=self.b_T0)
        k.dma("sp", [(self.sgub[:], self.sbd)], self.b_sgub, writes=[self.b_sgub])
        k.dma("sp", [(self.gvb[:], self.gvd.partition_broadcast(128))], self.b_gv, writes=[self.b_gv])
        k.op("dve", lambda e: e.memset(self.ones16[:], 1.0), writes=[self.b_ones])
        k.op("dve", lambda e: e.memset(self.ones32[:], 1.0), writes=[self.b_ones])
        k.op("dve", lambda e: e.tensor_copy(self.tri16[:], self.T1[:, 0:128]), reads=[self.b_T1[0]], writes=[self.b_tri])
        for g in range(8):
            k.op("dve", lambda e, g=g: e.tensor_tensor(self.swT16[:, g, :], self.T0[:, g * 128:(g + 1) * 128],
                                                      self.T1[:, 0:128], ALU.mult),
                 reads=self.b_T0 + [self.b_T1[0]], writes=[self.b_sw])
        k.op("dve", lambda e: e.memset(self.zb[:, 0:2], 0.0), writes=[self.b_zb])
        k.barrier()

    def gcol(self, base, i):
        return self.cst[:, base + i:base + i + 1]

    def rms_rstd(self, chunks, nfeat, banks=(6, 7)):
        k = self.k
        n = len(chunks)
        for c, (ap, bufs) in enumerate(chunks):
            sq, sqb = self.sq[c % 2], self.b_sq[c % 2]
            k.op("act", lambda e, ap=ap, sq=sq: e.activation(sq[:], ap, AF.Square), reads=bufs, writes=[sqb])
            for tg in range(2):
                pt, pb = self.ps[banks[tg]], self.psb[banks[tg]]
                k.mm([lambda e, pt=pt, sq=sq, tg=tg, c=c: e.matmul(pt[:], self.ones16[:], sq[:, tg * 512:(tg + 1) * 512],
                                                                 start=(c == 0), stop=(c == n - 1))],
                     reads=[sqb, self.b_ones], writes=[pb])
        for tg in range(2):
            pt, pb = self.ps[banks[tg]], self.psb[banks[tg]]
            k.op("act", lambda e, pt=pt, tg=tg: e.activation(self.rstd[:, tg * 512:(tg + 1) * 512], pt[:], AF.Sqrt,
                                                            bias=EPS, scale=1.0 / nfeat),
                 reads=[pb], writes=[self.b_rstd])
        k.op("dve", lambda e: e.reciprocal(self.rstd[:], self.rstd[:]), reads=[self.b_rstd], writes=[self.b_rstd])

    def norm_apply(self, src, srcbufs, gbase, i, dst, dstbufs):
        self.k.op("dve", lambda e: e.scalar_tensor_tensor(dst, src, self.gcol(gbase, i), self.rstd[:], ALU.mult, ALU.mult),
                  reads=list(srcbufs) + [self.b_rstd, self.b_cst], writes=list(dstbufs))

    def rope_evac(self, pa, pab, pbk, pbb, tg, out_ap, outbufs):
        k = self.k
        ta, tb = self.T0[0:64, 0:512], self.T0[0:64, 512:1024]
        k.op("dve", lambda e: e.tensor_tensor(ta, pa[0:64, :], self.Ctab[:, tg * 512:(tg + 1) * 512], ALU.mult),
             reads=[pab, self.b_C], writes=[self.b_T0[0]])
        k.op("dve", lambda e: e.tensor_tensor(tb, pbk[0:64, :], self.Stab[:, tg * 512:(tg + 1) * 512], ALU.mult),
             reads=[pbb, self.b_S], writes=[self.b_T0[1]])
        k.op("dve", lambda e: e.tensor_tensor(out_ap, ta, tb, ALU.add),
             reads=self.b_T0, writes=outbufs)

    def tap(self, name, chunks):
        if self.dbg != name or getattr(self, "_tapped", False):
            return
        self._tapped = True
        k = self.k
        k.barrier()
        for c, (ap, bufs) in enumerate(chunks):
            p = ap.shape[0]
            n = ap.shape[-1]
            k.op("act", lambda e, ap=ap, p=p, n=n: e.activation(self.T0[0:p, 0:n], ap, AF.Copy),
                 reads=bufs, writes=self.b_T0)
            k.dma("sp", [(self.dbgd[c, 0:p, 0:n], self.T0[0:p, 0:n])], self.b_T0[0], reads=self.b_T0, writes=[self.b_dbg])
        k.barrier()

    def half(self, h):
        k = self.k
        t0 = h * TH
        if h > 0:
            self.load_xs(h)
        self.tables(h)
        self.rms_rstd([(self.xs[:, c, :], [self.b_xs[c]]) for c in range(16)], D)
        for c in range(16):
            self.norm_apply(self.xs[:, c, :], [self.b_xs[c]], 0, c, self.hT[:, c, :], [self.b_h[c]])
        self.tap("h0", [(self.hT[:, c, :], [self.b_h[c]]) for c in range(16)])
        self.l0_inproj(h)
        k.handoff(self.b_T1, self.b_PT)
        self.l0_attn(h)
        k.handoff(self.b_PT, self.b_T1)
        self.tap("mix", [(self.RB[:, c, :], [self.b_rb[c]]) for c in range(16)])
        self.out_proj("e_w_out", 0, xh=h, next_g=16)
        self.tap("x1", [(self.xs[:, c, :], [self.b_xs[c]]) for c in range(16)])
        self.mlp(0, next_g=32)
        self.tap("x2", [(self.xs[:, c, :], [self.b_xs[c]]) for c in range(16)])
        self.l1_conv(h)
        self.out_proj("o_w_out", 0, next_g=48)
        self.tap("x3", [(self.xs[:, c, :], [self.b_xs[c]]) for c in range(16)])
        self.mlp(1)
        for c in range(16):
            T, tb = (self.T0, self.b_T0) if c % 2 == 0 else (self.T1, self.b_T1)
            self.norm_apply(self.xs[:, c, :], [self.b_xs[c]], 64, c, T[:], tb)
            k.dma("sp", [(self.outT[c, :, t0:t0 + TH], T[:])], tb[0], reads=tb, writes=[self.b_out])

    def load_xs(self, h):
        t0 = h * TH
        for b in range(4):
            self.k.dma("sp", [(self.xs[:, c, :], self.xT[c, :, t0:t0 + TH]) for c in range(4 * b, 4 * b + 4)],
                       self.b_xsio[b], writes=self.b_xs[4 * b:4 * b + 4])

    def tables(self, h):
        k = self.k
        t0 = h * TH
        k.dma("sp", [(self.posi[:], self.posd[:, t0:t0 + TH].partition_broadcast(64))], self.b_rb[0], writes=self.b_rb[0:2])
        k.op("dve", lambda e: e.tensor_copy(self.ang[:], self.posi[:]), reads=self.b_rb[0:8], writes=self.b_rb[0:8])
        k.op("dve", lambda e: e.tensor_scalar(self.ang[:], self.ang[:], self.cst[0:64, 152:153], None, ALU.mult),
             reads=self.b_rb[0:8] + [self.b_cst], writes=self.b_rb[0:8])
        for which in range(2):
            if which == 1:
                k.op("dve", lambda e: e.tensor_scalar(self.ang[:], self.ang[:], math.pi / 2, None, ALU.add),
                     reads=self.b_rb[0:8], writes=self.b_rb[0:8])
            k.op("dve", lambda e: e.tensor_scalar(self.tt[:], self.ang[:], 1.0 / (2 * math.pi), MAGIC, ALU.mult, ALU.add),
                 reads=self.b_rb[0:8], writes=self.b_rb[0:8])
            k.op("dve", lambda e: e.tensor_scalar(self.tt[:], self.tt[:], MAGIC, None, ALU.subtract), reads=self.b_rb[0:8], writes=self.b_rb[0:8])
            k.op("dve", lambda e: e.scalar_tensor_tensor(self.rr[:], self.tt[:], -C1, self.ang[:], ALU.mult, ALU.add),
                 reads=self.b_rb[0:8], writes=self.b_rb[0:8])
            k.op("dve", lambda e: e.scalar_tensor_tensor(self.rr[:], self.tt[:], -C2, self.rr[:], ALU.mult, ALU.add),
                 reads=self.b_rb[0:8], writes=self.b_rb[0:8])
            k.op("dve", lambda e: e.tensor_scalar(self.rr[:], self.rr[:], -math.pi, math.pi, ALU.max, ALU.min),
                 reads=self.b_rb[0:8], writes=self.b_rb[0:8])
            if which == 0:
                k.op("act", lambda e: e.activation(self.Stab[:], self.rr[:], AF.Sin), reads=self.b_rb[0:8], writes=[self.b_S])
                k.op("dve", lambda e: e.tensor_scalar(self.Stab[:], self.Stab[:], self.cst[0:64, 153:154], None, ALU.mult),
                     reads=[self.b_S, self.b_cst], writes=[self.b_S])
            else:
                k.op("act", lambda e: e.activation(self.Ctab[:], self.rr[:], AF.Sin), reads=self.b_rb[0:8], writes=[self.b_C])

    def l0_inproj(self, h):
        k = self.k
        t0 = h * TH
        rot = [0, 1, 2, 3, 4, 5]
        ar = np.arange
        for which, cbase, gbase, dst, dbase in ((0, 0, 80, self.cqn, 4), (1, 512, 84, self.ckvn, 6)):
            for cc in range(4):
                spec = [("e_w_in", 0, kc * 128, cbase + cc * 128 + ar(128)) for kc in range(16)]
                sl, slb = self.ws.next(spec)
                for tg in range(2):
                    pt, pb = self.bank(rot)
                    k.mm([lambda e, kc=kc, pt=pt, sl=sl, tg=tg: e.matmul(pt[:], sl[:, kc * 128:(kc + 1) * 128],
                                                                        self.hT[:, kc, tg * 512:(tg + 1) * 512],
                                                                        start=(kc == 0), stop=(kc == 15)) for kc in range(16)],
                         reads=[slb], per=[[self.b_h[kc]] for kc in range(16)], writes=[pb])
                    k.op("act", lambda e, pt=pt, cc=cc, tg=tg: e.activation(self.cq32[:, cc, tg * 512:(tg + 1) * 512], pt[:], AF.Copy),
                         reads=[pb], writes=[self.b_cq[cc]])
                self.ws.release()
            self.rms_rstd([(self.cq32[:, cc, :], [self.b_cq[cc]]) for cc in range(4)], 512)
            for cc in range(4):
                self.norm_apply(self.cq32[:, cc, :], [self.b_cq[cc]], gbase, cc, dst[:, cc, :], [self.b_xs[dbase + cc // 2]])
        sw = np.concatenate([ar(32, 64), ar(0, 32)])
        spec = [("e_w_in", 0, kc * 128, 1024 + ar(64)) for kc in range(16)] + \
               [("e_w_in", 0, kc * 128, 1024 + sw) for kc in range(16)]
        sl, slb = self.ws.next(spec)
        for tg in range(2):
            pa, pab = self.bank(rot)
            pbk, pbb = self.bank(rot)
            for pp, ppb, base in ((pa, pab, 0), (pbk, pbb, 1024)):
                k.mm([lambda e, kc=kc, pp=pp, base=base, tg=tg: e.matmul(pp[0:64, :], sl[:, base + kc * 64:base + (kc + 1) * 64],
                                                                        self.hT[:, kc, tg * 512:(tg + 1) * 512],
                                                                        start=(kc == 0), stop=(kc == 15)) for kc in range(16)],
                     reads=[slb], per=[[self.b_h[kc]] for kc in range(16)], writes=[ppb])
            self.rope_evac(pa, pab, pbk, pbb, tg, self.krT[:, t0 + tg * 512:t0 + (tg + 1) * 512], [self.b_kr])
        self.ws.release()
        it = 0
        for cg in range(2):
            sls = []
            for kq in range(4):
                spec = [("e_w_in", 0, (kq * 4 + i) * 128, 2112 + cg * 512 + ar(512)) for i in range(4)]
                sls.append(self.ws.next(spec))
            for tc in range(8):
                pt, pb = self.bank(rot)
                k.mm([lambda e, kc=kc, pt=pt, tc=tc: e.matmul(pt[:], self.hT[:, kc, tc * 128:(tc + 1) * 128],
                                                             sls[kc // 4][0][:, (kc % 4) * 512:(kc % 4 + 1) * 512],
                                                             start=(kc == 0), stop=(kc == 15)) for kc in range(16)],
                     reads=[s[1] for s in sls], per=[[self.b_h[kc]] for kc in range(16)], writes=[pb])
                i2 = it % 2
                it += 1
                ta, tab = self.T0[:, i2 * 512:(i2 + 1) * 512], self.b_T0[i2]
                tb, tbb = self.T1[:, i2 * 512:(i2 + 1) * 512], self.b_T1[i2]
                st, stb = self.st[:, i2, :], self.b_st[i2]
                k.op("act", lambda e, ta=ta, pt=pt: e.activation(ta, pt[:], AF.Gelu_apprx_tanh), reads=[pb], writes=[tab])
                k.op("act", lambda e, ta=ta, tb=tb: e.activation(tb, ta, AF.Square), reads=[tab], writes=[tbb])
                k.op("dve", lambda e, ta=ta, st=st: e.tensor_reduce(st[:, 0:4], ta.rearrange("p (g d) -> p g d", g=4), AX.X, ALU.add),
                     reads=[tab], writes=[stb])
                k.op("dve", lambda e, tb=tb, st=st: e.tensor_reduce(st[:, 4:8], tb.rearrange("p (g d) -> p g d", g=4), AX.X, ALU.add),
                     reads=[tbb], writes=[stb])
                k.op("dve", lambda e, st=st: e.tensor_scalar(st[:, 8:12], st[:, 0:4], 1.0 / 128, None, ALU.mult), reads=[stb], writes=[stb])
                k.op("dve", lambda e, st=st: e.tensor_tensor(st[:, 12:16], st[:, 8:12], st[:, 8:12], ALU.mult), reads=[stb], writes=[stb])
                k.op("dve", lambda e, st=st: e.scalar_tensor_tensor(st[:, 16:20], st[:, 4:8], 1.0 / 128, st[:, 12:16], ALU.mult, ALU.subtract),
                     reads=[stb], writes=[stb])
                k.op("act", lambda e, st=st: e.activation(st[:, 20:24], st[:, 16:20], AF.Sqrt, bias=EPS, scale=1.0), reads=[stb], writes=[stb])
                k.op("dve", lambda e, st=st: e.reciprocal(st[:, 24:28], st[:, 20:24]), reads=[stb], writes=[stb])
                for g in range(4):
                    k.op("dve", lambda e, g=g, ta=ta, st=st: e.tensor_scalar(ta[:, g * 128:(g + 1) * 128], ta[:, g * 128:(g + 1) * 128],
                                                                             st[:, 8 + g:9 + g], st[:, 24 + g:25 + g], ALU.subtract, ALU.mult),
                         reads=[tab, stb], writes=[tab])
                k.op("dve", lambda e, ta=ta, tc=tc, cg=cg: e.tensor_tensor(self.vn[:, tc, cg * 512:(cg + 1) * 512], ta,
                                                                           self.gvb[:, cg * 512:(cg + 1) * 512], ALU.mult),
                     reads=[tab, self.b_gv], writes=[self.b_vn[tc]])
            self.ws.release(4)
        self.tap("vn", [(self.vn[:, c, :], [self.b_vn[c]]) for c in range(8)])
        yrot = [6, 7]
        for g in range(8):
            spec = [("e_w_in", 0, kc * 128, 1088 + g * 128 + ar(128)) for kc in range(16)]
            sl, slb = self.ws.next(spec)
            ug, ugb = self.ug[g % 2], self.b_ug[g % 2]
            for tg in range(2):
                pt, pb = self.bank(rot)
                k.mm([lambda e, kc=kc, pt=pt, tg=tg: e.matmul(pt[:], sl[:, kc * 128:(kc + 1) * 128],
                                                             self.hT[:, kc, tg * 512:(tg + 1) * 512],
                                                             start=(kc == 0), stop=(kc == 15)) for kc in range(16)],
                     reads=[slb], per=[[self.b_h[kc]] for kc in range(16)], writes=[pb])
                k.op("act", lambda e, pt=pt, ug=ug, tg=tg: e.activation(ug[:, tg * 512:(tg + 1) * 512], pt[:], AF.Gelu_apprx_tanh),
                     reads=[pb], writes=[ugb])
            self.ws.release()
            if g < 4:
                s32, s32b = self.cq32[:, g, :], [self.b_cq[g]]
            else:
                s32, s32b = self.s32hi[:, g - 4, :], [self.b_rb[2 * (g - 4)], self.b_rb[2 * (g - 4) + 1]]
            for tq in range(2):
                pt, pb = self.ps[yrot[tq]], self.psb[yrot[tq]]
                fns = []
                for c4 in range(4):
                    c = tq * 4 + c4
                    fns.append(lambda e, pt=pt, c=c, c4=c4, g=g: e.matmul(pt[:, c4 * 128:(c4 + 1) * 128],
                                                                        self.vn[:, c, g * 128:(g + 1) * 128], self.swT16[:, g, :],
                                                                        start=True, stop=False))
                    fns.append(lambda e, pt=pt, c4=c4, g=g: e.matmul(pt[:, c4 * 128:(c4 + 1) * 128],
                                                                   self.ones32[0:1, :], self.sgub[0:1, g * 128:(g + 1) * 128],
                                                                   start=False, stop=True))
                k.mm(fns, reads=self.b_vn + [self.b_sw, self.b_ones, self.b_sgub], writes=[pb])
                k.op("dve", lambda e, pt=pt, tq=tq, s32=s32, ug=ug: e.tensor_tensor(s32[:, tq * 512:(tq + 1) * 512], pt[:],
                                                                                    ug[:, tq * 512:(tq + 1) * 512], ALU.mult),
                     reads=[pb, ugb], writes=s32b)
        chunks = [(self.cq32[:, g, :], [self.b_cq[g]]) for g in range(4)] + \
                 [(self.s32hi[:, g, :], [self.b_rb[2 * g], self.b_rb[2 * g + 1]]) for g in range(4)]
        self.tap("s32", chunks)
        self.rms_rstd(chunks, 1024, banks=(4, 5))
        for g in range(8):
            self.norm_apply(chunks[g][0], chunks[g][1], 96, g, self.RB[:, 8 + g, :], [self.b_rb[8 + g]])

    def l0_attn(self, h):
        k = self.k
        t0 = h * TH
        ar = np.arange
        rot = [4, 5, 6, 7]
        Vall, vb = self.hT, self.b_h
        if h == 1:
            k.dma("sp", [(Vall[:, 0:8, :], self.vd)], self.b_vio, reads=[self.b_vd], writes=vb[0:8])
        for hg in range(2):
            cols = np.concatenate([hd * 256 + 128 + ar(128) for hd in range(hg * 4, hg * 4 + 4)])
            spec = [("e_w_ukv", 0, kc * 128, cols) for kc in range(4)]
            sl, slb = self.ws.next(spec)
            for tc in range(8):
                pt, pb = self.bank(rot)
                k.mm([lambda e, kc=kc, pt=pt, tc=tc: e.matmul(pt[:], self.ckvn[:, kc, tc * 128:(tc + 1) * 128],
                                                             sl[:, kc * 512:(kc + 1) * 512], start=(kc == 0), stop=(kc == 3))
                      for kc in range(4)], reads=[slb, self.b_xs[6], self.b_xs[7]], writes=[pb])
                k.op("act", lambda e, pt=pt, tc=tc, hg=hg: e.activation(Vall[:, h * 8 + tc, hg * 512:(hg + 1) * 512], pt[:], AF.Copy),
                     reads=[pb], writes=[vb[h * 8 + tc]])
            self.ws.release()
        if h == 0:
            k.dma("sp", [(self.vd, Vall[:, 0:8, :])], self.b_vio, reads=vb[0:8], writes=[self.b_vd])
        sw = np.concatenate([ar(32, 64), ar(0, 32)])
        orot = [0, 1]
        srot = [2, 3]
        pti = 0
        for hd in range(8):
            i2 = hd % 2
            kh, khb = self.kh[i2], self.b_kh[i2]
            qn, qnb = self.qn[i2], self.b_qn[i2]
            qr, qrb = self.qr[i2], self.b_qr[i2]
            if h == 1:
                k.dma("sp", [(kh[:, 0:TH], self.kd[:, hd, :])], khb, reads=[self.b_kd], writes=[khb])
            spec = [("e_w_uq", 0, kc * 128, hd * 192 + ar(128)) for kc in range(4)] + \
                   [("e_w_uq", 0, kc * 128, hd * 192 + 128 + ar(64)) for kc in range(4)] + \
                   [("e_w_uq", 0, kc * 128, hd * 192 + 128 + sw) for kc in range(4)] + \
                   [("e_w_ukv", 0, kc * 128, hd * 256 + ar(128)) for kc in range(4)]
            sl, slb = self.ws.next(spec)
            for tg in range(2):
                pt, pb = self.bank(rot)
                k.mm([lambda e, kc=kc, pt=pt, tg=tg: e.matmul(pt[:], sl[:, kc * 128:(kc + 1) * 128],
                                                             self.cqn[:, kc, tg * 512:(tg + 1) * 512], start=(kc == 0), stop=(kc == 3))
                      for kc in range(4)], reads=[slb, self.b_xs[4], self.b_xs[5]], writes=[pb])
                k.op("act", lambda e, pt=pt, tg=tg, qn=qn: e.activation(qn[:, tg * 512:(tg + 1) * 512], pt[:], AF.Copy),
                     reads=[pb], writes=[qnb])
                pa, pab = self.bank(rot)
                pbk, pbb = self.bank(rot)
                for pp, ppb, base in ((pa, pab, 512), (pbk, pbb, 768)):
                    k.mm([lambda e, kc=kc, pp=pp, base=base, tg=tg: e.matmul(pp[0:64, :], sl[:, base + kc * 64:base + (kc + 1) * 64],
                                                                            self.cqn[:, kc, tg * 512:(tg + 1) * 512],
                                                                            start=(kc == 0), stop=(kc == 3)) for kc in range(4)],
                         reads=[slb, self.b_xs[4], self.b_xs[5]], writes=[ppb])
                self.rope_evac(pa, pab, pbk, pbb, tg, qr[:, tg * 512:(tg + 1) * 512], [qrb])
                pt, pb = self.bank(rot)
                k.mm([lambda e, kc=kc, pt=pt, tg=tg: e.matmul(pt[:], sl[:, 1024 + kc * 128:1024 + (kc + 1) * 128],
                                                             self.ckvn[:, kc, tg * 512:(tg + 1) * 512], start=(kc == 0), stop=(kc == 3))
                      for kc in range(4)], reads=[slb, self.b_xs[6], self.b_xs[7]], writes=[pb])
                k.op("act", lambda e, pt=pt, tg=tg, kh=kh: e.activation(kh[:, t0 + tg * 512:t0 + (tg + 1) * 512], pt[:], AF.Copy),
                     reads=[pb], writes=[khb])
            self.ws.release()
            if h == 0:
                k.dma("sp", [(self.kd[:, hd, :], kh[:, 0:TH])], khb, reads=[khb], writes=[self.b_kd])
            for tg in range(2):
                Q0 = t0 + tg * 512
                nkb = (Q0 + 512) // 128
                po, pob = self.bank(orot)
                psm, psmb = self.bank(srot)

                def c0_of(j):
                    r = j - Q0 // 128
                    return 128 * r if r > 0 else 0

                def score(j):
                    pt, pb = self.bank(rot)
                    c0 = c0_of(j)
                    k.mm([lambda e: e.matmul(pt[:, c0:512], kh[:, j * 128:(j + 1) * 128], qn[:, tg * 512 + c0:(tg + 1) * 512],
                                             start=True, stop=False),
                          lambda e: e.matmul(pt[:, c0:512], self.krT[:, j * 128:(j + 1) * 128], qr[:, tg * 512 + c0:(tg + 1) * 512],
                                             start=False, stop=True)],
                         reads=[khb, qnb, qrb, self.b_kr], writes=[pb])
                    return pt, pb

                nxt = score(0)
                for j in range(nkb):
                    pt, pb = nxt
                    if j + 1 < nkb:
                        nxt = score(j + 1)
                    c0 = c0_of(j)
                    r = j - Q0 // 128
                    P, Pb = self.PT[pti % 4], self.b_PT[pti % 4]
                    pti += 1
                    k.op("act", lambda e, pt=pt, P=P, c0=c0: e.activation(P[:, c0:512], pt[:, c0:512], AF.Exp, scale=SCALE),
                         reads=[pb], writes=[Pb])
                    if r >= 0:
                        k.op("dve", lambda e, P=P, c0=c0: e.tensor_tensor(P[:, c0:c0 + 128], P[:, c0:c0 + 128], self.tri16[:], ALU.mult),
                             reads=[Pb, self.b_tri], writes=[Pb])
                    k.mm([lambda e, P=P, c0=c0, j=j: e.matmul(po[:, c0:512], Vall[:, j, hd * 128:(hd + 1) * 128], P[:, c0:512],
                                                             start=(j == 0), stop=(j == nkb - 1))],
                         reads=[Pb, vb[j]], writes=[pob])
                    k.mm([lambda e, P=P, c0=c0, j=j: e.matmul(psm[:, c0:512], self.ones16[:], P[:, c0:512],
                                                             start=(j == 0), stop=(j == nkb - 1))],
                         reads=[Pb, self.b_ones], writes=[psmb])
                rs, rsb = self.T0[:, tg * 512:(tg + 1) * 512], self.b_T0[tg]
                k.op("dve", lambda e, rs=rs, psm=psm: e.reciprocal(rs, psm[:]), reads=[psmb], writes=[rsb])
                k.op("dve", lambda e, rs=rs, po=po, tg=tg, hd=hd: e.tensor_tensor(self.a32[:, hd, tg * 512:(tg + 1) * 512], po[:], rs, ALU.mult),
                     reads=[pob, rsb], writes=[self.b_a32[hd]])
        self.tap("a32", [(self.a32[:, c, :], [self.b_a32[c]]) for c in range(8)])
        self.rms_rstd([(self.a32[:, c, :], [self.b_a32[c]]) for c in range(8)], 1024, banks=(4, 5))
        for hd in range(8):
            self.norm_apply(self.a32[:, hd, :], [self.b_a32[hd]], 88, hd, self.RB[:, hd, :], [self.b_rb[hd]])

    def rms_mm(self, c, n, banks=(6, 7)):
        k = self.k
        sq, sqb = self.sq[c % 2], self.b_sq[c % 2]
        for tg in range(2):
            pt, pb = self.ps[banks[tg]], self.psb[banks[tg]]
            k.mm([lambda e: e.matmul(pt[:], self.ones16[:], sq[:, tg * 512:(tg + 1) * 512], start=(c == 0), stop=(c == n - 1))],
                 reads=[sqb, self.b_ones], writes=[pb])

    def rms_finish(self, nfeat, banks=(6, 7)):
        k = self.k
        for tg in range(2):
            pt, pb = self.ps[banks[tg]], self.psb[banks[tg]]
            k.op("act", lambda e: e.activation(self.rstd[:, tg * 512:(tg + 1) * 512], pt[:], AF.Sqrt, bias=EPS, scale=1.0 / nfeat),
                 reads=[pb], writes=[self.b_rstd])
        k.op("dve", lambda e: e.reciprocal(self.rstd[:], self.rstd[:]), reads=[self.b_rstd], writes=[self.b_rstd])

    def out_proj(self, wname, layer, xh=None, next_g=None):
        k = self.k
        prot = [0, 1, 2]
        ar = np.arange
        stg = [(self.T0, self.b_T0), (self.T1, self.b_T1)]

        def ld(j):
            T, tb = stg[j % 2]
            k.dma("sp", [(T[:], self.xT[j, :, xh * TH:(xh + 1) * TH])], tb[0], writes=tb)

        if xh is not None:
            ld(0)
        for j in range(16):
            if xh is not None and j + 1 < 16:
                ld(j + 1)
            spec = [(wname, layer, kc * 128, j * 128 + ar(128)) for kc in range(16)]
            sl, slb = self.ws.next(spec)
            p = prot[0]
            prot.append(prot.pop(0))
            for tg in range(2):
                pt, pb = self.ps[2 * p + tg], self.psb[2 * p + tg]
                k.mm([lambda e, kc=kc, pt=pt, tg=tg: e.matmul(pt[:], sl[:, kc * 128:(kc + 1) * 128],
                                                             self.RB[:, kc, tg * 512:(tg + 1) * 512], start=(kc == 0), stop=(kc == 15))
                      for kc in range(16)], reads=[slb], per=[[self.b_rb[kc]] for kc in range(16)], writes=[pb])
            pbs = [self.psb[2 * p], self.psb[2 * p + 1]]
            if xh is None:
                k.op("dve", lambda e, j=j, p=p: e.tensor_tensor(self.xs[:, j, :], self.xs[:, j, :], self.pp[p][:], ALU.add),
                     reads=pbs + [self.b_xs[j]], writes=[self.b_xs[j]])
            else:
                T, tb = stg[j % 2]
                k.op("dve", lambda e, j=j, p=p, T=T: e.tensor_tensor(self.xs[:, j, :], T[:], self.pp[p][:], ALU.add),
                     reads=pbs + tb, writes=[self.b_xs[j]])
            self.ws.release()
            if next_g is not None:
                if j >= 2:
                    self.rms_mm(j - 2, 16)
                k.op("act", lambda e, j=j: e.activation(self.hT[:, j, :], self.xs[:, j, :], AF.Copy, scale=self.gcol(next_g, j)),
                     reads=[self.b_xs[j], self.b_cst], writes=[self.b_h[j]])
                sq, sqb = self.sq[j % 2], self.b_sq[j % 2]
                k.op("act", lambda e, j=j, sq=sq: e.activation(sq[:], self.xs[:, j, :], AF.Square), reads=[self.b_xs[j]], writes=[sqb])
        if next_g is not None:
            self.rms_mm(14, 16)
            self.rms_mm(15, 16)
            self.rms_finish(D)

    def mlp(self, layer, next_g=None):
        k = self.k
        ar = np.arange
        prot = [0, 1, 2, 3]

        def pair():
            p = prot[0]
            prot.append(prot.pop(0))
            return p

        tmps = [(self.T0, self.b_T0), (self.T1, self.b_T1)]
        ti = [0]

        def m1(fb):
            ab = (fb % 4) * 4
            for fcl in range(4):
                spec = [("mlp_w1", layer, kc * 128, fb * 512 + fcl * 128 + ar(128)) for kc in range(16)]
                sl, slb = self.ws.next(spec)
                p = pair()
                for tg in range(2):
                    pt, pb = self.ps[2 * p + tg], self.psb[2 * p + tg]
                    k.mm([lambda e, kc=kc, pt=pt, tg=tg: e.matmul(pt[:], sl[:, kc * 128:(kc + 1) * 128],
                                                                 self.hT[:, kc, tg * 512:(tg + 1) * 512], start=(kc == 0), stop=(kc == 15))
                          for kc in range(16)], reads=[slb], per=[[self.b_h[kc]] for kc in range(16)], writes=[pb])
                self.ws.release()
                pbs = [self.psb[2 * p], self.psb[2 * p + 1]]
                tm, tmb = tmps[ti[0] % 2]
                ti[0] += 1
                c = ab + fcl
                k.op("act", lambda e, tm=tm, p=p: e.activation(tm[:], self.pp[p][:], AF.Relu), reads=pbs, writes=tmb)
                k.op("dve", lambda e, tm=tm: e.tensor_tensor(tm[:], tm[:], self.rstd[:], ALU.mult),
                     reads=tmb + [self.b_rstd], writes=tmb)
                k.op("act", lambda e, tm=tm, c=c: e.activation(self.RB[:, c, :], tm[:], AF.Square),
                     reads=tmb, writes=[self.b_rb[c]])

        def m2(fb):
            ab = (fb % 4) * 4
            last = (fb == 15)
            if last and 3 in prot:
                prot.remove(3)
            for dg in range(4):
                spec = [("mlp_w2", layer, fb * 512 + fcl * 128, dg * 512 + ar(512)) for fcl in range(4)]
                sl, slb = self.ws.next(spec)
                for dcl in range(4):
                    j = dg * 4 + dcl
                    p = pair()
                    for tg in range(2):
                        pt, pb = self.ps[2 * p + tg], self.psb[2 * p + tg]
                        k.mm([lambda e, fcl=fcl, pt=pt, tg=tg, dcl=dcl: e.matmul(pt[:], sl[:, fcl * 512 + dcl * 128:fcl * 512 + (dcl + 1) * 128],
                                                                                self.RB[:, ab + fcl, tg * 512:(tg + 1) * 512],
                                                                                start=(fcl == 0), stop=(fcl == 3)) for fcl in range(4)],
                             reads=[slb], per=[[self.b_rb[ab + f]] for f in range(4)], writes=[pb])
                    k.op("dve", lambda e, j=j, p=p: e.tensor_tensor(self.xs[:, j, :], self.xs[:, j, :], self.pp[p][:], ALU.add),
                         reads=[self.psb[2 * p], self.psb[2 * p + 1], self.b_xs[j]], writes=[self.b_xs[j]])
                    if last:
                        if j >= 2:
                            self.rms_mm(j - 2, 16)
                        if next_g is not None:
                            k.op("act", lambda e, j=j: e.activation(self.hT[:, j, :], self.xs[:, j, :], AF.Copy, scale=self.gcol(next_g, j)),
                                 reads=[self.b_xs[j], self.b_cst], writes=[self.b_h[j]])
                        sq, sqb = self.sq[j % 2], self.b_sq[j % 2]
                        k.op("act", lambda e, j=j, sq=sq: e.activation(sq[:], self.xs[:, j, :], AF.Square), reads=[self.b_xs[j]], writes=[sqb])
                self.ws.release()
            if last:
                self.rms_mm(14, 16)
                self.rms_mm(15, 16)
                self.rms_finish(D)

        m1(0)
        for fb in range(16):
            if fb + 1 < 16:
                m1(fb + 1)
            m2(fb)

    def l1_conv(self, h):
        k = self.k
        ar = np.arange
        rot = [0, 1, 2, 3, 4, 5, 6, 7]
        for j in range(16):
            banks = {}
            for name, cb in (("c", 2048), ("x", 4096), ("b", 0)):
                spec = [("o_w_in", 0, kc * 128, cb + j * 128 + ar(128)) for kc in range(16)]
                sl, slb = self.ws.next(spec)
                for tg in range(2):
                    pt, pb = self.bank(rot)
                    banks[(name, tg)] = (pt, pb)
                    k.mm([lambda e, kc=kc, pt=pt, tg=tg, sl=sl: e.matmul(pt[:], sl[:, kc * 128:(kc + 1) * 128],
                                                                        self.hT[:, kc, tg * 512:(tg + 1) * 512], start=(kc == 0), stop=(kc == 15))
                          for kc in range(16)], reads=[slb], per=[[self.b_h[kc]] for kc in range(16)], writes=[pb])
                    if name == "c":
                        k.op("dve", lambda e, pt=pt, tg=tg: e.tensor_tensor(self.T0[:, tg * 512:(tg + 1) * 512], pt[:],
                                                                           self.rstd[:, tg * 512:(tg + 1) * 512], ALU.mult),
                             reads=[pb, self.b_rstd], writes=[self.b_T0[tg]])
                    elif name == "x":
                        if tg == 0 and h == 1:
                            k.op("act", lambda e, j=j: e.activation(self.zb[:, 0:2], self.carry[:, j, :], AF.Copy),
                                 reads=[self.b_carry], writes=[self.b_zb])
                        k.op("dve", lambda e, pt=pt, tg=tg: e.tensor_tensor(self.zb[:, 2 + tg * 512:2 + (tg + 1) * 512], pt[:],
                                                                           self.rstd[:, tg * 512:(tg + 1) * 512], ALU.mult),
                             reads=[pb, self.b_rstd], writes=[self.b_zb])
                        k.op("dve", lambda e, tg=tg: e.tensor_tensor(self.zb[:, 2 + tg * 512:2 + (tg + 1) * 512],
                                                                    self.zb[:, 2 + tg * 512:2 + (tg + 1) * 512],
                                                                    self.T0[:, tg * 512:(tg + 1) * 512], ALU.mult),
                             reads=[self.b_zb, self.b_T0[tg]], writes=[self.b_zb])
                self.ws.release()
                if name == "x":
                    if h == 0:
                        k.op("act", lambda e, j=j: e.activation(self.carry[:, j, :], self.zb[:, 1024:1026], AF.Copy),
                             reads=[self.b_zb], writes=[self.b_carry])
                    k.op("dve", lambda e, j=j: e.tensor_scalar(self.T1[:], self.zb[:, 0:1024], self.gcol(104, j), None, ALU.mult),
                         reads=[self.b_zb, self.b_cst], writes=self.b_T1)
                    k.op("dve", lambda e, j=j: e.scalar_tensor_tensor(self.T1[:], self.zb[:, 1:1025], self.gcol(120, j), self.T1[:], ALU.mult, ALU.add),
                         reads=[self.b_zb, self.b_cst] + self.b_T1, writes=self.b_T1)
                    k.op("dve", lambda e, j=j: e.scalar_tensor_tensor(self.T1[:], self.zb[:, 2:1026], self.gcol(136, j), self.T1[:], ALU.mult, ALU.add),
                         reads=[self.b_zb, self.b_cst] + self.b_T1, writes=self.b_T1)
                    k.op("dve", lambda e: e.tensor_tensor(self.T1[:], self.T1[:], self.rstd[:], ALU.mult),
                         reads=self.b_T1 + [self.b_rstd], writes=self.b_T1)
            for tg in range(2):
                pt, pb = banks[("b", tg)]
                k.op("dve", lambda e, pt=pt, tg=tg, j=j: e.tensor_tensor(self.RB[:, j, tg * 512:(tg + 1) * 512], pt[:],
                                                                        self.T1[:, tg * 512:(tg + 1) * 512], ALU.mult),
                     reads=[pb, self.b_T1[tg]], writes=[self.b_rb[j]])


_CACHE = {}


def _get_prog(dbg=None):
    key = ("prog", dbg)
    if key not in _CACHE:
        _CACHE[key] = Prog(dbg=dbg, nt_hint=NT_TILES)
    return _CACHE[key]


def _build_wstream(specs, W):
    nt = len(specs)
    ws = np.zeros((nt, 128, 2048), np.float32)
    for t, spec in enumerate(specs):
        off = 0
        for (name, layer, r0, cols) in spec:
            n = len(cols)
            c0, c1 = int(cols[0]), int(cols[-1])
            m = W[name][layer]
            if c1 - c0 == n - 1:
                ws[t, :, off:off + n] = m[r0:r0 + 128, c0:c1 + 1]
            else:
                ws[t, :, off:off + n] = m[r0:r0 + 128][:, cols]
            off += n
        assert off <= 2048
    return ws


def _fm(v):
    return np.ascontiguousarray(np.asarray(v, np.float32).reshape(-1, 128).T)


def kernel(x, positions, e_norm_mix, e_w_in, e_q_norm, e_w_uq, e_kv_norm, e_w_ukv,
           e_v_norm, e_sgu_w, e_sgu_b, e_mla_out_norm, e_sgu_out_norm, e_w_out,
           o_norm_mix, o_w_in, o_conv_w, o_w_out, mlp_norm, mlp_w1, mlp_w2, final_norm, _dbg=None):
    prog = _get_prog(_dbg)
    W = {"e_w_in": np.asarray(e_w_in), "e_w_uq": np.asarray(e_w_uq), "e_w_ukv": np.asarray(e_w_ukv),
         "e_w_out": np.asarray(e_w_out), "o_w_in": np.asarray(o_w_in), "o_w_out": np.asarray(o_w_out),
         "mlp_w1": np.asarray(mlp_w1), "mlp_w2": np.asarray(mlp_w2)}
    wstream = _build_wstream(prog.ws.specs, W)
    cst = np.zeros((128, 192), np.float32)
    cst[:, 0:16] = _fm(e_norm_mix[0])
    cst[:, 16:32] = _fm(mlp_norm[0])
    cst[:, 32:48] = _fm(o_norm_mix[0])
    cst[:, 48:64] = _fm(mlp_norm[1])
    cst[:, 64:80] = _fm(final_norm)
    cst[:, 80:84] = _fm(e_q_norm[0])
    cst[:, 84:88] = _fm(e_kv_norm[0])
    cst[:, 88:96] = _fm(e_mla_out_norm[0])
    cst[:, 96:104] = _fm(e_sgu_out_norm[0])
    for kk in range(3):
        cst[:, 104 + 16 * kk:120 + 16 * kk] = _fm(np.asarray(o_conv_w)[0, kk])
    inv_freq = (np.float32(10000.0) ** (-np.arange(0, 64, 2, dtype=np.float32) / np.float32(64))).astype(np.float32)
    cst[0:64, 152] = np.concatenate([inv_freq, inv_freq])
    cst[0:32, 153] = -1.0
    cst[32:64, 153] = 1.0
    tri = (np.arange(128)[:, None] <= np.arange(128)[None, :]).astype(np.float32)
    swT = np.ascontiguousarray(np.asarray(e_sgu_w, np.float32)[0].transpose(2, 0, 1)).reshape(128, 1024)
    sgub = np.ascontiguousarray(np.asarray(e_sgu_b, np.float32)[0].reshape(1, 1024))
    gv = np.ascontiguousarray(np.asarray(e_v_norm, np.float32)[0].reshape(1, 1024))
    x = np.asarray(x, np.float32)
    positions = np.asarray(positions, np.int32)
    in_maps = []
    for c in range(NCORE):
        xT = np.ascontiguousarray(x[c].T).reshape(16, 128, SEQ)
        in_maps.append({"xT": xT, "pos": np.ascontiguousarray(positions[c].reshape(1, SEQ)), "wstream": wstream,
                        "cst": cst, "tri": tri, "swT": swT, "sgub": sgub, "gv": gv})
    res = run_bass_kernel_spmd(prog.nc, in_maps, core_ids=list(range(NCORE)))
    out = np.empty((NCORE, SEQ, D), np.float32)
    for c in range(NCORE):
        out[c] = res.results[c]["outT"].reshape(D, SEQ).T
    if _dbg:
        return out, [res.results[c]["dbg"] for c in range(NCORE)]
    return out
```

```python
import math
import numpy as np
import concourse.bass as bass
import concourse.mybir as mybir
from concourse.bass_utils import run_bass_kernel_spmd

F32 = mybir.dt.float32
BF16 = mybir.dt.bfloat16
I32 = mybir.dt.int32
AF = mybir.ActivationFunctionType
ALU = mybir.AluOpType
AX = mybir.AxisListType

D = 2048
SEQ = 2048
TH = 1024
NCORE = 8
EPS = 1e-6
RING = 8
SBUF_BASE = 16512
SBUF_TOP = 229344
SCALE = (128 + 64) ** -0.5
MAGIC = 12582912.0
C1 = 6.28125
C2 = 2.0 * math.pi - C1
NT_TILES = 371


class Buf:
    __slots__ = ("name", "w", "r", "dsem", "dcnt", "ring")

    def __init__(self, name, ring=False):
        self.name = name
        self.w = None
        self.r = {}
        self.dsem = None
        self.dcnt = 0
        self.ring = ring


class K:
    def __init__(self, nc):
        self.nc = nc
        self.eng = {"pe": nc.tensor, "act": nc.scalar, "dve": nc.vector,
                    "pool": nc.gpsimd, "sp": nc.sync}
        self.esem = {}
        self.ecnt = {}
        for e in ("pe", "act", "dve"):
            self.esem[e] = nc.alloc_semaphore(name="c_" + e)
            self.ecnt[e] = 0
        self.known = {e: {} for e in self.eng}
        self.prims = []
        self.ninst = {e: 0 for e in self.eng}

    def _wait(self, e, tok):
        if tok is None:
            return
        sem, val = tok
        if e == "pe" and sem is self.esem["pe"]:
            return
        key = id(sem)
        if self.known[e].get(key, 0) >= val:
            return
        self.eng[e].wait_ge(sem, val)
        self.known[e][key] = val
        self.ninst[e] += 1

    def _deps(self, e, reads, writes):
        for b in reads:
            self._wait(e, b.w)
        for b in writes:
            self._wait(e, b.w)
            for t in list(b.r.values()):
                self._wait(e, t)

    def _pend(self, e, reads, writes):
        out = {}
        toks = [b.w for b in reads]
        for b in writes:
            toks.append(b.w)
            toks.extend(b.r.values())
        for tok in toks:
            if tok is None:
                continue
            sem, val = tok
            if e == "pe" and sem is self.esem["pe"]:
                continue
            key = id(sem)
            if self.known[e].get(key, 0) >= val:
                continue
            if key in out and out[key][1] >= val:
                continue
            out[key] = tok
        return list(out.values())

    def _issue(self, e, pend, fn):
        emb = pend.pop() if pend else None
        for sem, val in pend:
            self.eng[e].wait_ge(sem, val)
            self.known[e][id(sem)] = val
            self.ninst[e] += 1
        ins = fn(self.eng[e])
        if emb is not None:
            ins._wait_ge(emb[0], emb[1])
            self.known[e][id(emb[0])] = emb[1]
        return ins

    @staticmethod
    def _commit(tok, reads, writes):
        key = id(tok[0])
        for b in reads:
            old = b.r.get(key)
            if old is None or old[1] < tok[1]:
                b.r[key] = tok
        for b in writes:
            b.w = tok
            b.r = {}

    def op(self, e, fn, reads=(), writes=()):
        ins = self._issue(e, self._pend(e, reads, writes), fn)
        self.ecnt[e] += 1
        ins.then_inc(self.esem[e], 1)
        tok = (self.esem[e], self.ecnt[e])
        self._commit(tok, reads, writes)
        self.ninst[e] += 1
        return tok

    def mm(self, fns, reads=(), writes=(), per=None):
        e = "pe"
        ins = None
        for i, fn in enumerate(fns):
            r_i = (list(reads) if i == 0 else []) + (list(per[i]) if per is not None else [])
            ins = self._issue(e, self._pend(e, r_i, writes if i == 0 else ()), fn)
        self.ecnt[e] += 1
        ins.then_inc(self.esem[e], 1)
        tok = (self.esem[e], self.ecnt[e])
        if per is not None:
            reads = list(reads) + [b for p in per for b in p]
        self._commit(tok, reads, writes)
        self.ninst[e] += len(fns)
        return tok

    def dma(self, q, pairs, prim, reads=(), writes=()):
        if prim.dsem is None:
            prim.dsem = self.nc.alloc_semaphore(name="d_" + prim.name)
            self.prims.append(prim)
        if prim.dcnt:
            self._wait(q, (prim.dsem, prim.dcnt))
        self._deps(q, reads, writes)
        for out_ap, in_ap in pairs:
            ins = self.eng[q].dma_start(out=out_ap, in_=in_ap)
            prim.dcnt += 16
            ins.then_inc(prim.dsem, 16)
            self.ninst[q] += 1
        tok = (prim.dsem, prim.dcnt)
        self._commit(tok, reads, writes)
        return tok

    @staticmethod
    def handoff(old, new):
        acc = {}
        for b in old:
            toks = list(b.r.values()) + ([b.w] if b.w is not None else [])
            for t in toks:
                key = id(t[0])
                if key not in acc or acc[key][1] < t[1]:
                    acc[key] = t
        for b in new:
            b.w = None
            b.r = dict(acc)

    def barrier(self):
        toks = [(self.esem[x], self.ecnt[x]) for x in ("pe", "act", "dve") if self.ecnt[x]]
        toks += [(b.dsem, b.dcnt) for b in self.prims if b.dcnt and not b.ring]
        for e in ("pe", "act", "dve", "sp"):
            for t in toks:
                self._wait(e, t)

    def final(self):
        for b in self.prims:
            if b.dcnt:
                self._wait("sp", (b.dsem, b.dcnt))
        for x in ("pe", "act", "dve"):
            self._wait("sp", (self.esem[x], self.ecnt[x]))


class WStream:
    def __init__(self, k, nc, wdram, base_off, nt):
        self.k, self.nc, self.w, self.nt = k, nc, wdram, nt
        self.slots = [nc.alloc_sbuf_tensor_at(f"ws{i}", [128, 2048], BF16, offset=base_off + i * 4096)
                      for i in range(RING)]
        self.bufs = [Buf(f"ws{i}", ring=True) for i in range(RING)]
        self.specs = []
        self.consumed = 0
        self.released = 0
        self.issued = 0

    def pump(self):
        while self.issued < min(2 * self.nt, self.released + RING):
            s = self.issued % RING
            self.k.dma("pool", [(self.slots[s][:], self.w[self.issued % self.nt])], self.bufs[s],
                       writes=[self.bufs[s]])
            self.issued += 1

    def next(self, spec):
        if self.consumed < self.nt:
            self.specs.append(spec)
        assert self.consumed < 2 * self.nt
        assert self.issued > self.consumed, (self.issued, self.consumed)
        s = self.consumed % RING
        self.consumed += 1
        return self.slots[s], self.bufs[s]

    def release(self, n=1):
        self.released += n
        assert self.released <= self.consumed
        self.pump()


class Prog:
    def __init__(self, dbg=None, nt_hint=None):
        self.dbg = dbg
        nc = bass.Bass("TRN2", target_bir_lowering=False)
        self.nc = nc
        self.k = K(nc)
        self.nt_hint = nt_hint
        self.build()

    def sb(self, name, shape, dt, off):
        return self.nc.alloc_sbuf_tensor_at(name, shape, dt, offset=SBUF_BASE + off)

    def bank(self, rot):
        i = rot[0]
        rot.append(rot.pop(0))
        return self.ps[i], self.psb[i]

    def build(self):
        nc, k = self.nc, self.k
        NT = self.nt_hint
        self.xT = nc.dram_tensor("xT", [16, 128, SEQ], F32, kind="ExternalInput").ap()
        self.posd = nc.dram_tensor("pos", [1, SEQ], I32, kind="ExternalInput").ap()
        self.wd = nc.dram_tensor("wstream", [NT, 128, 2048], F32, kind="ExternalInput").ap()
        self.cstd = nc.dram_tensor("cst", [128, 192], F32, kind="ExternalInput").ap()
        self.trid = nc.dram_tensor("tri", [128, 128], F32, kind="ExternalInput").ap()
        self.swd = nc.dram_tensor("swT", [128, 1024], F32, kind="ExternalInput").ap()
        self.sbd = nc.dram_tensor("sgub", [1, 1024], F32, kind="ExternalInput").ap()
        self.gvd = nc.dram_tensor("gv", [1, 1024], F32, kind="ExternalInput").ap()
        self.outT = nc.dram_tensor("outT", [16, 128, SEQ], F32, kind="ExternalOutput").ap()
        self.kd = nc.dram_tensor("kd", [128, 8, TH], BF16, kind="Internal").ap()
        self.vd = nc.dram_tensor("vd", [128, 8, 1024], BF16, kind="Internal").ap()
        if self.dbg:
            self.dbgd = nc.dram_tensor("dbg", [16, 128, TH], F32, kind="ExternalOutput").ap()
        self.pp = [nc.alloc_psum_tensor(f"pp{i}", [128, 1024], F32) for i in range(4)]
        self.ps = [self.pp[i // 2][:, (i % 2) * 512:(i % 2 + 1) * 512] for i in range(8)]
        self.psb = [Buf(f"ps{i}") for i in range(8)]
        o = 0
        self.cst = self.sb("cst", [128, 192], F32, o); o += 768
        self.ones16 = self.sb("ones16", [128, 128], BF16, o); o += 256
        self.tri16 = self.sb("tri16", [128, 128], BF16, o); o += 256
        self.ones32 = self.sb("ones32", [1, 128], F32, o); o += 512
        self.sgub = self.sb("sgub", [1, 1024], F32, o); o += 4096
        self.swT16 = self.sb("swT16", [128, 8, 128], BF16, o); o += 2048
        self.gvb = self.sb("gvb", [128, 1024], F32, o); o += 4096
        self.Ctab = self.sb("Ctab", [64, TH], F32, o); o += 4096
        self.Stab = self.sb("Stab", [64, TH], F32, o); o += 4096
        self.krT = self.sb("krT", [64, SEQ], BF16, o); o += 4096
        self.carry = self.sb("carry", [128, 16, 2], F32, o); o += 128
        self.zb = self.sb("zb", [128, 1026], F32, o); o += 4128
        self.st = self.sb("st", [128, 2, 32], F32, o); o += 256
        self.rstd = self.sb("rstd", [128, TH], F32, o); o += 4096
        self.sq = [self.sb(f"sq{i}", [128, TH], BF16, o + 2048 * i) for i in range(2)]; o += 4096
        self.T0 = self.sb("T0", [128, TH], F32, o); T0o = o; o += 4096
        self.T1 = self.sb("T1", [128, TH], F32, o); T1o = o; o += 4096
        self.PT = [self.sb(f"PT{i}", [128, 512], BF16, T1o + 1024 * i) for i in range(4)]
        self.ws = WStream(k, nc, self.wd, SBUF_BASE + o, NT); o += RING * 4096
        XO = o; o += 65536
        HO = o; o += 32768
        BO = o; o += 32768
        assert SBUF_BASE + o <= SBUF_TOP, o
        self.xs = self.sb("xs", [128, 16, TH], F32, XO)
        self.posi = self.sb("posi", [64, TH], I32, BO)
        self.ang = self.sb("ang", [64, TH], F32, BO + 4096)
        self.tt = self.sb("tt", [64, TH], F32, BO + 8192)
        self.rr = self.sb("rr", [64, TH], F32, BO + 12288)
        self.cq32 = self.sb("cq32", [128, 4, TH], F32, XO)
        self.cqn = self.sb("cqn", [128, 4, TH], BF16, XO + 16384)
        self.ckvn = self.sb("ckvn", [128, 4, TH], BF16, XO + 24576)
        self.vn = self.sb("vn", [128, 8, 1024], BF16, XO + 32768)
        self.ug = [self.sb(f"ug{i}", [128, TH], F32, XO + 49152 + 4096 * i) for i in range(2)]
        self.kh = [self.sb(f"kh{i}", [128, SEQ], BF16, XO + 4096 * i) for i in range(2)]
        self.qn = [self.sb(f"qn{i}", [128, TH], BF16, XO + 8192 + 4096 * i) for i in range(2)]
        self.qr = [self.sb(f"qr{i}", [64, TH], BF16, XO + 8192 + 4096 * i + 2048) for i in range(2)]
        self.a32 = self.sb("a32", [128, 8, TH], F32, XO + 32768)
        self.hT = self.sb("hT", [128, 16, TH], BF16, HO)
        self.RB = self.sb("RB", [128, 16, TH], BF16, BO)
        self.s32hi = self.sb("s32hi", [128, 4, TH], F32, BO)
        B = Buf
        self.b_cst, self.b_ones, self.b_tri, self.b_sgub, self.b_sw, self.b_gv = (
            B("cst"), B("ones"), B("tri"), B("sgub"), B("sw"), B("gv"))
        self.b_C, self.b_S, self.b_kr, self.b_carry, self.b_zb = B("C"), B("S"), B("kr"), B("carry"), B("zb")
        self.b_st = [B("st0"), B("st1")]
        self.b_rstd = B("rstd")
        self.b_sq = [B("sq0"), B("sq1")]
        self.b_T0 = [B("T0a"), B("T0b")]
        self.b_T1 = [B("T1a"), B("T1b")]
        self.b_PT = [B(f"PT{i}") for i in range(4)]
        self.b_xs = [B(f"xs{i}") for i in range(16)]
        self.b_xsio = [B(f"xsio{i}") for i in range(4)]
        self.b_cq = self.b_xs[0:4]
        self.b_cqn, self.b_ckvn = self.b_xs[4], self.b_xs[6]
        self.b_cqn2, self.b_ckvn2 = self.b_xs[5], self.b_xs[7]
        self.b_vn = [self.b_xs[8 + i // 2] for i in range(8)]
        self.b_ug = [self.b_xs[12], self.b_xs[13]]
        self.b_kh = self.b_xs[0:2]
        self.b_qn = self.b_xs[2:4]
        self.b_qr = self.b_xs[2:4]
        self.b_a32 = self.b_xs[8:16]
        self.b_h = [B(f"h{i}") for i in range(16)]
        self.b_rb = [B(f"rb{i}") for i in range(16)]
        self.b_vd, self.b_kd, self.b_out, self.b_dbg = B("vd"), B("kd"), B("out"), B("dbg")
        self.b_vio = B("vio")

        self.init_consts()
        self.load_xs(0)
        for b in self.b_xsio:
            k._wait("pool", (b.dsem, b.dcnt))
        self.ws.pump()
        for h in range(2):
            self.half(h)
        assert self.ws.consumed == 2 * NT and len(self.ws.specs) == NT, (self.ws.consumed, len(self.ws.specs))
        k.final()

    def init_consts(self):
        k = self.k
        k.dma("sp", [(self.cst[:], self.cstd)], self.b_cst, writes=[self.b_cst])
        k.dma("sp", [(self.T1[:, 0:128], self.trid)], self.b_T1[0], writes=[self.b_T1[0]])
        k.dma("sp", [(self.T0[:], self.swd)], self.b_T0[0], writes=self.b_T0)
        k.dma("sp", [(self.sgub[:], self.sbd)], self.b_sgub, writes=[self.b_sgub])
        k.dma("sp", [(self.gvb[:], self.gvd.partition_broadcast(128))], self.b_gv, writes=[self.b_gv])
        k.op("dve", lambda e: e.memset(self.ones16[:], 1.0), writes=[self.b_ones])
        k.op("dve", lambda e: e.memset(self.ones32[:], 1.0), writes=[self.b_ones])
        k.op("dve", lambda e: e.tensor_copy(self.tri16[:], self.T1[:, 0:128]), reads=[self.b_T1[0]], writes=[self.b_tri])
        for g in range(8):
            k.op("dve", lambda e, g=g: e.tensor_tensor(self.swT16[:, g, :], self.T0[:, g * 128:(g + 1) * 128],
                                                      self.T1[:, 0:128], ALU.mult),
                 reads=self.b_T0 + [self.b_T1[0]], writes=[self.b_sw])
        k.op("dve", lambda e: e.memset(self.zb[:, 0:2], 0.0), writes=[self.b_zb])
        k.barrier()

    def gcol(self, base, i):
        return self.cst[:, base + i:base + i + 1]

    def rms_rstd(self, chunks, nfeat, banks=(6, 7)):
        k = self.k
        n = len(chunks)
        for c, (ap, bufs) in enumerate(chunks):
            sq, sqb = self.sq[c % 2], self.b_sq[c % 2]
            k.op("act", lambda e, ap=ap, sq=sq: e.activation(sq[:], ap, AF.Square), reads=bufs, writes=[sqb])
            for tg in range(2):
                pt, pb = self.ps[banks[tg]], self.psb[banks[tg]]
                k.mm([lambda e, pt=pt, sq=sq, tg=tg, c=c: e.matmul(pt[:], self.ones16[:], sq[:, tg * 512:(tg + 1) * 512],
                                                                 start=(c == 0), stop=(c == n - 1))],
                     reads=[sqb, self.b_ones], writes=[pb])
        for tg in range(2):
            pt, pb = self.ps[banks[tg]], self.psb[banks[tg]]
            k.op("act", lambda e, pt=pt, tg=tg: e.activation(self.rstd[:, tg * 512:(tg + 1) * 512], pt[:], AF.Sqrt,
                                                            bias=EPS, scale=1.0 / nfeat),
                 reads=[pb], writes=[self.b_rstd])
        k.op("dve", lambda e: e.reciprocal(self.rstd[:], self.rstd[:]), reads=[self.b_rstd], writes=[self.b_rstd])

    def norm_apply(self, src, srcbufs, gbase, i, dst, dstbufs):
        self.k.op("dve", lambda e: e.scalar_tensor_tensor(dst, src, self.gcol(gbase, i), self.rstd[:], ALU.mult, ALU.mult),
                  reads=list(srcbufs) + [self.b_rstd, self.b_cst], writes=list(dstbufs))

    def rope_evac(self, pa, pab, pbk, pbb, tg, out_ap, outbufs):
        k = self.k
        ta, tb = self.T0[0:64, 0:512], self.T0[0:64, 512:1024]
        k.op("dve", lambda e: e.tensor_tensor(ta, pa[0:64, :], self.Ctab[:, tg * 512:(tg + 1) * 512], ALU.mult),
             reads=[pab, self.b_C], writes=[self.b_T0[0]])
        k.op("dve", lambda e: e.tensor_tensor(tb, pbk[0:64, :], self.Stab[:, tg * 512:(tg + 1) * 512], ALU.mult),
             reads=[pbb, self.b_S], writes=[self.b_T0[1]])
        k.op("dve", lambda e: e.tensor_tensor(out_ap, ta, tb, ALU.add),
             reads=self.b_T0, writes=outbufs)

    def tap(self, name, chunks):
        if self.dbg != name or getattr(self, "_tapped", False):
            return
        self._tapped = True
        k = self.k
        k.barrier()
        for c, (ap, bufs) in enumerate(chunks):
            p = ap.shape[0]
            n = ap.shape[-1]
            k.op("act", lambda e, ap=ap, p=p, n=n: e.activation(self.T0[0:p, 0:n], ap, AF.Copy),
                 reads=bufs, writes=self.b_T0)
            k.dma("sp", [(self.dbgd[c, 0:p, 0:n], self.T0[0:p, 0:n])], self.b_T0[0], reads=self.b_T0, writes=[self.b_dbg])
        k.barrier()

    def half(self, h):
        k = self.k
        t0 = h * TH
        if h > 0:
            self.load_xs(h)
        self.tables(h)
        self.rms_rstd([(self.xs[:, c, :], [self.b_xs[c]]) for c in range(16)], D)
        for c in range(16):
            self.norm_apply(self.xs[:, c, :], [self.b_xs[c]], 0, c, self.hT[:, c, :], [self.b_h[c]])
        self.tap("h0", [(self.hT[:, c, :], [self.b_h[c]]) for c in range(16)])
        self.l0_inproj(h)
        k.handoff(self.b_T1, self.b_PT)
        self.l0_attn(h)
        k.handoff(self.b_PT, self.b_T1)
        self.tap("mix", [(self.RB[:, c, :], [self.b_rb[c]]) for c in range(16)])
        self.out_proj("e_w_out", 0, xh=h, next_g=16)
        self.tap("x1", [(self.xs[:, c, :], [self.b_xs[c]]) for c in range(16)])
        self.mlp(0, next_g=32)
        self.tap("x2", [(self.xs[:, c, :], [self.b_xs[c]]) for c in range(16)])
        self.l1_conv(h)
        self.out_proj("o_w_out", 0, next_g=48)
        self.tap("x3", [(self.xs[:, c, :], [self.b_xs[c]]) for c in range(16)])
        self.mlp(1)
        for c in range(16):
            self.norm_apply(self.xs[:, c, :], [self.b_xs[c]], 64, c, self.xs[:, c, :], [self.b_xs[c]])
            if c % 4 == 3:
                b = c // 4
                k.dma("sp", [(self.outT[cc, :, t0:t0 + TH], self.xs[:, cc, :]) for cc in range(4 * b, 4 * b + 4)],
                      self.b_xsio[b], reads=self.b_xs[4 * b:4 * b + 4], writes=[self.b_out])

    def load_xs(self, h):
        t0 = h * TH
        for b in range(4):
            self.k.dma("sp", [(self.xs[:, c, :], self.xT[c, :, t0:t0 + TH]) for c in range(4 * b, 4 * b + 4)],
                       self.b_xsio[b], writes=self.b_xs[4 * b:4 * b + 4])

    def tables(self, h):
        k = self.k
        t0 = h * TH
        k.dma("sp", [(self.posi[:], self.posd[:, t0:t0 + TH].partition_broadcast(64))], self.b_rb[0], writes=self.b_rb[0:2])
        k.op("dve", lambda e: e.tensor_copy(self.ang[:], self.posi[:]), reads=self.b_rb[0:8], writes=self.b_rb[0:8])
        k.op("dve", lambda e: e.tensor_scalar(self.ang[:], self.ang[:], self.cst[0:64, 152:153], None, ALU.mult),
             reads=self.b_rb[0:8] + [self.b_cst], writes=self.b_rb[0:8])
        for which in range(2):
            if which == 1:
                k.op("dve", lambda e: e.tensor_scalar(self.ang[:], self.ang[:], math.pi / 2, None, ALU.add),
                     reads=self.b_rb[0:8], writes=self.b_rb[0:8])
            k.op("dve", lambda e: e.tensor_scalar(self.tt[:], self.ang[:], 1.0 / (2 * math.pi), MAGIC, ALU.mult, ALU.add),
                 reads=self.b_rb[0:8], writes=self.b_rb[0:8])
            k.op("dve", lambda e: e.tensor_scalar(self.tt[:], self.tt[:], MAGIC, None, ALU.subtract), reads=self.b_rb[0:8], writes=self.b_rb[0:8])
            k.op("dve", lambda e: e.scalar_tensor_tensor(self.rr[:], self.tt[:], -C1, self.ang[:], ALU.mult, ALU.add),
                 reads=self.b_rb[0:8], writes=self.b_rb[0:8])
            k.op("dve", lambda e: e.scalar_tensor_tensor(self.rr[:], self.tt[:], -C2, self.rr[:], ALU.mult, ALU.add),
                 reads=self.b_rb[0:8], writes=self.b_rb[0:8])
            k.op("dve", lambda e: e.tensor_scalar(self.rr[:], self.rr[:], -math.pi, math.pi, ALU.max, ALU.min),
                 reads=self.b_rb[0:8], writes=self.b_rb[0:8])
            if which == 0:
                k.op("act", lambda e: e.activation(self.Stab[:], self.rr[:], AF.Sin), reads=self.b_rb[0:8], writes=[self.b_S])
                k.op("dve", lambda e: e.tensor_scalar(self.Stab[:], self.Stab[:], self.cst[0:64, 153:154], None, ALU.mult),
                     reads=[self.b_S, self.b_cst], writes=[self.b_S])
            else:
                k.op("act", lambda e: e.activation(self.Ctab[:], self.rr[:], AF.Sin), reads=self.b_rb[0:8], writes=[self.b_C])

    def l0_inproj(self, h):
        k = self.k
        t0 = h * TH
        rot = [0, 1, 2, 3, 4, 5]
        ar = np.arange
        for which, cbase, gbase, dst, dbase in ((0, 0, 80, self.cqn, 4), (1, 512, 84, self.ckvn, 6)):
            for cc in range(4):
                spec = [("e_w_in", 0, kc * 128, cbase + cc * 128 + ar(128)) for kc in range(16)]
                sl, slb = self.ws.next(spec)
                for tg in range(2):
                    pt, pb = self.bank(rot)
                    k.mm([lambda e, kc=kc, pt=pt, sl=sl, tg=tg: e.matmul(pt[:], sl[:, kc * 128:(kc + 1) * 128],
                                                                        self.hT[:, kc, tg * 512:(tg + 1) * 512],
                                                                        start=(kc == 0), stop=(kc == 15)) for kc in range(16)],
                         reads=[slb], per=[[self.b_h[kc]] for kc in range(16)], writes=[pb])
                    k.op("act", lambda e, pt=pt, cc=cc, tg=tg: e.activation(self.cq32[:, cc, tg * 512:(tg + 1) * 512], pt[:], AF.Copy),
                         reads=[pb], writes=[self.b_cq[cc]])
                self.ws.release()
            self.rms_rstd([(self.cq32[:, cc, :], [self.b_cq[cc]]) for cc in range(4)], 512)
            for cc in range(4):
                self.norm_apply(self.cq32[:, cc, :], [self.b_cq[cc]], gbase, cc, dst[:, cc, :], [self.b_xs[dbase + cc // 2]])
        sw = np.concatenate([ar(32, 64), ar(0, 32)])
        spec = [("e_w_in", 0, kc * 128, 1024 + ar(64)) for kc in range(16)] + \
               [("e_w_in", 0, kc * 128, 1024 + sw) for kc in range(16)]
        sl, slb = self.ws.next(spec)
        for tg in range(2):
            pa, pab = self.bank(rot)
            pbk, pbb = self.bank(rot)
            for pp, ppb, base in ((pa, pab, 0), (pbk, pbb, 1024)):
                k.mm([lambda e, kc=kc, pp=pp, base=base, tg=tg: e.matmul(pp[0:64, :], sl[:, base + kc * 64:base + (kc + 1) * 64],
                                                                        self.hT[:, kc, tg * 512:(tg + 1) * 512],
                                                                        start=(kc == 0), stop=(kc == 15)) for kc in range(16)],
                     reads=[slb], per=[[self.b_h[kc]] for kc in range(16)], writes=[ppb])
            self.rope_evac(pa, pab, pbk, pbb, tg, self.krT[:, t0 + tg * 512:t0 + (tg + 1) * 512], [self.b_kr])
        self.ws.release()
        it = 0
        for cg in range(2):
            sls = []
            for kq in range(4):
                spec = [("e_w_in", 0, (kq * 4 + i) * 128, 2112 + cg * 512 + ar(512)) for i in range(4)]
                sls.append(self.ws.next(spec))
            for tc in range(8):
                pt, pb = self.bank(rot)
                k.mm([lambda e, kc=kc, pt=pt, tc=tc: e.matmul(pt[:], self.hT[:, kc, tc * 128:(tc + 1) * 128],
                                                             sls[kc // 4][0][:, (kc % 4) * 512:(kc % 4 + 1) * 512],
                                                             start=(kc == 0), stop=(kc == 15)) for kc in range(16)],
                     reads=[s[1] for s in sls], per=[[self.b_h[kc]] for kc in range(16)], writes=[pb])
                i2 = it % 2
                it += 1
                ta, tab = self.T0[:, i2 * 512:(i2 + 1) * 512], self.b_T0[i2]
                tb, tbb = self.T1[:, i2 * 512:(i2 + 1) * 512], self.b_T1[i2]
                st, stb = self.st[:, i2, :], self.b_st[i2]
                k.op("act", lambda e, ta=ta, pt=pt: e.activation(ta, pt[:], AF.Gelu_apprx_tanh), reads=[pb], writes=[tab])
                k.op("act", lambda e, ta=ta, tb=tb: e.activation(tb, ta, AF.Square), reads=[tab], writes=[tbb])
                k.op("dve", lambda e, ta=ta, st=st: e.tensor_reduce(st[:, 0:4], ta.rearrange("p (g d) -> p g d", g=4), AX.X, ALU.add),
                     reads=[tab], writes=[stb])
                k.op("dve", lambda e, tb=tb, st=st: e.tensor_reduce(st[:, 4:8], tb.rearrange("p (g d) -> p g d", g=4), AX.X, ALU.add),
                     reads=[tbb], writes=[stb])
                k.op("dve", lambda e, st=st: e.tensor_scalar(st[:, 8:12], st[:, 0:4], 1.0 / 128, None, ALU.mult), reads=[stb], writes=[stb])
                k.op("dve", lambda e, st=st: e.tensor_tensor(st[:, 12:16], st[:, 8:12], st[:, 8:12], ALU.mult), reads=[stb], writes=[stb])
                k.op("dve", lambda e, st=st: e.scalar_tensor_tensor(st[:, 16:20], st[:, 4:8], 1.0 / 128, st[:, 12:16], ALU.mult, ALU.subtract),
                     reads=[stb], writes=[stb])
                k.op("act", lambda e, st=st: e.activation(st[:, 20:24], st[:, 16:20], AF.Sqrt, bias=EPS, scale=1.0), reads=[stb], writes=[stb])
                k.op("dve", lambda e, st=st: e.reciprocal(st[:, 24:28], st[:, 20:24]), reads=[stb], writes=[stb])
                for g in range(4):
                    k.op("dve", lambda e, g=g, ta=ta, st=st: e.tensor_scalar(ta[:, g * 128:(g + 1) * 128], ta[:, g * 128:(g + 1) * 128],
                                                                             st[:, 8 + g:9 + g], st[:, 24 + g:25 + g], ALU.subtract, ALU.mult),
                         reads=[tab, stb], writes=[tab])
                k.op("dve", lambda e, ta=ta, tc=tc, cg=cg: e.tensor_tensor(self.vn[:, tc, cg * 512:(cg + 1) * 512], ta,
                                                                           self.gvb[:, cg * 512:(cg + 1) * 512], ALU.mult),
                     reads=[tab, self.b_gv], writes=[self.b_vn[tc]])
            self.ws.release(4)
        self.tap("vn", [(self.vn[:, c, :], [self.b_vn[c]]) for c in range(8)])
        yrot = [6, 7]
        for g in range(8):
            spec = [("e_w_in", 0, kc * 128, 1088 + g * 128 + ar(128)) for kc in range(16)]
            sl, slb = self.ws.next(spec)
            ug, ugb = self.ug[g % 2], self.b_ug[g % 2]
            for tg in range(2):
                pt, pb = self.bank(rot)
                k.mm([lambda e, kc=kc, pt=pt, tg=tg: e.matmul(pt[:], sl[:, kc * 128:(kc + 1) * 128],
                                                             self.hT[:, kc, tg * 512:(tg + 1) * 512],
                                                             start=(kc == 0), stop=(kc == 15)) for kc in range(16)],
                     reads=[slb], per=[[self.b_h[kc]] for kc in range(16)], writes=[pb])
                k.op("act", lambda e, pt=pt, ug=ug, tg=tg: e.activation(ug[:, tg * 512:(tg + 1) * 512], pt[:], AF.Gelu_apprx_tanh),
                     reads=[pb], writes=[ugb])
            self.ws.release()
            if g < 4:
                s32, s32b = self.cq32[:, g, :], [self.b_cq[g]]
            else:
                s32, s32b = self.s32hi[:, g - 4, :], [self.b_rb[2 * (g - 4)], self.b_rb[2 * (g - 4) + 1]]
            for tq in range(2):
                pt, pb = self.ps[yrot[tq]], self.psb[yrot[tq]]
                fns = []
                for c4 in range(4):
                    c = tq * 4 + c4
                    fns.append(lambda e, pt=pt, c=c, c4=c4, g=g: e.matmul(pt[:, c4 * 128:(c4 + 1) * 128],
                                                                        self.vn[:, c, g * 128:(g + 1) * 128], self.swT16[:, g, :],
                                                                        start=True, stop=False))
                    fns.append(lambda e, pt=pt, c4=c4, g=g: e.matmul(pt[:, c4 * 128:(c4 + 1) * 128],
                                                                   self.ones32[0:1, :], self.sgub[0:1, g * 128:(g + 1) * 128],
                                                                   start=False, stop=True))
                k.mm(fns, reads=self.b_vn + [self.b_sw, self.b_ones, self.b_sgub], writes=[pb])
                k.op("dve", lambda e, pt=pt, tq=tq, s32=s32, ug=ug: e.tensor_tensor(s32[:, tq * 512:(tq + 1) * 512], pt[:],
                                                                                    ug[:, tq * 512:(tq + 1) * 512], ALU.mult),
                     reads=[pb, ugb], writes=s32b)
        chunks = [(self.cq32[:, g, :], [self.b_cq[g]]) for g in range(4)] + \
                 [(self.s32hi[:, g, :], [self.b_rb[2 * g], self.b_rb[2 * g + 1]]) for g in range(4)]
        self.tap("s32", chunks)
        self.rms_rstd(chunks, 1024, banks=(4, 5))
        for g in range(8):
            self.norm_apply(chunks[g][0], chunks[g][1], 96, g, self.RB[:, 8 + g, :], [self.b_rb[8 + g]])

    def l0_attn(self, h):
        k = self.k
        t0 = h * TH
        ar = np.arange
        rot = [4, 5, 6, 7]
        Vall, vb = self.hT, self.b_h
        if h == 1:
            k.dma("sp", [(Vall[:, 0:8, :], self.vd)], self.b_vio, reads=[self.b_vd], writes=vb[0:8])
        for hg in range(2):
            cols = np.concatenate([hd * 256 + 128 + ar(128) for hd in range(hg * 4, hg * 4 + 4)])
            spec = [("e_w_ukv", 0, kc * 128, cols) for kc in range(4)]
            sl, slb = self.ws.next(spec)
            for tc in range(8):
                pt, pb = self.bank(rot)
                k.mm([lambda e, kc=kc, pt=pt, tc=tc: e.matmul(pt[:], self.ckvn[:, kc, tc * 128:(tc + 1) * 128],
                                                             sl[:, kc * 512:(kc + 1) * 512], start=(kc == 0), stop=(kc == 3))
                      for kc in range(4)], reads=[slb, self.b_xs[6], self.b_xs[7]], writes=[pb])
                k.op("act", lambda e, pt=pt, tc=tc, hg=hg: e.activation(Vall[:, h * 8 + tc, hg * 512:(hg + 1) * 512], pt[:], AF.Copy),
                     reads=[pb], writes=[vb[h * 8 + tc]])
            self.ws.release()
        if h == 0:
            k.dma("sp", [(self.vd, Vall[:, 0:8, :])], self.b_vio, reads=vb[0:8], writes=[self.b_vd])
        sw = np.concatenate([ar(32, 64), ar(0, 32)])
        orot = [0, 1]
        srot = [2, 3]
        pti = 0
        for hd in range(8):
            i2 = hd % 2
            kh, khb = self.kh[i2], self.b_kh[i2]
            qn, qnb = self.qn[i2], self.b_qn[i2]
            qr, qrb = self.qr[i2], self.b_qr[i2]
            if h == 1:
                k.dma("sp", [(kh[:, 0:TH], self.kd[:, hd, :])], khb, reads=[self.b_kd], writes=[khb])
            spec = [("e_w_uq", 0, kc * 128, hd * 192 + ar(128)) for kc in range(4)] + \
                   [("e_w_uq", 0, kc * 128, hd * 192 + 128 + ar(64)) for kc in range(4)] + \
                   [("e_w_uq", 0, kc * 128, hd * 192 + 128 + sw) for kc in range(4)] + \
                   [("e_w_ukv", 0, kc * 128, hd * 256 + ar(128)) for kc in range(4)]
            sl, slb = self.ws.next(spec)
            for tg in range(2):
                pt, pb = self.bank(rot)
                k.mm([lambda e, kc=kc, pt=pt, tg=tg: e.matmul(pt[:], sl[:, kc * 128:(kc + 1) * 128],
                                                             self.cqn[:, kc, tg * 512:(tg + 1) * 512], start=(kc == 0), stop=(kc == 3))
                      for kc in range(4)], reads=[slb, self.b_xs[4], self.b_xs[5]], writes=[pb])
                k.op("act", lambda e, pt=pt, tg=tg, qn=qn: e.activation(qn[:, tg * 512:(tg + 1) * 512], pt[:], AF.Copy),
                     reads=[pb], writes=[qnb])
                pa, pab = self.bank(rot)
                pbk, pbb = self.bank(rot)
                for pp, ppb, base in ((pa, pab, 512), (pbk, pbb, 768)):
                    k.mm([lambda e, kc=kc, pp=pp, base=base, tg=tg: e.matmul(pp[0:64, :], sl[:, base + kc * 64:base + (kc + 1) * 64],
                                                                            self.cqn[:, kc, tg * 512:(tg + 1) * 512],
                                                                            start=(kc == 0), stop=(kc == 3)) for kc in range(4)],
                         reads=[slb, self.b_xs[4], self.b_xs[5]], writes=[ppb])
                self.rope_evac(pa, pab, pbk, pbb, tg, qr[:, tg * 512:(tg + 1) * 512], [qrb])
                pt, pb = self.bank(rot)
                k.mm([lambda e, kc=kc, pt=pt, tg=tg: e.matmul(pt[:], sl[:, 1024 + kc * 128:1024 + (kc + 1) * 128],
                                                             self.ckvn[:, kc, tg * 512:(tg + 1) * 512], start=(kc == 0), stop=(kc == 3))
                      for kc in range(4)], reads=[slb, self.b_xs[6], self.b_xs[7]], writes=[pb])
                k.op("act", lambda e, pt=pt, tg=tg, kh=kh: e.activation(kh[:, t0 + tg * 512:t0 + (tg + 1) * 512], pt[:], AF.Copy),
                     reads=[pb], writes=[khb])
            self.ws.release()
            if h == 0:
                k.dma("sp", [(self.kd[:, hd, :], kh[:, 0:TH])], khb, reads=[khb], writes=[self.b_kd])
            for tg in range(2):
                Q0 = t0 + tg * 512
                nkb = (Q0 + 512) // 128
                po, pob = self.bank(orot)
                psm, psmb = self.bank(srot)

                def c0_of(j):
                    r = j - Q0 // 128
                    return 128 * r if r > 0 else 0

                def score(j):
                    pt, pb = self.bank(rot)
                    c0 = c0_of(j)
                    k.mm([lambda e: e.matmul(pt[:, c0:512], kh[:, j * 128:(j + 1) * 128], qn[:, tg * 512 + c0:(tg + 1) * 512],
                                             start=True, stop=False),
                          lambda e: e.matmul(pt[:, c0:512], self.krT[:, j * 128:(j + 1) * 128], qr[:, tg * 512 + c0:(tg + 1) * 512],
                                             start=False, stop=True)],
                         reads=[khb, qnb, qrb, self.b_kr], writes=[pb])
                    return pt, pb

                pend_s = [score(jj) for jj in range(min(2, nkb))]
                for j in range(nkb):
                    pt, pb = pend_s.pop(0)
                    if j + 2 < nkb:
                        pend_s.append(score(j + 2))
                    c0 = c0_of(j)
                    r = j - Q0 // 128
                    P, Pb = self.PT[pti % 4], self.b_PT[pti % 4]
                    pti += 1
                    k.op("act", lambda e, pt=pt, P=P, c0=c0: e.activation(P[:, c0:512], pt[:, c0:512], AF.Exp, scale=SCALE),
                         reads=[pb], writes=[Pb])
                    if r >= 0:
                        k.op("dve", lambda e, P=P, c0=c0: e.tensor_tensor(P[:, c0:c0 + 128], P[:, c0:c0 + 128], self.tri16[:], ALU.mult),
                             reads=[Pb, self.b_tri], writes=[Pb])
                    k.mm([lambda e, P=P, c0=c0, j=j: e.matmul(po[:, c0:512], Vall[:, j, hd * 128:(hd + 1) * 128], P[:, c0:512],
                                                             start=(j == 0), stop=(j == nkb - 1))],
                         reads=[Pb, vb[j]], writes=[pob])
                    k.mm([lambda e, P=P, c0=c0, j=j: e.matmul(psm[:, c0:512], self.ones16[:], P[:, c0:512],
                                                             start=(j == 0), stop=(j == nkb - 1))],
                         reads=[Pb, self.b_ones], writes=[psmb])
                rs, rsb = self.T0[:, tg * 512:(tg + 1) * 512], self.b_T0[tg]
                k.op("dve", lambda e, rs=rs, psm=psm: e.reciprocal(rs, psm[:]), reads=[psmb], writes=[rsb])
                k.op("dve", lambda e, rs=rs, po=po, tg=tg, hd=hd: e.tensor_tensor(self.a32[:, hd, tg * 512:(tg + 1) * 512], po[:], rs, ALU.mult),
                     reads=[pob, rsb], writes=[self.b_a32[hd]])
        self.tap("a32", [(self.a32[:, c, :], [self.b_a32[c]]) for c in range(8)])
        self.rms_rstd([(self.a32[:, c, :], [self.b_a32[c]]) for c in range(8)], 1024, banks=(4, 5))
        for hd in range(8):
            self.norm_apply(self.a32[:, hd, :], [self.b_a32[hd]], 88, hd, self.RB[:, hd, :], [self.b_rb[hd]])

    def rms_mm(self, c, n, banks=(6, 7)):
        k = self.k
        sq, sqb = self.sq[c % 2], self.b_sq[c % 2]
        for tg in range(2):
            pt, pb = self.ps[banks[tg]], self.psb[banks[tg]]
            k.mm([lambda e: e.matmul(pt[:], self.ones16[:], sq[:, tg * 512:(tg + 1) * 512], start=(c == 0), stop=(c == n - 1))],
                 reads=[sqb, self.b_ones], writes=[pb])

    def rms_finish(self, nfeat, banks=(6, 7)):
        k = self.k
        for tg in range(2):
            pt, pb = self.ps[banks[tg]], self.psb[banks[tg]]
            k.op("act", lambda e: e.activation(self.rstd[:, tg * 512:(tg + 1) * 512], pt[:], AF.Sqrt, bias=EPS, scale=1.0 / nfeat),
                 reads=[pb], writes=[self.b_rstd])
        k.op("dve", lambda e: e.reciprocal(self.rstd[:], self.rstd[:]), reads=[self.b_rstd], writes=[self.b_rstd])

    def out_proj(self, wname, layer, xh=None, next_g=None):
        k = self.k
        prot = [0, 1, 2]
        ar = np.arange
        stg = [(self.T0, self.b_T0), (self.T1, self.b_T1)]

        def ld(j):
            T, tb = stg[j % 2]
            k.dma("sp", [(T[:], self.xT[j, :, xh * TH:(xh + 1) * TH])], tb[0], writes=tb)

        if xh is not None:
            ld(0)
        for j in range(16):
            if xh is not None and j + 1 < 16:
                ld(j + 1)
            spec = [(wname, layer, kc * 128, j * 128 + ar(128)) for kc in range(16)]
            sl, slb = self.ws.next(spec)
            p = prot[0]
            prot.append(prot.pop(0))
            for tg in range(2):
                pt, pb = self.ps[2 * p + tg], self.psb[2 * p + tg]
                k.mm([lambda e, kc=kc, pt=pt, tg=tg: e.matmul(pt[:], sl[:, kc * 128:(kc + 1) * 128],
                                                             self.RB[:, kc, tg * 512:(tg + 1) * 512], start=(kc == 0), stop=(kc == 15))
                      for kc in range(16)], reads=[slb], per=[[self.b_rb[kc]] for kc in range(16)], writes=[pb])
            pbs = [self.psb[2 * p], self.psb[2 * p + 1]]
            if xh is None:
                k.op("dve", lambda e, j=j, p=p: e.tensor_tensor(self.xs[:, j, :], self.xs[:, j, :], self.pp[p][:], ALU.add),
                     reads=pbs + [self.b_xs[j]], writes=[self.b_xs[j]])
            else:
                T, tb = stg[j % 2]
                k.op("dve", lambda e, j=j, p=p, T=T: e.tensor_tensor(self.xs[:, j, :], T[:], self.pp[p][:], ALU.add),
                     reads=pbs + tb, writes=[self.b_xs[j]])
            self.ws.release()
            if next_g is not None:
                if j >= 2:
                    self.rms_mm(j - 2, 16)
                k.op("act", lambda e, j=j: e.activation(self.hT[:, j, :], self.xs[:, j, :], AF.Copy, scale=self.gcol(next_g, j)),
                     reads=[self.b_xs[j], self.b_cst], writes=[self.b_h[j]])
                sq, sqb = self.sq[j % 2], self.b_sq[j % 2]
                k.op("act", lambda e, j=j, sq=sq: e.activation(sq[:], self.xs[:, j, :], AF.Square), reads=[self.b_xs[j]], writes=[sqb])
        if next_g is not None:
            self.rms_mm(14, 16)
            self.rms_mm(15, 16)
            self.rms_finish(D)

    def mlp(self, layer, next_g=None):
        k = self.k
        ar = np.arange
        prot = [0, 1, 2, 3]

        def pair():
            p = prot[0]
            prot.append(prot.pop(0))
            return p

        tmps = [(self.T0, self.b_T0), (self.T1, self.b_T1)]
        ti = [0]

        def m1(fb):
            ab = (fb % 4) * 4
            for fcl in range(4):
                spec = [("mlp_w1", layer, kc * 128, fb * 512 + fcl * 128 + ar(128)) for kc in range(16)]
                sl, slb = self.ws.next(spec)
                p = pair()
                for tg in range(2):
                    pt, pb = self.ps[2 * p + tg], self.psb[2 * p + tg]
                    k.mm([lambda e, kc=kc, pt=pt, tg=tg: e.matmul(pt[:], sl[:, kc * 128:(kc + 1) * 128],
                                                                 self.hT[:, kc, tg * 512:(tg + 1) * 512], start=(kc == 0), stop=(kc == 15))
                          for kc in range(16)], reads=[slb], per=[[self.b_h[kc]] for kc in range(16)], writes=[pb])
                self.ws.release()
                pbs = [self.psb[2 * p], self.psb[2 * p + 1]]
                tm, tmb = tmps[ti[0] % 2]
                ti[0] += 1
                c = ab + fcl
                k.op("act", lambda e, tm=tm, p=p: e.activation(tm[:], self.pp[p][:], AF.Relu), reads=pbs, writes=tmb)
                k.op("dve", lambda e, tm=tm: e.tensor_tensor(tm[:], tm[:], self.rstd[:], ALU.mult),
                     reads=tmb + [self.b_rstd], writes=tmb)
                k.op("act", lambda e, tm=tm, c=c: e.activation(self.RB[:, c, :], tm[:], AF.Square),
                     reads=tmb, writes=[self.b_rb[c]])

        def m2(fb):
            ab = (fb % 4) * 4
            last = (fb == 15)
            if last and 3 in prot:
                prot.remove(3)
            for dg in range(4):
                spec = [("mlp_w2", layer, fb * 512 + fcl * 128, dg * 512 + ar(512)) for fcl in range(4)]
                sl, slb = self.ws.next(spec)
                for dcl in range(4):
                    j = dg * 4 + dcl
                    p = pair()
                    for tg in range(2):
                        pt, pb = self.ps[2 * p + tg], self.psb[2 * p + tg]
                        k.mm([lambda e, fcl=fcl, pt=pt, tg=tg, dcl=dcl: e.matmul(pt[:], sl[:, fcl * 512 + dcl * 128:fcl * 512 + (dcl + 1) * 128],
                                                                                self.RB[:, ab + fcl, tg * 512:(tg + 1) * 512],
                                                                                start=(fcl == 0), stop=(fcl == 3)) for fcl in range(4)],
                             reads=[slb], per=[[self.b_rb[ab + f]] for f in range(4)], writes=[pb])
                    k.op("dve", lambda e, j=j, p=p: e.tensor_tensor(self.xs[:, j, :], self.xs[:, j, :], self.pp[p][:], ALU.add),
                         reads=[self.psb[2 * p], self.psb[2 * p + 1], self.b_xs[j]], writes=[self.b_xs[j]])
                    if last:
                        if j >= 2:
                            self.rms_mm(j - 2, 16)
                        if next_g is not None:
                            k.op("act", lambda e, j=j: e.activation(self.hT[:, j, :], self.xs[:, j, :], AF.Copy, scale=self.gcol(next_g, j)),
                                 reads=[self.b_xs[j], self.b_cst], writes=[self.b_h[j]])
                        sq, sqb = self.sq[j % 2], self.b_sq[j % 2]
                        k.op("act", lambda e, j=j, sq=sq: e.activation(sq[:], self.xs[:, j, :], AF.Square), reads=[self.b_xs[j]], writes=[sqb])
                self.ws.release()
            if last:
                self.rms_mm(14, 16)
                self.rms_mm(15, 16)
                self.rms_finish(D)

        m1(0)
        for fb in range(16):
            if fb + 1 < 16:
                m1(fb + 1)
            m2(fb)

    def l1_conv(self, h):
        k = self.k
        ar = np.arange
        rot = [0, 1, 2, 3, 4, 5, 6, 7]
        for j in range(16):
            banks = {}
            for name, cb in (("c", 2048), ("x", 4096), ("b", 0)):
                spec = [("o_w_in", 0, kc * 128, cb + j * 128 + ar(128)) for kc in range(16)]
                sl, slb = self.ws.next(spec)
                for tg in range(2):
                    pt, pb = self.bank(rot)
                    banks[(name, tg)] = (pt, pb)
                    k.mm([lambda e, kc=kc, pt=pt, tg=tg, sl=sl: e.matmul(pt[:], sl[:, kc * 128:(kc + 1) * 128],
                                                                        self.hT[:, kc, tg * 512:(tg + 1) * 512], start=(kc == 0), stop=(kc == 15))
                          for kc in range(16)], reads=[slb], per=[[self.b_h[kc]] for kc in range(16)], writes=[pb])
                    if name == "c":
                        k.op("dve", lambda e, pt=pt, tg=tg: e.tensor_tensor(self.T0[:, tg * 512:(tg + 1) * 512], pt[:],
                                                                           self.rstd[:, tg * 512:(tg + 1) * 512], ALU.mult),
                             reads=[pb, self.b_rstd], writes=[self.b_T0[tg]])
                    elif name == "x":
                        if tg == 0 and h == 1:
                            k.op("act", lambda e, j=j: e.activation(self.zb[:, 0:2], self.carry[:, j, :], AF.Copy),
                                 reads=[self.b_carry], writes=[self.b_zb])
                        k.op("dve", lambda e, pt=pt, tg=tg: e.tensor_tensor(self.zb[:, 2 + tg * 512:2 + (tg + 1) * 512], pt[:],
                                                                           self.rstd[:, tg * 512:(tg + 1) * 512], ALU.mult),
                             reads=[pb, self.b_rstd], writes=[self.b_zb])
                        k.op("dve", lambda e, tg=tg: e.tensor_tensor(self.zb[:, 2 + tg * 512:2 + (tg + 1) * 512],
                                                                    self.zb[:, 2 + tg * 512:2 + (tg + 1) * 512],
                                                                    self.T0[:, tg * 512:(tg + 1) * 512], ALU.mult),
                             reads=[self.b_zb, self.b_T0[tg]], writes=[self.b_zb])
                self.ws.release()
                if name == "x":
                    if h == 0:
                        k.op("act", lambda e, j=j: e.activation(self.carry[:, j, :], self.zb[:, 1024:1026], AF.Copy),
                             reads=[self.b_zb], writes=[self.b_carry])
                    k.op("dve", lambda e, j=j: e.tensor_scalar(self.T1[:], self.zb[:, 0:1024], self.gcol(104, j), None, ALU.mult),
                         reads=[self.b_zb, self.b_cst], writes=self.b_T1)
                    k.op("dve", lambda e, j=j: e.scalar_tensor_tensor(self.T1[:], self.zb[:, 1:1025], self.gcol(120, j), self.T1[:], ALU.mult, ALU.add),
                         reads=[self.b_zb, self.b_cst] + self.b_T1, writes=self.b_T1)
                    k.op("dve", lambda e, j=j: e.scalar_tensor_tensor(self.T1[:], self.zb[:, 2:1026], self.gcol(136, j), self.T1[:], ALU.mult, ALU.add),
                         reads=[self.b_zb, self.b_cst] + self.b_T1, writes=self.b_T1)
                    k.op("dve", lambda e: e.tensor_tensor(self.T1[:], self.T1[:], self.rstd[:], ALU.mult),
                         reads=self.b_T1 + [self.b_rstd], writes=self.b_T1)
            for tg in range(2):
                pt, pb = banks[("b", tg)]
                k.op("dve", lambda e, pt=pt, tg=tg, j=j: e.tensor_tensor(self.RB[:, j, tg * 512:(tg + 1) * 512], pt[:],
                                                                        self.T1[:, tg * 512:(tg + 1) * 512], ALU.mult),
                     reads=[pb, self.b_T1[tg]], writes=[self.b_rb[j]])


_CACHE = {}


def _get_prog(dbg=None):
    key = ("prog", dbg)
    if key not in _CACHE:
        _CACHE[key] = Prog(dbg=dbg, nt_hint=NT_TILES)
    return _CACHE[key]


def _build_wstream(specs, W):
    nt = len(specs)
    ws = np.zeros((nt, 128, 2048), np.float32)
    for t, spec in enumerate(specs):
        off = 0
        for (name, layer, r0, cols) in spec:
            n = len(cols)
            c0, c1 = int(cols[0]), int(cols[-1])
            m = W[name][layer]
            if c1 - c0 == n - 1:
                ws[t, :, off:off + n] = m[r0:r0 + 128, c0:c1 + 1]
            else:
                ws[t, :, off:off + n] = m[r0:r0 + 128][:, cols]
            off += n
        assert off <= 2048
    return ws


def _fm(v):
    return np.ascontiguousarray(np.asarray(v, np.float32).reshape(-1, 128).T)


def kernel(x, positions, e_norm_mix, e_w_in, e_q_norm, e_w_uq, e_kv_norm, e_w_ukv,
           e_v_norm, e_sgu_w, e_sgu_b, e_mla_out_norm, e_sgu_out_norm, e_w_out,
           o_norm_mix, o_w_in, o_conv_w, o_w_out, mlp_norm, mlp_w1, mlp_w2, final_norm, _dbg=None):
    prog = _get_prog(_dbg)
    W = {"e_w_in": np.asarray(e_w_in), "e_w_uq": np.asarray(e_w_uq), "e_w_ukv": np.asarray(e_w_ukv),
         "e_w_out": np.asarray(e_w_out), "o_w_in": np.asarray(o_w_in), "o_w_out": np.asarray(o_w_out),
         "mlp_w1": np.asarray(mlp_w1), "mlp_w2": np.asarray(mlp_w2)}
    wstream = _build_wstream(prog.ws.specs, W)
    cst = np.zeros((128, 192), np.float32)
    cst[:, 0:16] = _fm(e_norm_mix[0])
    cst[:, 16:32] = _fm(mlp_norm[0])
    cst[:, 32:48] = _fm(o_norm_mix[0])
    cst[:, 48:64] = _fm(mlp_norm[1])
    cst[:, 64:80] = _fm(final_norm)
    cst[:, 80:84] = _fm(e_q_norm[0])
    cst[:, 84:88] = _fm(e_kv_norm[0])
    cst[:, 88:96] = _fm(e_mla_out_norm[0])
    cst[:, 96:104] = _fm(e_sgu_out_norm[0])
    for kk in range(3):
        cst[:, 104 + 16 * kk:120 + 16 * kk] = _fm(np.asarray(o_conv_w)[0, kk])
    inv_freq = (np.float32(10000.0) ** (-np.arange(0, 64, 2, dtype=np.float32) / np.float32(64))).astype(np.float32)
    cst[0:64, 152] = np.concatenate([inv_freq, inv_freq])
    cst[0:32, 153] = -1.0
    cst[32:64, 153] = 1.0
    tri = (np.arange(128)[:, None] <= np.arange(128)[None, :]).astype(np.float32)
    swT = np.ascontiguousarray(np.asarray(e_sgu_w, np.float32)[0].transpose(2, 0, 1)).reshape(128, 1024)
    sgub = np.ascontiguousarray(np.asarray(e_sgu_b, np.float32)[0].reshape(1, 1024))
    gv = np.ascontiguousarray(np.asarray(e_v_norm, np.float32)[0].reshape(1, 1024))
    x = np.asarray(x, np.float32)
    positions = np.asarray(positions, np.int32)
    in_maps = []
    for c in range(NCORE):
        xT = np.ascontiguousarray(x[c].T).reshape(16, 128, SEQ)
        in_maps.append({"xT": xT, "pos": np.ascontiguousarray(positions[c].reshape(1, SEQ)), "wstream": wstream,
                        "cst": cst, "tri": tri, "swT": swT, "sgub": sgub, "gv": gv})
    res = run_bass_kernel_spmd(prog.nc, in_maps, core_ids=list(range(NCORE)))
    out = np.empty((NCORE, SEQ, D), np.float32)
    for c in range(NCORE):
        out[c] = res.results[c]["outT"].reshape(D, SEQ).T
    if _dbg:
        return out, [res.results[c]["dbg"] for c in range(NCORE)]
    return out
```

```python
import math
import numpy as np
import concourse.bass as bass
import concourse.mybir as mybir
from concourse.bass_utils import run_bass_kernel_spmd

F32 = mybir.dt.float32
BF16 = mybir.dt.bfloat16
I32 = mybir.dt.int32
AF = mybir.ActivationFunctionType
ALU = mybir.AluOpType
AX = mybir.AxisListType

D = 2048
SEQ = 2048
TH = 1024
NCORE = 8
EPS = 1e-6
RING = 8
SBUF_BASE = 16512
SBUF_TOP = 229344
SCALE = (128 + 64) ** -0.5
MAGIC = 12582912.0
C1 = 6.28125
C2 = 2.0 * math.pi - C1
NT_TILES = 371


class Buf:
    __slots__ = ("name", "w", "r", "dsem", "dcnt", "ring")

    def __init__(self, name, ring=False):
        self.name = name
        self.w = None
        self.r = {}
        self.dsem = None
        self.dcnt = 0
        self.ring = ring


class K:
    def __init__(self, nc):
        self.nc = nc
        self.eng = {"pe": nc.tensor, "act": nc.scalar, "dve": nc.vector,
                    "pool": nc.gpsimd, "sp": nc.sync}
        self.esem = {}
        self.ecnt = {}
        for e in ("pe", "act", "dve"):
            self.esem[e] = nc.alloc_semaphore(name="c_" + e)
            self.ecnt[e] = 0
        self.known = {e: {} for e in self.eng}
        self.prims = []
        self.ninst = {e: 0 for e in self.eng}

    def _wait(self, e, tok):
        if tok is None:
            return
        sem, val = tok
        if e == "pe" and sem is self.esem["pe"]:
            return
        key = id(sem)
        if self.known[e].get(key, 0) >= val:
            return
        self.eng[e].wait_ge(sem, val)
        self.known[e][key] = val
        self.ninst[e] += 1

    def _deps(self, e, reads, writes):
        for b in reads:
            self._wait(e, b.w)
        for b in writes:
            self._wait(e, b.w)
            for t in list(b.r.values()):
                self._wait(e, t)

    def _pend(self, e, reads, writes):
        out = {}
        toks = [b.w for b in reads]
        for b in writes:
            toks.append(b.w)
            toks.extend(b.r.values())
        for tok in toks:
            if tok is None:
                continue
            sem, val = tok
            if e == "pe" and sem is self.esem["pe"]:
                continue
            key = id(sem)
            if self.known[e].get(key, 0) >= val:
                continue
            if key in out and out[key][1] >= val:
                continue
            out[key] = tok
        return list(out.values())

    def _issue(self, e, pend, fn):
        emb = pend.pop() if pend else None
        for sem, val in pend:
            self.eng[e].wait_ge(sem, val)
            self.known[e][id(sem)] = val
            self.ninst[e] += 1
        ins = fn(self.eng[e])
        if emb is not None:
            ins._wait_ge(emb[0], emb[1])
            self.known[e][id(emb[0])] = emb[1]
        return ins

    @staticmethod
    def _commit(tok, reads, writes):
        key = id(tok[0])
        for b in reads:
            old = b.r.get(key)
            if old is None or old[1] < tok[1]:
                b.r[key] = tok
        for b in writes:
            b.w = tok
            b.r = {}

    def op(self, e, fn, reads=(), writes=()):
        ins = self._issue(e, self._pend(e, reads, writes), fn)
        self.ecnt[e] += 1
        ins.then_inc(self.esem[e], 1)
        tok = (self.esem[e], self.ecnt[e])
        self._commit(tok, reads, writes)
        self.ninst[e] += 1
        return tok

    def mm(self, fns, reads=(), writes=(), per=None):
        e = "pe"
        ins = None
        for i, fn in enumerate(fns):
            r_i = (list(reads) if i == 0 else []) + (list(per[i]) if per is not None else [])
            ins = self._issue(e, self._pend(e, r_i, writes if i == 0 else ()), fn)
        self.ecnt[e] += 1
        ins.then_inc(self.esem[e], 1)
        tok = (self.esem[e], self.ecnt[e])
        if per is not None:
            reads = list(reads) + [b for p in per for b in p]
        self._commit(tok, reads, writes)
        self.ninst[e] += len(fns)
        return tok

    def dma(self, q, pairs, prim, reads=(), writes=()):
        if prim.dsem is None:
            prim.dsem = self.nc.alloc_semaphore(name="d_" + prim.name)
            self.prims.append(prim)
        if prim.dcnt:
            self._wait(q, (prim.dsem, prim.dcnt))
        self._deps(q, reads, writes)
        for out_ap, in_ap in pairs:
            ins = self.eng[q].dma_start(out=out_ap, in_=in_ap)
            prim.dcnt += 16
            ins.then_inc(prim.dsem, 16)
            self.ninst[q] += 1
        tok = (prim.dsem, prim.dcnt)
        self._commit(tok, reads, writes)
        return tok

    @staticmethod
    def handoff(old, new):
        acc = {}
        for b in old:
            toks = list(b.r.values()) + ([b.w] if b.w is not None else [])
            for t in toks:
                key = id(t[0])
                if key not in acc or acc[key][1] < t[1]:
                    acc[key] = t
        for b in new:
            b.w = None
            b.r = dict(acc)

    def barrier(self):
        toks = [(self.esem[x], self.ecnt[x]) for x in ("pe", "act", "dve") if self.ecnt[x]]
        toks += [(b.dsem, b.dcnt) for b in self.prims if b.dcnt and not b.ring]
        for e in ("pe", "act", "dve", "sp"):
            for t in toks:
                self._wait(e, t)

    def final(self):
        for b in self.prims:
            if b.dcnt:
                self._wait("sp", (b.dsem, b.dcnt))
        for x in ("pe", "act", "dve"):
            self._wait("sp", (self.esem[x], self.ecnt[x]))


class WStream:
    def __init__(self, k, nc, wdram, base_off, nt):
        self.k, self.nc, self.w, self.nt = k, nc, wdram, nt
        self.slots = [nc.alloc_sbuf_tensor_at(f"ws{i}", [128, 2048], BF16, offset=base_off + i * 4096)
                      for i in range(RING)]
        self.bufs = [Buf(f"ws{i}", ring=True) for i in range(RING)]
        self.specs = []
        self.consumed = 0
        self.released = 0
        self.issued = 0

    def pump(self):
        while self.issued < min(2 * self.nt, self.released + RING):
            s = self.issued % RING
            self.k.dma("pool", [(self.slots[s][:], self.w[self.issued % self.nt])], self.bufs[s],
                       writes=[self.bufs[s]])
            self.issued += 1

    def next(self, spec):
        if self.consumed < self.nt:
            self.specs.append(spec)
        assert self.consumed < 2 * self.nt
        assert self.issued > self.consumed, (self.issued, self.consumed)
        s = self.consumed % RING
        self.consumed += 1
        return self.slots[s], self.bufs[s]

    def release(self, n=1):
        self.released += n
        assert self.released <= self.consumed
        self.pump()


class Prog:
    def __init__(self, dbg=None, nt_hint=None):
        self.dbg = dbg
        nc = bass.Bass("TRN2", target_bir_lowering=False)
        self.nc = nc
        self.k = K(nc)
        self.nt_hint = nt_hint
        self.build()

    def sb(self, name, shape, dt, off):
        return self.nc.alloc_sbuf_tensor_at(name, shape, dt, offset=SBUF_BASE + off)

    def bank(self, rot):
        i = rot[0]
        rot.append(rot.pop(0))
        return self.ps[i], self.psb[i]

    def build(self):
        nc, k = self.nc, self.k
        NT = self.nt_hint
        self.xT = nc.dram_tensor("xT", [16, 128, SEQ], F32, kind="ExternalInput").ap()
        self.posd = nc.dram_tensor("pos", [1, SEQ], I32, kind="ExternalInput").ap()
        self.wd = nc.dram_tensor("wstream", [NT, 128, 2048], F32, kind="ExternalInput").ap()
        self.cstd = nc.dram_tensor("cst", [128, 192], F32, kind="ExternalInput").ap()
        self.trid = nc.dram_tensor("tri", [128, 128], F32, kind="ExternalInput").ap()
        self.swd = nc.dram_tensor("swT", [128, 1024], F32, kind="ExternalInput").ap()
        self.sbd = nc.dram_tensor("sgub", [1, 1024], F32, kind="ExternalInput").ap()
        self.gvd = nc.dram_tensor("gv", [1, 1024], F32, kind="ExternalInput").ap()
        self.outT = nc.dram_tensor("outT", [16, 128, SEQ], F32, kind="ExternalOutput").ap()
        self.kd = nc.dram_tensor("kd", [128, 8, TH], BF16, kind="Internal").ap()
        self.vd = nc.dram_tensor("vd", [128, 8, 1024], BF16, kind="Internal").ap()
        if self.dbg:
            self.dbgd = nc.dram_tensor("dbg", [16, 128, TH], F32, kind="ExternalOutput").ap()
        self.pp = [nc.alloc_psum_tensor(f"pp{i}", [128, 1024], F32) for i in range(4)]
        self.ps = [self.pp[i // 2][:, (i % 2) * 512:(i % 2 + 1) * 512] for i in range(8)]
        self.psb = [Buf(f"ps{i}") for i in range(8)]
        o = 0
        self.cst = self.sb("cst", [128, 192], F32, o); o += 768
        self.ones16 = self.sb("ones16", [128, 128], BF16, o); o += 256
        self.tri16 = self.sb("tri16", [128, 128], BF16, o); o += 256
        self.ones32 = self.sb("ones32", [1, 128], F32, o); o += 512
        self.sgub = self.sb("sgub", [1, 1024], F32, o); o += 4096
        self.swT16 = self.sb("swT16", [128, 8, 128], BF16, o); o += 2048
        self.gvb = self.sb("gvb", [128, 1024], F32, o); o += 4096
        self.Ctab = self.sb("Ctab", [64, TH], F32, o); o += 4096
        self.Stab = self.sb("Stab", [64, TH], F32, o); o += 4096
        self.krT = self.sb("krT", [64, SEQ], BF16, o); o += 4096
        self.carry = self.sb("carry", [128, 16, 2], F32, o); o += 128
        self.zb = self.sb("zb", [128, 1026], F32, o); o += 4128
        self.st = self.sb("st", [128, 2, 32], F32, o); o += 256
        self.rstd = self.sb("rstd", [128, TH], F32, o); o += 4096
        self.sq = [self.sb(f"sq{i}", [128, TH], BF16, o + 2048 * i) for i in range(2)]; SQo = o; o += 4096
        self.T0 = self.sb("T0", [128, TH], F32, o); T0o = o; o += 4096
        self.T1 = self.sb("T1", [128, TH], F32, o); T1o = o; o += 4096
        self.PT = [self.sb(f"PT{i}", [128, 512], BF16, T1o + 1024 * i) for i in range(4)]
        self.ws = WStream(k, nc, self.wd, SBUF_BASE + o, NT); o += RING * 4096
        XO = o; o += 65536
        HO = o; o += 32768
        BO = o; o += 32768
        assert SBUF_BASE + o <= SBUF_TOP, o
        self.xs = self.sb("xs", [128, 16, TH], F32, XO)
        self.posi = self.sb("posi", [64, TH], I32, T0o)
        self.ang = self.T1[0:64, :]
        self.tt = self.sb("tt", [64, TH], F32, SQo)
        self.rr = self.zb[0:64, 2:1026]
        self.ob = self.sb("ob", [128, 16, TH], F32, HO)
        self.cq32 = self.sb("cq32", [128, 4, TH], F32, XO)
        self.cqn = self.sb("cqn", [128, 4, TH], BF16, XO + 16384)
        self.ckvn = self.sb("ckvn", [128, 4, TH], BF16, XO + 24576)
        self.vn = self.sb("vn", [128, 8, 1024], BF16, XO + 32768)
        self.ug = [self.sb(f"ug{i}", [128, TH], F32, XO + 49152 + 4096 * i) for i in range(2)]
        self.kh = [self.sb(f"kh{i}", [128, SEQ], BF16, XO + 4096 * i) for i in range(2)]
        self.qn = [self.sb(f"qn{i}", [128, TH], BF16, XO + 8192 + 4096 * i) for i in range(2)]
        self.qr = [self.sb(f"qr{i}", [64, TH], BF16, XO + 8192 + 4096 * i + 2048) for i in range(2)]
        self.a32 = self.sb("a32", [128, 8, TH], F32, XO + 32768)
        self.hT = self.sb("hT", [128, 16, TH], BF16, HO)
        self.RB = self.sb("RB", [128, 16, TH], BF16, BO)
        self.s32hi = self.sb("s32hi", [128, 4, TH], F32, BO)
        B = Buf
        self.b_cst, self.b_ones, self.b_tri, self.b_sgub, self.b_sw, self.b_gv = (
            B("cst"), B("ones"), B("tri"), B("sgub"), B("sw"), B("gv"))
        self.b_C, self.b_S, self.b_kr, self.b_carry, self.b_zb = B("C"), B("S"), B("kr"), B("carry"), B("zb")
        self.b_st = [B("st0"), B("st1")]
        self.b_rstd = B("rstd")
        self.b_sq = [B("sq0"), B("sq1")]
        self.b_T0 = [B("T0a"), B("T0b")]
        self.b_T1 = [B("T1a"), B("T1b")]
        self.b_PT = [B(f"PT{i}") for i in range(4)]
        self.b_xs = [B(f"xs{i}") for i in range(16)]
        self.b_xsio = [B(f"xsio{i}") for i in range(4)]
        self.b_cq = self.b_xs[0:4]
        self.b_cqn, self.b_ckvn = self.b_xs[4], self.b_xs[6]
        self.b_cqn2, self.b_ckvn2 = self.b_xs[5], self.b_xs[7]
        self.b_vn = [self.b_xs[8 + i // 2] for i in range(8)]
        self.b_ug = [self.b_xs[12], self.b_xs[13]]
        self.b_kh = self.b_xs[0:2]
        self.b_qn = self.b_xs[2:4]
        self.b_qr = self.b_xs[2:4]
        self.b_a32 = self.b_xs[8:16]
        self.b_h = [B(f"h{i}") for i in range(16)]
        self.b_rb = [B(f"rb{i}") for i in range(16)]
        self.b_vd, self.b_kd, self.b_out, self.b_dbg = B("vd"), B("kd"), B("out"), B("dbg")
        self.b_vio = B("vio")
        self.b_oio = [B(f"oio{i}") for i in range(4)]
        self.b_ob = [[self.b_h[2 * c], self.b_h[2 * c + 1]] for c in range(8)] + \
                    [[self.b_rb[2 * c], self.b_rb[2 * c + 1]] for c in range(8)]

        self.init_consts()
        self.load_xs(0)
        for b in self.b_xsio:
            k._wait("pool", (b.dsem, b.dcnt))
        self.ws.pump()
        for h in range(2):
            self.half(h)
        assert self.ws.consumed == 2 * NT and len(self.ws.specs) == NT, (self.ws.consumed, len(self.ws.specs))
        k.final()

    def init_consts(self):
        k = self.k
        k.dma("sp", [(self.cst[:], self.cstd)], self.b_cst, writes=[self.b_cst])
        k.dma("sp", [(self.T1[:, 0:128], self.trid)], self.b_T1[0], writes=[self.b_T1[0]])
        k.dma("sp", [(self.T0[:], self.swd)], self.b_T0[0], writes=self.b_T0)
        k.dma("sp", [(self.sgub[:], self.sbd)], self.b_sgub, writes=[self.b_sgub])
        k.dma("sp", [(self.gvb[:], self.gvd.partition_broadcast(128))], self.b_gv, writes=[self.b_gv])
        k.op("dve", lambda e: e.memset(self.ones16[:], 1.0), writes=[self.b_ones])
        k.op("dve", lambda e: e.memset(self.ones32[:], 1.0), writes=[self.b_ones])
        k.op("dve", lambda e: e.tensor_copy(self.tri16[:], self.T1[:, 0:128]), reads=[self.b_T1[0]], writes=[self.b_tri])
        for g in range(8):
            k.op("dve", lambda e, g=g: e.tensor_tensor(self.swT16[:, g, :], self.T0[:, g * 128:(g + 1) * 128],
                                                      self.T1[:, 0:128], ALU.mult),
                 reads=self.b_T0 + [self.b_T1[0]], writes=[self.b_sw])
        k.op("dve", lambda e: e.memset(self.zb[:, 0:2], 0.0), writes=[self.b_zb])
        k.barrier()

    def gcol(self, base, i):
        return self.cst[:, base + i:base + i + 1]

    def rms_rstd(self, chunks, nfeat, banks=(6, 7)):
        k = self.k
        n = len(chunks)
        for c, (ap, bufs) in enumerate(chunks):
            sq, sqb = self.sq[c % 2], self.b_sq[c % 2]
            k.op("act", lambda e, ap=ap, sq=sq: e.activation(sq[:], ap, AF.Square), reads=bufs, writes=[sqb])
            for tg in range(2):
                pt, pb = self.ps[banks[tg]], self.psb[banks[tg]]
                k.mm([lambda e, pt=pt, sq=sq, tg=tg, c=c: e.matmul(pt[:], self.ones16[:], sq[:, tg * 512:(tg + 1) * 512],
                                                                 start=(c == 0), stop=(c == n - 1))],
                     reads=[sqb, self.b_ones], writes=[pb])
        for tg in range(2):
            pt, pb = self.ps[banks[tg]], self.psb[banks[tg]]
            k.op("act", lambda e, pt=pt, tg=tg: e.activation(self.rstd[:, tg * 512:(tg + 1) * 512], pt[:], AF.Sqrt,
                                                            bias=EPS, scale=1.0 / nfeat),
                 reads=[pb], writes=[self.b_rstd])
        k.op("dve", lambda e: e.reciprocal(self.rstd[:], self.rstd[:]), reads=[self.b_rstd], writes=[self.b_rstd])

    def norm_apply(self, src, srcbufs, gbase, i, dst, dstbufs):
        self.k.op("dve", lambda e: e.scalar_tensor_tensor(dst, src, self.gcol(gbase, i), self.rstd[:], ALU.mult, ALU.mult),
                  reads=list(srcbufs) + [self.b_rstd, self.b_cst], writes=list(dstbufs))

    def rope_evac(self, pa, pab, pbk, pbb, tg, out_ap, outbufs):
        k = self.k
        ta, tb = self.T0[0:64, 0:512], self.T0[0:64, 512:1024]
        k.op("dve", lambda e: e.tensor_tensor(ta, pa[0:64, :], self.Ctab[:, tg * 512:(tg + 1) * 512], ALU.mult),
             reads=[pab, self.b_C], writes=[self.b_T0[0]])
        k.op("dve", lambda e: e.tensor_tensor(tb, pbk[0:64, :], self.Stab[:, tg * 512:(tg + 1) * 512], ALU.mult),
             reads=[pbb, self.b_S], writes=[self.b_T0[1]])
        k.op("dve", lambda e: e.tensor_tensor(out_ap, ta, tb, ALU.add),
             reads=self.b_T0, writes=outbufs)

    def tap(self, name, chunks):
        if self.dbg != name or getattr(self, "_tapped", False):
            return
        self._tapped = True
        k = self.k
        k.barrier()
        for c, (ap, bufs) in enumerate(chunks):
            p = ap.shape[0]
            n = ap.shape[-1]
            k.op("act", lambda e, ap=ap, p=p, n=n: e.activation(self.T0[0:p, 0:n], ap, AF.Copy),
                 reads=bufs, writes=self.b_T0)
            k.dma("sp", [(self.dbgd[c, 0:p, 0:n], self.T0[0:p, 0:n])], self.b_T0[0], reads=self.b_T0, writes=[self.b_dbg])
        k.barrier()

    def half(self, h):
        k = self.k
        t0 = h * TH
        if h == 0:
            self.tables(h)
        self.rms_rstd([(self.xs[:, c, :], [self.b_xs[c]]) for c in range(16)], D)
        for c in range(16):
            self.norm_apply(self.xs[:, c, :], [self.b_xs[c]], 0, c, self.hT[:, c, :], [self.b_h[c]])
        self.tap("h0", [(self.hT[:, c, :], [self.b_h[c]]) for c in range(16)])
        self.l0_inproj(h)
        k.handoff(self.b_T1, self.b_PT)
        self.l0_attn(h)
        k.handoff(self.b_PT, self.b_T1)
        self.tap("mix", [(self.RB[:, c, :], [self.b_rb[c]]) for c in range(16)])
        self.out_proj("e_w_out", 0, xh=h, next_g=16)
        self.tap("x1", [(self.xs[:, c, :], [self.b_xs[c]]) for c in range(16)])
        self.mlp(0, next_g=32)
        self.tap("x2", [(self.xs[:, c, :], [self.b_xs[c]]) for c in range(16)])
        self.l1_conv(h)
        self.out_proj("o_w_out", 0, next_g=48)
        self.tap("x3", [(self.xs[:, c, :], [self.b_xs[c]]) for c in range(16)])
        self.mlp(1)
        for c in range(16):
            self.norm_apply(self.xs[:, c, :], [self.b_xs[c]], 64, c, self.ob[:, c, :], self.b_ob[c])
        if h == 0:
            self.load_xs(1)
            self.tables(1)
        for b in range(4):
            k.dma("sp", [(self.outT[cc, :, t0:t0 + TH], self.ob[:, cc, :]) for cc in range(4 * b, 4 * b + 4)],
                  self.b_oio[b], reads=[x for cc in range(4 * b, 4 * b + 4) for x in self.b_ob[cc]], writes=[self.b_out])

    def load_xs(self, h):
        t0 = h * TH
        for b in range(4):
            self.k.dma("sp", [(self.xs[:, c, :], self.xT[c, :, t0:t0 + TH]) for c in range(4 * b, 4 * b + 4)],
                       self.b_xsio[b], writes=self.b_xs[4 * b:4 * b + 4])

    def tables(self, h):
        k = self.k
        t0 = h * TH
        TB = self.b_T0 + self.b_T1 + self.b_sq + [self.b_zb]
        k.dma("sp", [(self.posi[:], self.posd[:, t0:t0 + TH].partition_broadcast(64))], self.b_T0[0], writes=self.b_T0)
        k.op("dve", lambda e: e.tensor_copy(self.ang[:], self.posi[:]), reads=TB, writes=TB)
        k.op("dve", lambda e: e.tensor_scalar(self.ang[:], self.ang[:], self.cst[0:64, 152:153], None, ALU.mult),
             reads=TB + [self.b_cst], writes=TB)
        for which in range(2):
            if which == 1:
                k.op("dve", lambda e: e.tensor_scalar(self.ang[:], self.ang[:], math.pi / 2, None, ALU.add),
                     reads=TB, writes=TB)
            k.op("dve", lambda e: e.tensor_scalar(self.tt[:], self.ang[:], 1.0 / (2 * math.pi), MAGIC, ALU.mult, ALU.add),
                 reads=TB, writes=TB)
            k.op("dve", lambda e: e.tensor_scalar(self.tt[:], self.tt[:], MAGIC, None, ALU.subtract), reads=TB, writes=TB)
            k.op("dve", lambda e: e.scalar_tensor_tensor(self.rr[:], self.tt[:], -C1, self.ang[:], ALU.mult, ALU.add),
                 reads=TB, writes=TB)
            k.op("dve", lambda e: e.scalar_tensor_tensor(self.rr[:], self.tt[:], -C2, self.rr[:], ALU.mult, ALU.add),
                 reads=TB, writes=TB)
            k.op("dve", lambda e: e.tensor_scalar(self.rr[:], self.rr[:], -math.pi, math.pi, ALU.max, ALU.min),
                 reads=TB, writes=TB)
            if which == 0:
                k.op("act", lambda e: e.activation(self.Stab[:], self.rr[:], AF.Sin), reads=TB, writes=[self.b_S])
                k.op("dve", lambda e: e.tensor_scalar(self.Stab[:], self.Stab[:], self.cst[0:64, 153:154], None, ALU.mult),
                     reads=[self.b_S, self.b_cst], writes=[self.b_S])
            else:
                k.op("act", lambda e: e.activation(self.Ctab[:], self.rr[:], AF.Sin), reads=TB, writes=[self.b_C])

    def l0_inproj(self, h):
        k = self.k
        t0 = h * TH
        rot = [0, 1, 2, 3, 4, 5]
        ar = np.arange
        for which, cbase, gbase, dst, dbase in ((0, 0, 80, self.cqn, 4), (1, 512, 84, self.ckvn, 6)):
            for cc in range(4):
                spec = [("e_w_in", 0, kc * 128, cbase + cc * 128 + ar(128)) for kc in range(16)]
                sl, slb = self.ws.next(spec)
                for tg in range(2):
                    pt, pb = self.bank(rot)
                    k.mm([lambda e, kc=kc, pt=pt, sl=sl, tg=tg: e.matmul(pt[:], sl[:, kc * 128:(kc + 1) * 128],
                                                                        self.hT[:, kc, tg * 512:(tg + 1) * 512],
                                                                        start=(kc == 0), stop=(kc == 15)) for kc in range(16)],
                         reads=[slb], per=[[self.b_h[kc]] for kc in range(16)], writes=[pb])
                    k.op("act", lambda e, pt=pt, cc=cc, tg=tg: e.activation(self.cq32[:, cc, tg * 512:(tg + 1) * 512], pt[:], AF.Copy),
                         reads=[pb], writes=[self.b_cq[cc]])
                self.ws.release()
            self.rms_rstd([(self.cq32[:, cc, :], [self.b_cq[cc]]) for cc in range(4)], 512)
            for cc in range(4):
                self.norm_apply(self.cq32[:, cc, :], [self.b_cq[cc]], gbase, cc, dst[:, cc, :], [self.b_xs[dbase + cc // 2]])
        sw = np.concatenate([ar(32, 64), ar(0, 32)])
        spec = [("e_w_in", 0, kc * 128, 1024 + ar(64)) for kc in range(16)] + \
               [("e_w_in", 0, kc * 128, 1024 + sw) for kc in range(16)]
        sl, slb = self.ws.next(spec)
        for tg in range(2):
            pa, pab = self.bank(rot)
            pbk, pbb = self.bank(rot)
            for pp, ppb, base in ((pa, pab, 0), (pbk, pbb, 1024)):
                k.mm([lambda e, kc=kc, pp=pp, base=base, tg=tg: e.matmul(pp[0:64, :], sl[:, base + kc * 64:base + (kc + 1) * 64],
                                                                        self.hT[:, kc, tg * 512:(tg + 1) * 512],
                                                                        start=(kc == 0), stop=(kc == 15)) for kc in range(16)],
                     reads=[slb], per=[[self.b_h[kc]] for kc in range(16)], writes=[ppb])
            self.rope_evac(pa, pab, pbk, pbb, tg, self.krT[:, t0 + tg * 512:t0 + (tg + 1) * 512], [self.b_kr])
        self.ws.release()
        it = 0
        for cg in range(2):
            sls = []
            for kq in range(4):
                spec = [("e_w_in", 0, (kq * 4 + i) * 128, 2112 + cg * 512 + ar(512)) for i in range(4)]
                sls.append(self.ws.next(spec))
            for tc in range(8):
                pt, pb = self.bank(rot)
                k.mm([lambda e, kc=kc, pt=pt, tc=tc: e.matmul(pt[:], self.hT[:, kc, tc * 128:(tc + 1) * 128],
                                                             sls[kc // 4][0][:, (kc % 4) * 512:(kc % 4 + 1) * 512],
                                                             start=(kc == 0), stop=(kc == 15)) for kc in range(16)],
                     reads=[s[1] for s in sls], per=[[self.b_h[kc]] for kc in range(16)], writes=[pb])
                i2 = it % 2
                it += 1
                ta, tab = self.T0[:, i2 * 512:(i2 + 1) * 512], self.b_T0[i2]
                tb, tbb = self.T1[:, i2 * 512:(i2 + 1) * 512], self.b_T1[i2]
                st, stb = self.st[:, i2, :], self.b_st[i2]
                k.op("act", lambda e, ta=ta, pt=pt: e.activation(ta, pt[:], AF.Gelu_apprx_tanh), reads=[pb], writes=[tab])
                k.op("act", lambda e, ta=ta, tb=tb: e.activation(tb, ta, AF.Square), reads=[tab], writes=[tbb])
                k.op("dve", lambda e, ta=ta, st=st: e.tensor_reduce(st[:, 0:4], ta.rearrange("p (g d) -> p g d", g=4), AX.X, ALU.add),
                     reads=[tab], writes=[stb])
                k.op("dve", lambda e, tb=tb, st=st: e.tensor_reduce(st[:, 4:8], tb.rearrange("p (g d) -> p g d", g=4), AX.X, ALU.add),
                     reads=[tbb], writes=[stb])
                k.op("dve", lambda e, st=st: e.tensor_scalar(st[:, 8:12], st[:, 0:4], 1.0 / 128, None, ALU.mult), reads=[stb], writes=[stb])
                k.op("dve", lambda e, st=st: e.tensor_tensor(st[:, 12:16], st[:, 8:12], st[:, 8:12], ALU.mult), reads=[stb], writes=[stb])
                k.op("dve", lambda e, st=st: e.scalar_tensor_tensor(st[:, 16:20], st[:, 4:8], 1.0 / 128, st[:, 12:16], ALU.mult, ALU.subtract),
                     reads=[stb], writes=[stb])
                k.op("act", lambda e, st=st: e.activation(st[:, 20:24], st[:, 16:20], AF.Sqrt, bias=EPS, scale=1.0), reads=[stb], writes=[stb])
                k.op("dve", lambda e, st=st: e.reciprocal(st[:, 24:28], st[:, 20:24]), reads=[stb], writes=[stb])
                for g in range(4):
                    k.op("dve", lambda e, g=g, ta=ta, st=st: e.tensor_scalar(ta[:, g * 128:(g + 1) * 128], ta[:, g * 128:(g + 1) * 128],
                                                                             st[:, 8 + g:9 + g], st[:, 24 + g:25 + g], ALU.subtract, ALU.mult),
                         reads=[tab, stb], writes=[tab])
                k.op("dve", lambda e, ta=ta, tc=tc, cg=cg: e.tensor_tensor(self.vn[:, tc, cg * 512:(cg + 1) * 512], ta,
                                                                           self.gvb[:, cg * 512:(cg + 1) * 512], ALU.mult),
                     reads=[tab, self.b_gv], writes=[self.b_vn[tc]])
            self.ws.release(4)
        self.tap("vn", [(self.vn[:, c, :], [self.b_vn[c]]) for c in range(8)])
        yrot = [6, 7]
        for g in range(8):
            spec = [("e_w_in", 0, kc * 128, 1088 + g * 128 + ar(128)) for kc in range(16)]
            sl, slb = self.ws.next(spec)
            ug, ugb = self.ug[g % 2], self.b_ug[g % 2]
            for tg in range(2):
                pt, pb = self.bank(rot)
                k.mm([lambda e, kc=kc, pt=pt, tg=tg: e.matmul(pt[:], sl[:, kc * 128:(kc + 1) * 128],
                                                             self.hT[:, kc, tg * 512:(tg + 1) * 512],
                                                             start=(kc == 0), stop=(kc == 15)) for kc in range(16)],
                     reads=[slb], per=[[self.b_h[kc]] for kc in range(16)], writes=[pb])
                k.op("act", lambda e, pt=pt, ug=ug, tg=tg: e.activation(ug[:, tg * 512:(tg + 1) * 512], pt[:], AF.Gelu_apprx_tanh),
                     reads=[pb], writes=[ugb])
            self.ws.release()
            if g < 4:
                s32, s32b = self.cq32[:, g, :], [self.b_cq[g]]
            else:
                s32, s32b = self.s32hi[:, g - 4, :], [self.b_rb[2 * (g - 4)], self.b_rb[2 * (g - 4) + 1]]
            for tq in range(2):
                pt, pb = self.ps[yrot[tq]], self.psb[yrot[tq]]
                fns = []
                for c4 in range(4):
                    c = tq * 4 + c4
                    fns.append(lambda e, pt=pt, c=c, c4=c4, g=g: e.matmul(pt[:, c4 * 128:(c4 + 1) * 128],
                                                                        self.vn[:, c, g * 128:(g + 1) * 128], self.swT16[:, g, :],
                                                                        start=True, stop=False))
                    fns.append(lambda e, pt=pt, c4=c4, g=g: e.matmul(pt[:, c4 * 128:(c4 + 1) * 128],
                                                                   self.ones32[0:1, :], self.sgub[0:1, g * 128:(g + 1) * 128],
                                                                   start=False, stop=True))
                k.mm(fns, reads=self.b_vn + [self.b_sw, self.b_ones, self.b_sgub], writes=[pb])
                k.op("dve", lambda e, pt=pt, tq=tq, s32=s32, ug=ug: e.tensor_tensor(s32[:, tq * 512:(tq + 1) * 512], pt[:],
                                                                                    ug[:, tq * 512:(tq + 1) * 512], ALU.mult),
                     reads=[pb, ugb], writes=s32b)
        chunks = [(self.cq32[:, g, :], [self.b_cq[g]]) for g in range(4)] + \
                 [(self.s32hi[:, g, :], [self.b_rb[2 * g], self.b_rb[2 * g + 1]]) for g in range(4)]
        self.tap("s32", chunks)
        self.rms_rstd(chunks, 1024, banks=(4, 5))
        for g in range(8):
            self.norm_apply(chunks[g][0], chunks[g][1], 96, g, self.RB[:, 8 + g, :], [self.b_rb[8 + g]])

    def l0_attn(self, h):
        k = self.k
        t0 = h * TH
        ar = np.arange
        rot = [4, 5, 6, 7]
        Vall, vb = self.hT, self.b_h
        if h == 1:
            k.dma("sp", [(Vall[:, 0:8, :], self.vd)], self.b_vio, reads=[self.b_vd], writes=vb[0:8])
        for hg in range(2):
            cols = np.concatenate([hd * 256 + 128 + ar(128) for hd in range(hg * 4, hg * 4 + 4)])
            spec = [("e_w_ukv", 0, kc * 128, cols) for kc in range(4)]
            sl, slb = self.ws.next(spec)
            for tc in range(8):
                pt, pb = self.bank(rot)
                k.mm([lambda e, kc=kc, pt=pt, tc=tc: e.matmul(pt[:], self.ckvn[:, kc, tc * 128:(tc + 1) * 128],
                                                             sl[:, kc * 512:(kc + 1) * 512], start=(kc == 0), stop=(kc == 3))
                      for kc in range(4)], reads=[slb, self.b_xs[6], self.b_xs[7]], writes=[pb])
                k.op("act", lambda e, pt=pt, tc=tc, hg=hg: e.activation(Vall[:, h * 8 + tc, hg * 512:(hg + 1) * 512], pt[:], AF.Copy),
                     reads=[pb], writes=[vb[h * 8 + tc]])
            self.ws.release()
        if h == 0:
            k.dma("sp", [(self.vd, Vall[:, 0:8, :])], self.b_vio, reads=vb[0:8], writes=[self.b_vd])
        sw = np.concatenate([ar(32, 64), ar(0, 32)])
        orot = [0, 1]
        srot = [2, 3]
        pti = 0
        for hd in range(8):
            i2 = hd % 2
            kh, khb = self.kh[i2], self.b_kh[i2]
            qn, qnb = self.qn[i2], self.b_qn[i2]
            qr, qrb = self.qr[i2], self.b_qr[i2]
            if h == 1:
                k.dma("sp", [(kh[:, 0:TH], self.kd[:, hd, :])], khb, reads=[self.b_kd], writes=[khb])
            spec = [("e_w_uq", 0, kc * 128, hd * 192 + ar(128)) for kc in range(4)] + \
                   [("e_w_uq", 0, kc * 128, hd * 192 + 128 + ar(64)) for kc in range(4)] + \
                   [("e_w_uq", 0, kc * 128, hd * 192 + 128 + sw) for kc in range(4)] + \
                   [("e_w_ukv", 0, kc * 128, hd * 256 + ar(128)) for kc in range(4)]
            sl, slb = self.ws.next(spec)
            for tg in range(2):
                pt, pb = self.bank(rot)
                k.mm([lambda e, kc=kc, pt=pt, tg=tg: e.matmul(pt[:], sl[:, kc * 128:(kc + 1) * 128],
                                                             self.cqn[:, kc, tg * 512:(tg + 1) * 512], start=(kc == 0), stop=(kc == 3))
                      for kc in range(4)], reads=[slb, self.b_xs[4], self.b_xs[5]], writes=[pb])
                k.op("act", lambda e, pt=pt, tg=tg, qn=qn: e.activation(qn[:, tg * 512:(tg + 1) * 512], pt[:], AF.Copy),
                     reads=[pb], writes=[qnb])
                pa, pab = self.bank(rot)
                pbk, pbb = self.bank(rot)
                for pp, ppb, base in ((pa, pab, 512), (pbk, pbb, 768)):
                    k.mm([lambda e, kc=kc, pp=pp, base=base, tg=tg: e.matmul(pp[0:64, :], sl[:, base + kc * 64:base + (kc + 1) * 64],
                                                                            self.cqn[:, kc, tg * 512:(tg + 1) * 512],
                                                                            start=(kc == 0), stop=(kc == 3)) for kc in range(4)],
                         reads=[slb, self.b_xs[4], self.b_xs[5]], writes=[ppb])
                self.rope_evac(pa, pab, pbk, pbb, tg, qr[:, tg * 512:(tg + 1) * 512], [qrb])
                pt, pb = self.bank(rot)
                k.mm([lambda e, kc=kc, pt=pt, tg=tg: e.matmul(pt[:], sl[:, 1024 + kc * 128:1024 + (kc + 1) * 128],
                                                             self.ckvn[:, kc, tg * 512:(tg + 1) * 512], start=(kc == 0), stop=(kc == 3))
                      for kc in range(4)], reads=[slb, self.b_xs[6], self.b_xs[7]], writes=[pb])
                k.op("act", lambda e, pt=pt, tg=tg, kh=kh: e.activation(kh[:, t0 + tg * 512:t0 + (tg + 1) * 512], pt[:], AF.Copy),
                     reads=[pb], writes=[khb])
            self.ws.release()
            if h == 0:
                k.dma("sp", [(self.kd[:, hd, :], kh[:, 0:TH])], khb, reads=[khb], writes=[self.b_kd])
            for tg in range(2):
                Q0 = t0 + tg * 512
                nkb = (Q0 + 512) // 128
                po, pob = self.bank(orot)
                psm, psmb = self.bank(srot)

                def c0_of(j):
                    r = j - Q0 // 128
                    return 128 * r if r > 0 else 0

                def score(j):
                    pt, pb = self.bank(rot)
                    c0 = c0_of(j)
                    k.mm([lambda e: e.matmul(pt[:, c0:512], kh[:, j * 128:(j + 1) * 128], qn[:, tg * 512 + c0:(tg + 1) * 512],
                                             start=True, stop=False),
                          lambda e: e.matmul(pt[:, c0:512], self.krT[:, j * 128:(j + 1) * 128], qr[:, tg * 512 + c0:(tg + 1) * 512],
                                             start=False, stop=True)],
                         reads=[khb, qnb, qrb, self.b_kr], writes=[pb])
                    return pt, pb

                pend_s = [score(jj) for jj in range(min(2, nkb))]
                for j in range(nkb):
                    pt, pb = pend_s.pop(0)
                    if j + 2 < nkb:
                        pend_s.append(score(j + 2))
                    c0 = c0_of(j)
                    r = j - Q0 // 128
                    P, Pb = self.PT[pti % 4], self.b_PT[pti % 4]
                    pti += 1
                    k.op("act", lambda e, pt=pt, P=P, c0=c0: e.activation(P[:, c0:512], pt[:, c0:512], AF.Exp, scale=SCALE),
                         reads=[pb], writes=[Pb])
                    if r >= 0:
                        k.op("dve", lambda e, P=P, c0=c0: e.tensor_tensor(P[:, c0:c0 + 128], P[:, c0:c0 + 128], self.tri16[:], ALU.mult),
                             reads=[Pb, self.b_tri], writes=[Pb])
                    k.mm([lambda e, P=P, c0=c0, j=j: e.matmul(po[:, c0:512], Vall[:, j, hd * 128:(hd + 1) * 128], P[:, c0:512],
                                                             start=(j == 0), stop=(j == nkb - 1))],
                         reads=[Pb, vb[j]], writes=[pob])
                    k.mm([lambda e, P=P, c0=c0, j=j: e.matmul(psm[:, c0:512], self.ones16[:], P[:, c0:512],
                                                             start=(j == 0), stop=(j == nkb - 1))],
                         reads=[Pb, self.b_ones], writes=[psmb])
                rs, rsb = self.T0[:, tg * 512:(tg + 1) * 512], self.b_T0[tg]
                k.op("dve", lambda e, rs=rs, psm=psm: e.reciprocal(rs, psm[:]), reads=[psmb], writes=[rsb])
                k.op("dve", lambda e, rs=rs, po=po, tg=tg, hd=hd: e.tensor_tensor(self.a32[:, hd, tg * 512:(tg + 1) * 512], po[:], rs, ALU.mult),
                     reads=[pob, rsb], writes=[self.b_a32[hd]])
        self.tap("a32", [(self.a32[:, c, :], [self.b_a32[c]]) for c in range(8)])
        self.rms_rstd([(self.a32[:, c, :], [self.b_a32[c]]) for c in range(8)], 1024, banks=(4, 5))
        for hd in range(8):
            self.norm_apply(self.a32[:, hd, :], [self.b_a32[hd]], 88, hd, self.RB[:, hd, :], [self.b_rb[hd]])

    def rms_mm(self, c, n, banks=(6, 7)):
        k = self.k
        sq, sqb = self.sq[c % 2], self.b_sq[c % 2]
        for tg in range(2):
            pt, pb = self.ps[banks[tg]], self.psb[banks[tg]]
            k.mm([lambda e: e.matmul(pt[:], self.ones16[:], sq[:, tg * 512:(tg + 1) * 512], start=(c == 0), stop=(c == n - 1))],
                 reads=[sqb, self.b_ones], writes=[pb])

    def rms_finish(self, nfeat, banks=(6, 7)):
        k = self.k
        for tg in range(2):
            pt, pb = self.ps[banks[tg]], self.psb[banks[tg]]
            k.op("act", lambda e: e.activation(self.rstd[:, tg * 512:(tg + 1) * 512], pt[:], AF.Sqrt, bias=EPS, scale=1.0 / nfeat),
                 reads=[pb], writes=[self.b_rstd])
        k.op("dve", lambda e: e.reciprocal(self.rstd[:], self.rstd[:]), reads=[self.b_rstd], writes=[self.b_rstd])

    def out_proj(self, wname, layer, xh=None, next_g=None):
        k = self.k
        prot = [0, 1, 2]
        ar = np.arange
        stg = [(self.T0, self.b_T0), (self.T1, self.b_T1)]

        def ld(j):
            T, tb = stg[j % 2]
            k.dma("sp", [(T[:], self.xT[j, :, xh * TH:(xh + 1) * TH])], tb[0], writes=tb)

        if xh is not None:
            ld(0)
        for j in range(16):
            if xh is not None and j + 1 < 16:
                ld(j + 1)
            spec = [(wname, layer, kc * 128, j * 128 + ar(128)) for kc in range(16)]
            sl, slb = self.ws.next(spec)
            p = prot[0]
            prot.append(prot.pop(0))
            for tg in range(2):
                pt, pb = self.ps[2 * p + tg], self.psb[2 * p + tg]
                k.mm([lambda e, kc=kc, pt=pt, tg=tg: e.matmul(pt[:], sl[:, kc * 128:(kc + 1) * 128],
                                                             self.RB[:, kc, tg * 512:(tg + 1) * 512], start=(kc == 0), stop=(kc == 15))
                      for kc in range(16)], reads=[slb], per=[[self.b_rb[kc]] for kc in range(16)], writes=[pb])
            pbs = [self.psb[2 * p], self.psb[2 * p + 1]]
            if xh is None:
                k.op("dve", lambda e, j=j, p=p: e.tensor_tensor(self.xs[:, j, :], self.xs[:, j, :], self.pp[p][:], ALU.add),
                     reads=pbs + [self.b_xs[j]], writes=[self.b_xs[j]])
            else:
                T, tb = stg[j % 2]
                k.op("dve", lambda e, j=j, p=p, T=T: e.tensor_tensor(self.xs[:, j, :], T[:], self.pp[p][:], ALU.add),
                     reads=pbs + tb, writes=[self.b_xs[j]])
            self.ws.release()
            if next_g is not None:
                if j >= 2:
                    self.rms_mm(j - 2, 16)
                k.op("act", lambda e, j=j: e.activation(self.hT[:, j, :], self.xs[:, j, :], AF.Copy, scale=self.gcol(next_g, j)),
                     reads=[self.b_xs[j], self.b_cst], writes=[self.b_h[j]])
                sq, sqb = self.sq[j % 2], self.b_sq[j % 2]
                k.op("act", lambda e, j=j, sq=sq: e.activation(sq[:], self.xs[:, j, :], AF.Square), reads=[self.b_xs[j]], writes=[sqb])
        if next_g is not None:
            self.rms_mm(14, 16)
            self.rms_mm(15, 16)
            self.rms_finish(D)

    def mlp(self, layer, next_g=None):
        k = self.k
        ar = np.arange
        prot = [0, 1, 2, 3]

        def pair():
            p = prot[0]
            prot.append(prot.pop(0))
            return p

        tmps = [(self.T0, self.b_T0), (self.T1, self.b_T1)]
        ti = [0]

        def m1(fb):
            ab = (fb % 4) * 4
            for fcl in range(4):
                spec = [("mlp_w1", layer, kc * 128, fb * 512 + fcl * 128 + ar(128)) for kc in range(16)]
                sl, slb = self.ws.next(spec)
                p = pair()
                for tg in range(2):
                    pt, pb = self.ps[2 * p + tg], self.psb[2 * p + tg]
                    k.mm([lambda e, kc=kc, pt=pt, tg=tg: e.matmul(pt[:], sl[:, kc * 128:(kc + 1) * 128],
                                                                 self.hT[:, kc, tg * 512:(tg + 1) * 512], start=(kc == 0), stop=(kc == 15))
                          for kc in range(16)], reads=[slb], per=[[self.b_h[kc]] for kc in range(16)], writes=[pb])
                self.ws.release()
                pbs = [self.psb[2 * p], self.psb[2 * p + 1]]
                tm, tmb = tmps[ti[0] % 2]
                ti[0] += 1
                c = ab + fcl
                k.op("act", lambda e, tm=tm, p=p: e.activation(tm[:], self.pp[p][:], AF.Relu), reads=pbs, writes=tmb)
                k.op("dve", lambda e, tm=tm: e.tensor_tensor(tm[:], tm[:], self.rstd[:], ALU.mult),
                     reads=tmb + [self.b_rstd], writes=tmb)
                k.op("act", lambda e, tm=tm, c=c: e.activation(self.RB[:, c, :], tm[:], AF.Square),
                     reads=tmb, writes=[self.b_rb[c]])

        def m2(fb):
            ab = (fb % 4) * 4
            last = (fb == 15)
            if last and 3 in prot:
                prot.remove(3)
            for dg in range(4):
                spec = [("mlp_w2", layer, fb * 512 + fcl * 128, dg * 512 + ar(512)) for fcl in range(4)]
                sl, slb = self.ws.next(spec)
                for dcl in range(4):
                    j = dg * 4 + dcl
                    p = pair()
                    for tg in range(2):
                        pt, pb = self.ps[2 * p + tg], self.psb[2 * p + tg]
                        k.mm([lambda e, fcl=fcl, pt=pt, tg=tg, dcl=dcl: e.matmul(pt[:], sl[:, fcl * 512 + dcl * 128:fcl * 512 + (dcl + 1) * 128],
                                                                                self.RB[:, ab + fcl, tg * 512:(tg + 1) * 512],
                                                                                start=(fcl == 0), stop=(fcl == 3)) for fcl in range(4)],
                             reads=[slb], per=[[self.b_rb[ab + f]] for f in range(4)], writes=[pb])
                    k.op("dve", lambda e, j=j, p=p: e.tensor_tensor(self.xs[:, j, :], self.xs[:, j, :], self.pp[p][:], ALU.add),
                         reads=[self.psb[2 * p], self.psb[2 * p + 1], self.b_xs[j]], writes=[self.b_xs[j]])
                    if last:
                        if j >= 2:
                            self.rms_mm(j - 2, 16)
                        if next_g is not None:
                            k.op("act", lambda e, j=j: e.activation(self.hT[:, j, :], self.xs[:, j, :], AF.Copy, scale=self.gcol(next_g, j)),
                                 reads=[self.b_xs[j], self.b_cst], writes=[self.b_h[j]])
                        sq, sqb = self.sq[j % 2], self.b_sq[j % 2]
                        k.op("act", lambda e, j=j, sq=sq: e.activation(sq[:], self.xs[:, j, :], AF.Square), reads=[self.b_xs[j]], writes=[sqb])
                self.ws.release()
            if last:
                self.rms_mm(14, 16)
                self.rms_mm(15, 16)
                self.rms_finish(D)

        m1(0)
        for fb in range(16):
            if fb + 1 < 16:
                m1(fb + 1)
            m2(fb)

    def l1_conv(self, h):
        k = self.k
        ar = np.arange
        rot = [0, 1, 2, 3, 4, 5, 6, 7]
        for j in range(16):
            banks = {}
            for name, cb in (("c", 2048), ("x", 4096), ("b", 0)):
                spec = [("o_w_in", 0, kc * 128, cb + j * 128 + ar(128)) for kc in range(16)]
                sl, slb = self.ws.next(spec)
                for tg in range(2):
                    pt, pb = self.bank(rot)
                    banks[(name, tg)] = (pt, pb)
                    k.mm([lambda e, kc=kc, pt=pt, tg=tg, sl=sl: e.matmul(pt[:], sl[:, kc * 128:(kc + 1) * 128],
                                                                        self.hT[:, kc, tg * 512:(tg + 1) * 512], start=(kc == 0), stop=(kc == 15))
                          for kc in range(16)], reads=[slb], per=[[self.b_h[kc]] for kc in range(16)], writes=[pb])
                    if name == "c":
                        k.op("dve", lambda e, pt=pt, tg=tg: e.tensor_tensor(self.T0[:, tg * 512:(tg + 1) * 512], pt[:],
                                                                           self.rstd[:, tg * 512:(tg + 1) * 512], ALU.mult),
                             reads=[pb, self.b_rstd], writes=[self.b_T0[tg]])
                    elif name == "x":
                        if tg == 0 and h == 1:
                            k.op("act", lambda e, j=j: e.activation(self.zb[:, 0:2], self.carry[:, j, :], AF.Copy),
                                 reads=[self.b_carry], writes=[self.b_zb])
                        k.op("dve", lambda e, pt=pt, tg=tg: e.tensor_tensor(self.zb[:, 2 + tg * 512:2 + (tg + 1) * 512], pt[:],
                                                                           self.rstd[:, tg * 512:(tg + 1) * 512], ALU.mult),
                             reads=[pb, self.b_rstd], writes=[self.b_zb])
                        k.op("dve", lambda e, tg=tg: e.tensor_tensor(self.zb[:, 2 + tg * 512:2 + (tg + 1) * 512],
                                                                    self.zb[:, 2 + tg * 512:2 + (tg + 1) * 512],
                                                                    self.T0[:, tg * 512:(tg + 1) * 512], ALU.mult),
                             reads=[self.b_zb, self.b_T0[tg]], writes=[self.b_zb])
                self.ws.release()
                if name == "x":
                    if h == 0:
                        k.op("act", lambda e, j=j: e.activation(self.carry[:, j, :], self.zb[:, 1024:1026], AF.Copy),
                             reads=[self.b_zb], writes=[self.b_carry])
                    k.op("dve", lambda e, j=j: e.tensor_scalar(self.T1[:], self.zb[:, 0:1024], self.gcol(104, j), None, ALU.mult),
                         reads=[self.b_zb, self.b_cst], writes=self.b_T1)
                    k.op("dve", lambda e, j=j: e.scalar_tensor_tensor(self.T1[:], self.zb[:, 1:1025], self.gcol(120, j), self.T1[:], ALU.mult, ALU.add),
                         reads=[self.b_zb, self.b_cst] + self.b_T1, writes=self.b_T1)
                    k.op("dve", lambda e, j=j: e.scalar_tensor_tensor(self.T1[:], self.zb[:, 2:1026], self.gcol(136, j), self.T1[:], ALU.mult, ALU.add),
                         reads=[self.b_zb, self.b_cst] + self.b_T1, writes=self.b_T1)
                    k.op("dve", lambda e: e.tensor_tensor(self.T1[:], self.T1[:], self.rstd[:], ALU.mult),
                         reads=self.b_T1 + [self.b_rstd], writes=self.b_T1)
            for tg in range(2):
                pt, pb = banks[("b", tg)]
                k.op("dve", lambda e, pt=pt, tg=tg, j=j: e.tensor_tensor(self.RB[:, j, tg * 512:(tg + 1) * 512], pt[:],
                                                                        self.T1[:, tg * 512:(tg + 1) * 512], ALU.mult),
                     reads=[pb, self.b_T1[tg]], writes=[self.b_rb[j]])


_CACHE = {}


def _get_prog(dbg=None):
    key = ("prog", dbg)
    if key not in _CACHE:
        _CACHE[key] = Prog(dbg=dbg, nt_hint=NT_TILES)
    return _CACHE[key]


def _build_wstream(specs, W):
    nt = len(specs)
    ws = np.zeros((nt, 128, 2048), np.float32)
    for t, spec in enumerate(specs):
        off = 0
        for (name, layer, r0, cols) in spec:
            n = len(cols)
            c0, c1 = int(cols[0]), int(cols[-1])
            m = W[name][layer]
            if c1 - c0 == n - 1:
                ws[t, :, off:off + n] = m[r0:r0 + 128, c0:c1 + 1]
            else:
                ws[t, :, off:off + n] = m[r0:r0 + 128][:, cols]
            off += n
        assert off <= 2048
    return ws


def _fm(v):
    return np.ascontiguousarray(np.asarray(v, np.float32).reshape(-1, 128).T)


def kernel(x, positions, e_norm_mix, e_w_in, e_q_norm, e_w_uq, e_kv_norm, e_w_ukv,
           e_v_norm, e_sgu_w, e_sgu_b, e_mla_out_norm, e_sgu_out_norm, e_w_out,
           o_norm_mix, o_w_in, o_conv_w, o_w_out, mlp_norm, mlp_w1, mlp_w2, final_norm, _dbg=None):
    prog = _get_prog(_dbg)
    W = {"e_w_in": np.asarray(e_w_in), "e_w_uq": np.asarray(e_w_uq), "e_w_ukv": np.asarray(e_w_ukv),
         "e_w_out": np.asarray(e_w_out), "o_w_in": np.asarray(o_w_in), "o_w_out": np.asarray(o_w_out),
         "mlp_w1": np.asarray(mlp_w1), "mlp_w2": np.asarray(mlp_w2)}
    wstream = _build_wstream(prog.ws.specs, W)
    cst = np.zeros((128, 192), np.float32)
    cst[:, 0:16] = _fm(e_norm_mix[0])
    cst[:, 16:32] = _fm(mlp_norm[0])
    cst[:, 32:48] = _fm(o_norm_mix[0])
    cst[:, 48:64] = _fm(mlp_norm[1])
    cst[:, 64:80] = _fm(final_norm)
    cst[:, 80:84] = _fm(e_q_norm[0])
    cst[:, 84:88] = _fm(e_kv_norm[0])
    cst[:, 88:96] = _fm(e_mla_out_norm[0])
    cst[:, 96:104] = _fm(e_sgu_out_norm[0])
    for kk in range(3):
        cst[:, 104 + 16 * kk:120 + 16 * kk] = _fm(np.asarray(o_conv_w)[0, kk])
    inv_freq = (np.float32(10000.0) ** (-np.arange(0, 64, 2, dtype=np.float32) / np.float32(64))).astype(np.float32)
    cst[0:64, 152] = np.concatenate([inv_freq, inv_freq])
    cst[0:32, 153] = -1.0
    cst[32:64, 153] = 1.0
    tri = (np.arange(128)[:, None] <= np.arange(128)[None, :]).astype(np.float32)
    swT = np.ascontiguousarray(np.asarray(e_sgu_w, np.float32)[0].transpose(2, 0, 1)).reshape(128, 1024)
    sgub = np.ascontiguousarray(np.asarray(e_sgu_b, np.float32)[0].reshape(1, 1024))
    gv = np.ascontiguousarray(np.asarray(e_v_norm, np.float32)[0].reshape(1, 1024))
    x = np.asarray(x, np.float32)
    positions = np.asarray(positions, np.int32)
    in_maps = []
    for c in range(NCORE):
        xT = np.ascontiguousarray(x[c].T).reshape(16, 128, SEQ)
        in_maps.append({"xT": xT, "pos": np.ascontiguousarray(positions[c].reshape(1, SEQ)), "wstream": wstream,
                        "cst": cst, "tri": tri, "swT": swT, "sgub": sgub, "gv": gv})
    res = run_bass_kernel_spmd(prog.nc, in_maps, core_ids=list(range(NCORE)))
    out = np.empty((NCORE, SEQ, D), np.float32)
    for c in range(NCORE):
        out[c] = res.results[c]["outT"].reshape(D, SEQ).T
    if _dbg:
        return out, [res.results[c]["dbg"] for c in range(NCORE)]
    return out
```

```python
import math
import numpy as np
import concourse.bass as bass
import concourse.mybir as mybir
from concourse.bass_utils import run_bass_kernel_spmd

F32 = mybir.dt.float32
BF16 = mybir.dt.bfloat16
I32 = mybir.dt.int32
AF = mybir.ActivationFunctionType
ALU = mybir.AluOpType
AX = mybir.AxisListType

D = 2048
SEQ = 2048
TH = 1024
NCORE = 8
EPS = 1e-6
RING = 8
SBUF_BASE = 16512
SBUF_TOP = 229344
SCALE = (128 + 64) ** -0.5
MAGIC = 12582912.0
C1 = 6.28125
C2 = 2.0 * math.pi - C1
NT_TILES = 371


class Buf:
    __slots__ = ("name", "w", "r", "dsem", "dcnt", "ring")

    def __init__(self, name, ring=False):
        self.name = name
        self.w = None
        self.r = {}
        self.dsem = None
        self.dcnt = 0
        self.ring = ring


class K:
    def __init__(self, nc):
        self.nc = nc
        self.eng = {"pe": nc.tensor, "act": nc.scalar, "dve": nc.vector,
                    "pool": nc.gpsimd, "sp": nc.sync}
        self.esem = {}
        self.ecnt = {}
        for e in ("pe", "act", "dve"):
            self.esem[e] = nc.alloc_semaphore(name="c_" + e)
            self.ecnt[e] = 0
        self.known = {e: {} for e in self.eng}
        self.prims = []
        self.ninst = {e: 0 for e in self.eng}

    def _wait(self, e, tok):
        if tok is None:
            return
        sem, val = tok
        if e == "pe" and sem is self.esem["pe"]:
            return
        key = id(sem)
        if self.known[e].get(key, 0) >= val:
            return
        self.eng[e].wait_ge(sem, val)
        self.known[e][key] = val
        self.ninst[e] += 1

    def _deps(self, e, reads, writes):
        for b in reads:
            self._wait(e, b.w)
        for b in writes:
            self._wait(e, b.w)
            for t in list(b.r.values()):
                self._wait(e, t)

    def _pend(self, e, reads, writes):
        out = {}
        toks = [b.w for b in reads]
        for b in writes:
            toks.append(b.w)
            toks.extend(b.r.values())
        for tok in toks:
            if tok is None:
                continue
            sem, val = tok
            if e == "pe" and sem is self.esem["pe"]:
                continue
            key = id(sem)
            if self.known[e].get(key, 0) >= val:
                continue
            if key in out and out[key][1] >= val:
                continue
            out[key] = tok
        return list(out.values())

    def _issue(self, e, pend, fn):
        emb = pend.pop() if pend else None
        for sem, val in pend:
            self.eng[e].wait_ge(sem, val)
            self.known[e][id(sem)] = val
            self.ninst[e] += 1
        ins = fn(self.eng[e])
        if emb is not None:
            ins._wait_ge(emb[0], emb[1])
            self.known[e][id(emb[0])] = emb[1]
        return ins

    @staticmethod
    def _commit(tok, reads, writes):
        key = id(tok[0])
        for b in reads:
            old = b.r.get(key)
            if old is None or old[1] < tok[1]:
                b.r[key] = tok
        for b in writes:
            b.w = tok
            b.r = {}

    def op(self, e, fn, reads=(), writes=()):
        ins = self._issue(e, self._pend(e, reads, writes), fn)
        self.ecnt[e] += 1
        ins.then_inc(self.esem[e], 1)
        tok = (self.esem[e], self.ecnt[e])
        self._commit(tok, reads, writes)
        self.ninst[e] += 1
        return tok

    def mm(self, fns, reads=(), writes=(), per=None):
        e = "pe"
        ins = None
        for i, fn in enumerate(fns):
            r_i = (list(reads) if i == 0 else []) + (list(per[i]) if per is not None else [])
            ins = self._issue(e, self._pend(e, r_i, writes if i == 0 else ()), fn)
        self.ecnt[e] += 1
        ins.then_inc(self.esem[e], 1)
        tok = (self.esem[e], self.ecnt[e])
        if per is not None:
            reads = list(reads) + [b for p in per for b in p]
        self._commit(tok, reads, writes)
        self.ninst[e] += len(fns)
        return tok

    def dma(self, q, pairs, prim, reads=(), writes=()):
        if prim.dsem is None:
            prim.dsem = self.nc.alloc_semaphore(name="d_" + prim.name)
            self.prims.append(prim)
        if prim.dcnt:
            self._wait(q, (prim.dsem, prim.dcnt))
        self._deps(q, reads, writes)
        for out_ap, in_ap in pairs:
            ins = self.eng[q].dma_start(out=out_ap, in_=in_ap)
            prim.dcnt += 16
            ins.then_inc(prim.dsem, 16)
            self.ninst[q] += 1
        tok = (prim.dsem, prim.dcnt)
        self._commit(tok, reads, writes)
        return tok

    @staticmethod
    def handoff(old, new):
        acc = {}
        for b in old:
            toks = list(b.r.values()) + ([b.w] if b.w is not None else [])
            for t in toks:
                key = id(t[0])
                if key not in acc or acc[key][1] < t[1]:
                    acc[key] = t
        for b in new:
            b.w = None
            b.r = dict(acc)

    def barrier(self):
        toks = [(self.esem[x], self.ecnt[x]) for x in ("pe", "act", "dve") if self.ecnt[x]]
        toks += [(b.dsem, b.dcnt) for b in self.prims if b.dcnt and not b.ring]
        for e in ("pe", "act", "dve", "sp"):
            for t in toks:
                self._wait(e, t)

    def final(self):
        for b in self.prims:
            if b.dcnt:
                self._wait("sp", (b.dsem, b.dcnt))
        for x in ("pe", "act", "dve"):
            self._wait("sp", (self.esem[x], self.ecnt[x]))


class WStream:
    def __init__(self, k, nc, wdram, base_off, nt):
        self.k, self.nc, self.w, self.nt = k, nc, wdram, nt
        self.slots = [nc.alloc_sbuf_tensor_at(f"ws{i}", [128, 2048], BF16, offset=base_off + i * 4096)
                      for i in range(RING)]
        self.bufs = [Buf(f"ws{i}", ring=True) for i in range(RING)]
        self.specs = []
        self.consumed = 0
        self.released = 0
        self.issued = 0

    def pump(self):
        while self.issued < min(2 * self.nt, self.released + RING):
            s = self.issued % RING
            self.k.dma("pool", [(self.slots[s][:], self.w[self.issued % self.nt])], self.bufs[s],
                       writes=[self.bufs[s]])
            self.issued += 1

    def next(self, spec):
        if self.consumed < self.nt:
            self.specs.append(spec)
        assert self.consumed < 2 * self.nt
        assert self.issued > self.consumed, (self.issued, self.consumed)
        s = self.consumed % RING
        self.consumed += 1
        return self.slots[s], self.bufs[s]

    def release(self, n=1):
        self.released += n
        assert self.released <= self.consumed
        self.pump()


class Prog:
    def __init__(self, dbg=None, nt_hint=None):
        self.dbg = dbg
        nc = bass.Bass("TRN2", target_bir_lowering=False)
        self.nc = nc
        self.k = K(nc)
        self.nt_hint = nt_hint
        self.build()

    def sb(self, name, shape, dt, off):
        return self.nc.alloc_sbuf_tensor_at(name, shape, dt, offset=SBUF_BASE + off)

    def bank(self, rot):
        i = rot[0]
        rot.append(rot.pop(0))
        return self.ps[i], self.psb[i]

    def build(self):
        nc, k = self.nc, self.k
        NT = self.nt_hint
        self.xT = nc.dram_tensor("xT", [16, 128, SEQ], F32, kind="ExternalInput").ap()
        self.posd = nc.dram_tensor("pos", [1, SEQ], I32, kind="ExternalInput").ap()
        self.wd = nc.dram_tensor("wstream", [NT, 128, 2048], F32, kind="ExternalInput").ap()
        self.cstd = nc.dram_tensor("cst", [128, 192], F32, kind="ExternalInput").ap()
        self.trid = nc.dram_tensor("tri", [128, 128], F32, kind="ExternalInput").ap()
        self.swd = nc.dram_tensor("swT", [128, 1024], F32, kind="ExternalInput").ap()
        self.sbd = nc.dram_tensor("sgub", [1, 1024], F32, kind="ExternalInput").ap()
        self.gvd = nc.dram_tensor("gv", [1, 1024], F32, kind="ExternalInput").ap()
        self.outT = nc.dram_tensor("outT", [16, 128, SEQ], F32, kind="ExternalOutput").ap()
        self.kd = nc.dram_tensor("kd", [128, 8, TH], BF16, kind="Internal").ap()
        self.vd = nc.dram_tensor("vd", [128, 8, 1024], BF16, kind="Internal").ap()
        if self.dbg:
            self.dbgd = nc.dram_tensor("dbg", [16, 128, TH], F32, kind="ExternalOutput").ap()
        self.pp = [nc.alloc_psum_tensor(f"pp{i}", [128, 1024], F32) for i in range(4)]
        self.ps = [self.pp[i // 2][:, (i % 2) * 512:(i % 2 + 1) * 512] for i in range(8)]
        self.psb = [Buf(f"ps{i}") for i in range(8)]
        o = 0
        self.cst = self.sb("cst", [128, 192], F32, o); o += 768
        self.ones16 = self.sb("ones16", [128, 128], BF16, o); o += 256
        self.tri16 = self.sb("tri16", [128, 128], BF16, o); o += 256
        self.ones32 = self.sb("ones32", [1, 128], F32, o); o += 512
        self.sgub = self.sb("sgub", [1, 1024], F32, o); o += 4096
        self.swT16 = self.sb("swT16", [128, 8, 128], BF16, o); o += 2048
        self.gvb = self.sb("gvb", [128, 1024], F32, o); o += 4096
        self.Ctab = self.sb("Ctab", [64, TH], F32, o); o += 4096
        self.Stab = self.sb("Stab", [64, TH], F32, o); o += 4096
        self.krT = self.sb("krT", [64, SEQ], BF16, o); o += 4096
        self.carry = self.sb("carry", [128, 16, 2], F32, o); o += 128
        self.zb = self.sb("zb", [128, 1026], F32, o); self.zbi = self.sb("zbi", [128, 1026], I32, o); o += 4128
        self.st = self.sb("st", [128, 2, 32], F32, o); o += 256
        self.rstd = self.sb("rstd", [128, TH], F32, o); o += 4096
        self.sq = [self.sb(f"sq{i}", [128, TH], BF16, o + 2048 * i) for i in range(2)]; SQo = o; o += 4096
        self.T0 = self.sb("T0", [128, TH], F32, o); T0o = o; o += 4096
        self.T1 = self.sb("T1", [128, TH], F32, o); T1o = o; o += 4096
        self.PT = [self.sb(f"PT{i}", [128, 512], BF16, T1o + 1024 * i) for i in range(4)]
        self.ws = WStream(k, nc, self.wd, SBUF_BASE + o, NT); o += RING * 4096
        XO = o; o += 65536
        HO = o; o += 32768
        BO = o; o += 32768
        assert SBUF_BASE + o <= SBUF_TOP, o
        self.xs = self.sb("xs", [128, 16, TH], F32, XO)
        self.posi = self.zbi[0:64, 2:1026]
        self.ang = self.zb[0:64, 2:1026]
        self.tt = self.sb("tt", [64, TH], F32, SQo)
        self.rr = self.Ctab
        self.ob = self.sb("ob", [128, 16, TH], F32, HO)
        self.cq32 = self.sb("cq32", [128, 4, TH], F32, XO)
        self.cqn = self.sb("cqn", [128, 4, TH], BF16, XO + 16384)
        self.ckvn = self.sb("ckvn", [128, 4, TH], BF16, XO + 24576)
        self.vn = self.sb("vn", [128, 8, 1024], BF16, XO + 32768)
        self.ug = [self.sb(f"ug{i}", [128, TH], F32, XO + 49152 + 4096 * i) for i in range(2)]
        self.kh = [self.sb(f"kh{i}", [128, SEQ], BF16, XO + 4096 * i) for i in range(2)]
        self.qn = [self.sb(f"qn{i}", [128, TH], BF16, XO + 8192 + 4096 * i) for i in range(2)]
        self.qr = [self.sb(f"qr{i}", [64, TH], BF16, XO + 8192 + 4096 * i + 2048) for i in range(2)]
        self.a32 = self.sb("a32", [128, 8, TH], F32, XO + 32768)
        self.hT = self.sb("hT", [128, 16, TH], BF16, HO)
        self.RB = self.sb("RB", [128, 16, TH], BF16, BO)
        self.s32hi = self.sb("s32hi", [128, 4, TH], F32, BO)
        B = Buf
        self.b_cst, self.b_ones, self.b_tri, self.b_sgub, self.b_sw, self.b_gv = (
            B("cst"), B("ones"), B("tri"), B("sgub"), B("sw"), B("gv"))
        self.b_C, self.b_S, self.b_kr, self.b_carry, self.b_zb = B("C"), B("S"), B("kr"), B("carry"), B("zb")
        self.b_st = [B("st0"), B("st1")]
        self.b_rstd = B("rstd")
        self.b_sq = [B("sq0"), B("sq1")]
        self.b_T0 = [B("T0a"), B("T0b")]
        self.b_T1 = [B("T1a"), B("T1b")]
        self.b_PT = [B(f"PT{i}") for i in range(4)]
        self.b_xs = [B(f"xs{i}") for i in range(16)]
        self.b_xsio = [B(f"xsio{i}") for i in range(4)]
        self.b_cq = self.b_xs[0:4]
        self.b_cqn, self.b_ckvn = self.b_xs[4], self.b_xs[6]
        self.b_cqn2, self.b_ckvn2 = self.b_xs[5], self.b_xs[7]
        self.b_vn = [self.b_xs[8 + i // 2] for i in range(8)]
        self.b_ug = [self.b_xs[12], self.b_xs[13]]
        self.b_kh = self.b_xs[0:2]
        self.b_qn = self.b_xs[2:4]
        self.b_qr = self.b_xs[2:4]
        self.b_a32 = self.b_xs[8:16]
        self.b_h = [B(f"h{i}") for i in range(16)]
        self.b_rb = [B(f"rb{i}") for i in range(16)]
        self.b_vd, self.b_kd, self.b_out, self.b_dbg = B("vd"), B("kd"), B("out"), B("dbg")
        self.b_vio = B("vio")
        self.b_oio = [B(f"oio{i}") for i in range(4)]
        self.b_ob = [[self.b_h[2 * c], self.b_h[2 * c + 1]] for c in range(8)] + \
                    [[self.b_rb[2 * c], self.b_rb[2 * c + 1]] for c in range(8)]

        self.init_consts()
        self.tables(0)
        self.load_xs(0)
        for b in self.b_xsio:
            k._wait("pool", (b.dsem, b.dcnt))
        self.ws.pump()
        for h in range(2):
            self.half(h)
        assert self.ws.consumed == 2 * NT and len(self.ws.specs) == NT, (self.ws.consumed, len(self.ws.specs))
        k.final()

    def init_consts(self):
        k = self.k
        k.dma("sp", [(self.cst[:], self.cstd)], self.b_cst, writes=[self.b_cst])
        k.dma("sp", [(self.T1[:, 0:128], self.trid)], self.b_T1[0], writes=[self.b_T1[0]])
        k.dma("sp", [(self.T0[:], self.swd)], self.b_T0[0], writes=self.b_T0)
        k.dma("sp", [(self.sgub[:], self.sbd)], self.b_sgub, writes=[self.b_sgub])
        k.dma("sp", [(self.gvb[:], self.gvd.partition_broadcast(128))], self.b_gv, writes=[self.b_gv])
        k.op("dve", lambda e: e.memset(self.ones16[:], 1.0), writes=[self.b_ones])
        k.op("dve", lambda e: e.memset(self.ones32[:], 1.0), writes=[self.b_ones])
        k.op("dve", lambda e: e.tensor_copy(self.tri16[:], self.T1[:, 0:128]), reads=[self.b_T1[0]], writes=[self.b_tri])
        for g in range(8):
            k.op("dve", lambda e, g=g: e.tensor_tensor(self.swT16[:, g, :], self.T0[:, g * 128:(g + 1) * 128],
                                                      self.T1[:, 0:128], ALU.mult),
                 reads=self.b_T0 + [self.b_T1[0]], writes=[self.b_sw])
        k.op("dve", lambda e: e.memset(self.zb[:, 0:2], 0.0), writes=[self.b_zb])
        k.barrier()

    def gcol(self, base, i):
        return self.cst[:, base + i:base + i + 1]

    def rms_rstd(self, chunks, nfeat, banks=(6, 7)):
        k = self.k
        n = len(chunks)
        for c, (ap, bufs) in enumerate(chunks):
            sq, sqb = self.sq[c % 2], self.b_sq[c % 2]
            k.op("act", lambda e, ap=ap, sq=sq: e.activation(sq[:], ap, AF.Square), reads=bufs, writes=[sqb])
            for tg in range(2):
                pt, pb = self.ps[banks[tg]], self.psb[banks[tg]]
                k.mm([lambda e, pt=pt, sq=sq, tg=tg, c=c: e.matmul(pt[:], self.ones16[:], sq[:, tg * 512:(tg + 1) * 512],
                                                                 start=(c == 0), stop=(c == n - 1))],
                     reads=[sqb, self.b_ones], writes=[pb])
        for tg in range(2):
            pt, pb = self.ps[banks[tg]], self.psb[banks[tg]]
            k.op("act", lambda e, pt=pt, tg=tg: e.activation(self.rstd[:, tg * 512:(tg + 1) * 512], pt[:], AF.Sqrt,
                                                            bias=EPS, scale=1.0 / nfeat),
                 reads=[pb], writes=[self.b_rstd])
        k.op("dve", lambda e: e.reciprocal(self.rstd[:], self.rstd[:]), reads=[self.b_rstd], writes=[self.b_rstd])

    def norm_apply(self, src, srcbufs, gbase, i, dst, dstbufs):
        self.k.op("dve", lambda e: e.scalar_tensor_tensor(dst, src, self.gcol(gbase, i), self.rstd[:], ALU.mult, ALU.mult),
                  reads=list(srcbufs) + [self.b_rstd, self.b_cst], writes=list(dstbufs))

    def rope_evac(self, pa, pab, pbk, pbb, tg, out_ap, outbufs):
        k = self.k
        ta, tb = self.T0[0:64, 0:512], self.T0[0:64, 512:1024]
        k.op("dve", lambda e: e.tensor_tensor(ta, pa[0:64, :], self.Ctab[:, tg * 512:(tg + 1) * 512], ALU.mult),
             reads=[pab, self.b_C], writes=[self.b_T0[0]])
        k.op("dve", lambda e: e.tensor_tensor(tb, pbk[0:64, :], self.Stab[:, tg * 512:(tg + 1) * 512], ALU.mult),
             reads=[pbb, self.b_S], writes=[self.b_T0[1]])
        k.op("dve", lambda e: e.tensor_tensor(out_ap, ta, tb, ALU.add),
             reads=self.b_T0, writes=outbufs)

    def tap(self, name, chunks):
        if self.dbg != name or getattr(self, "_tapped", False):
            return
        self._tapped = True
        k = self.k
        k.barrier()
        for c, (ap, bufs) in enumerate(chunks):
            p = ap.shape[0]
            n = ap.shape[-1]
            k.op("act", lambda e, ap=ap, p=p, n=n: e.activation(self.T0[0:p, 0:n], ap, AF.Copy),
                 reads=bufs, writes=self.b_T0)
            k.dma("sp", [(self.dbgd[c, 0:p, 0:n], self.T0[0:p, 0:n])], self.b_T0[0], reads=self.b_T0, writes=[self.b_dbg])
        k.barrier()

    def half(self, h):
        k = self.k
        t0 = h * TH
        self.rms_rstd([(self.xs[:, c, :], [self.b_xs[c]]) for c in range(16)], D)
        for c in range(16):
            self.norm_apply(self.xs[:, c, :], [self.b_xs[c]], 0, c, self.hT[:, c, :], [self.b_h[c]])
        self.tap("h0", [(self.hT[:, c, :], [self.b_h[c]]) for c in range(16)])
        self.l0_inproj(h)
        k.handoff(self.b_T1, self.b_PT)
        self.l0_attn(h)
        k.handoff(self.b_PT, self.b_T1)
        self.tap("mix", [(self.RB[:, c, :], [self.b_rb[c]]) for c in range(16)])
        self.out_proj("e_w_out", 0, xh=h, next_g=16)
        self.tap("x1", [(self.xs[:, c, :], [self.b_xs[c]]) for c in range(16)])
        self.mlp(0, next_g=32)
        self.tap("x2", [(self.xs[:, c, :], [self.b_xs[c]]) for c in range(16)])
        self.l1_conv(h)
        self.out_proj("o_w_out", 0, next_g=48)
        self.tap("x3", [(self.xs[:, c, :], [self.b_xs[c]]) for c in range(16)])
        if h == 0:
            self.tables(1)
        self.mlp(1)
        for c in range(16):
            self.norm_apply(self.xs[:, c, :], [self.b_xs[c]], 64, c, self.ob[:, c, :], self.b_ob[c])
        if h == 0:
            self.load_xs(1)
        for b in range(4):
            k.dma("sp", [(self.outT[cc, :, t0:t0 + TH], self.ob[:, cc, :]) for cc in range(4 * b, 4 * b + 4)],
                  self.b_oio[b], reads=[x for cc in range(4 * b, 4 * b + 4) for x in self.b_ob[cc]], writes=[self.b_out])

    def load_xs(self, h):
        t0 = h * TH
        for b in range(4):
            self.k.dma("sp", [(self.xs[:, c, :], self.xT[c, :, t0:t0 + TH]) for c in range(4 * b, 4 * b + 4)],
                       self.b_xsio[b], writes=self.b_xs[4 * b:4 * b + 4])

    def tables(self, h):
        k = self.k
        t0 = h * TH
        TB = [self.b_zb, self.b_C] + self.b_sq
        k.dma("sp", [(self.posi[:], self.posd[:, t0:t0 + TH].partition_broadcast(64))], self.b_zb, writes=[self.b_zb])
        k.op("dve", lambda e: e.tensor_copy(self.ang[:], self.posi[:]), reads=TB, writes=TB)
        k.op("dve", lambda e: e.tensor_scalar(self.ang[:], self.ang[:], self.cst[0:64, 152:153], None, ALU.mult),
             reads=TB + [self.b_cst], writes=TB)
        for which in range(2):
            if which == 1:
                k.op("dve", lambda e: e.tensor_scalar(self.ang[:], self.ang[:], math.pi / 2, None, ALU.add),
                     reads=TB, writes=TB)
            k.op("dve", lambda e: e.tensor_scalar(self.tt[:], self.ang[:], 1.0 / (2 * math.pi), MAGIC, ALU.mult, ALU.add),
                 reads=TB, writes=TB)
            k.op("dve", lambda e: e.tensor_scalar(self.tt[:], self.tt[:], MAGIC, None, ALU.subtract), reads=TB, writes=TB)
            k.op("dve", lambda e: e.scalar_tensor_tensor(self.rr[:], self.tt[:], -C1, self.ang[:], ALU.mult, ALU.add),
                 reads=TB, writes=TB)
            k.op("dve", lambda e: e.scalar_tensor_tensor(self.rr[:], self.tt[:], -C2, self.rr[:], ALU.mult, ALU.add),
                 reads=TB, writes=TB)
            k.op("dve", lambda e: e.tensor_scalar(self.rr[:], self.rr[:], -math.pi, math.pi, ALU.max, ALU.min),
                 reads=TB, writes=TB)
            if which == 0:
                k.op("act", lambda e: e.activation(self.Stab[:], self.rr[:], AF.Sin), reads=TB, writes=[self.b_S])
                k.op("dve", lambda e: e.tensor_scalar(self.Stab[:], self.Stab[:], self.cst[0:64, 153:154], None, ALU.mult),
                     reads=[self.b_S, self.b_cst], writes=[self.b_S])
            else:
                k.op("act", lambda e: e.activation(self.Ctab[:], self.rr[:], AF.Sin), reads=TB, writes=[self.b_C])

    def l0_inproj(self, h):
        k = self.k
        t0 = h * TH
        rot = [0, 1, 2, 3, 4, 5]
        ar = np.arange
        for which, cbase, gbase, dst, dbase in ((0, 0, 80, self.cqn, 4), (1, 512, 84, self.ckvn, 6)):
            for cc in range(4):
                spec = [("e_w_in", 0, kc * 128, cbase + cc * 128 + ar(128)) for kc in range(16)]
                sl, slb = self.ws.next(spec)
                for tg in range(2):
                    pt, pb = self.bank(rot)
                    k.mm([lambda e, kc=kc, pt=pt, sl=sl, tg=tg: e.matmul(pt[:], sl[:, kc * 128:(kc + 1) * 128],
                                                                        self.hT[:, kc, tg * 512:(tg + 1) * 512],
                                                                        start=(kc == 0), stop=(kc == 15)) for kc in range(16)],
                         reads=[slb], per=[[self.b_h[kc]] for kc in range(16)], writes=[pb])
                    k.op("act", lambda e, pt=pt, cc=cc, tg=tg: e.activation(self.cq32[:, cc, tg * 512:(tg + 1) * 512], pt[:], AF.Copy),
                         reads=[pb], writes=[self.b_cq[cc]])
                self.ws.release()
            self.rms_rstd([(self.cq32[:, cc, :], [self.b_cq[cc]]) for cc in range(4)], 512)
            for cc in range(4):
                self.norm_apply(self.cq32[:, cc, :], [self.b_cq[cc]], gbase, cc, dst[:, cc, :], [self.b_xs[dbase + cc // 2]])
        sw = np.concatenate([ar(32, 64), ar(0, 32)])
        spec = [("e_w_in", 0, kc * 128, 1024 + ar(64)) for kc in range(16)] + \
               [("e_w_in", 0, kc * 128, 1024 + sw) for kc in range(16)]
        sl, slb = self.ws.next(spec)
        for tg in range(2):
            pa, pab = self.bank(rot)
            pbk, pbb = self.bank(rot)
            for pp, ppb, base in ((pa, pab, 0), (pbk, pbb, 1024)):
                k.mm([lambda e, kc=kc, pp=pp, base=base, tg=tg: e.matmul(pp[0:64, :], sl[:, base + kc * 64:base + (kc + 1) * 64],
                                                                        self.hT[:, kc, tg * 512:(tg + 1) * 512],
                                                                        start=(kc == 0), stop=(kc == 15)) for kc in range(16)],
                     reads=[slb], per=[[self.b_h[kc]] for kc in range(16)], writes=[ppb])
            self.rope_evac(pa, pab, pbk, pbb, tg, self.krT[:, t0 + tg * 512:t0 + (tg + 1) * 512], [self.b_kr])
        self.ws.release()
        it = 0
        for cg in range(2):
            sls = []
            for kq in range(4):
                spec = [("e_w_in", 0, (kq * 4 + i) * 128, 2112 + cg * 512 + ar(512)) for i in range(4)]
                sls.append(self.ws.next(spec))
            for tc in range(8):
                pt, pb = self.bank(rot)
                k.mm([lambda e, kc=kc, pt=pt, tc=tc: e.matmul(pt[:], self.hT[:, kc, tc * 128:(tc + 1) * 128],
                                                             sls[kc // 4][0][:, (kc % 4) * 512:(kc % 4 + 1) * 512],
                                                             start=(kc == 0), stop=(kc == 15)) for kc in range(16)],
                     reads=[s[1] for s in sls], per=[[self.b_h[kc]] for kc in range(16)], writes=[pb])
                i2 = it % 2
                it += 1
                ta, tab = self.T0[:, i2 * 512:(i2 + 1) * 512], self.b_T0[i2]
                tb, tbb = self.T1[:, i2 * 512:(i2 + 1) * 512], self.b_T1[i2]
                st, stb = self.st[:, i2, :], self.b_st[i2]
                k.op("act", lambda e, ta=ta, pt=pt: e.activation(ta, pt[:], AF.Gelu_apprx_tanh), reads=[pb], writes=[tab])
                k.op("act", lambda e, ta=ta, tb=tb: e.activation(tb, ta, AF.Square), reads=[tab], writes=[tbb])
                k.op("dve", lambda e, ta=ta, st=st: e.tensor_reduce(st[:, 0:4], ta.rearrange("p (g d) -> p g d", g=4), AX.X, ALU.add),
                     reads=[tab], writes=[stb])
                k.op("dve", lambda e, tb=tb, st=st: e.tensor_reduce(st[:, 4:8], tb.rearrange("p (g d) -> p g d", g=4), AX.X, ALU.add),
                     reads=[tbb], writes=[stb])
                k.op("dve", lambda e, st=st: e.tensor_scalar(st[:, 8:12], st[:, 0:4], 1.0 / 128, None, ALU.mult), reads=[stb], writes=[stb])
                k.op("dve", lambda e, st=st: e.tensor_tensor(st[:, 12:16], st[:, 8:12], st[:, 8:12], ALU.mult), reads=[stb], writes=[stb])
                k.op("dve", lambda e, st=st: e.scalar_tensor_tensor(st[:, 16:20], st[:, 4:8], 1.0 / 128, st[:, 12:16], ALU.mult, ALU.subtract),
                     reads=[stb], writes=[stb])
                k.op("act", lambda e, st=st: e.activation(st[:, 20:24], st[:, 16:20], AF.Sqrt, bias=EPS, scale=1.0), reads=[stb], writes=[stb])
                k.op("dve", lambda e, st=st: e.reciprocal(st[:, 24:28], st[:, 20:24]), reads=[stb], writes=[stb])
                for g in range(4):
                    k.op("dve", lambda e, g=g, ta=ta, st=st: e.tensor_scalar(ta[:, g * 128:(g + 1) * 128], ta[:, g * 128:(g + 1) * 128],
                                                                             st[:, 8 + g:9 + g], st[:, 24 + g:25 + g], ALU.subtract, ALU.mult),
                         reads=[tab, stb], writes=[tab])
                k.op("dve", lambda e, ta=ta, tc=tc, cg=cg: e.tensor_tensor(self.vn[:, tc, cg * 512:(cg + 1) * 512], ta,
                                                                           self.gvb[:, cg * 512:(cg + 1) * 512], ALU.mult),
                     reads=[tab, self.b_gv], writes=[self.b_vn[tc]])
            self.ws.release(4)
        self.tap("vn", [(self.vn[:, c, :], [self.b_vn[c]]) for c in range(8)])
        yrot = [6, 7]
        for g in range(8):
            spec = [("e_w_in", 0, kc * 128, 1088 + g * 128 + ar(128)) for kc in range(16)]
            sl, slb = self.ws.next(spec)
            ug, ugb = self.ug[g % 2], self.b_ug[g % 2]
            for tg in range(2):
                pt, pb = self.bank(rot)
                k.mm([lambda e, kc=kc, pt=pt, tg=tg: e.matmul(pt[:], sl[:, kc * 128:(kc + 1) * 128],
                                                             self.hT[:, kc, tg * 512:(tg + 1) * 512],
                                                             start=(kc == 0), stop=(kc == 15)) for kc in range(16)],
                     reads=[slb], per=[[self.b_h[kc]] for kc in range(16)], writes=[pb])
                k.op("act", lambda e, pt=pt, ug=ug, tg=tg: e.activation(ug[:, tg * 512:(tg + 1) * 512], pt[:], AF.Gelu_apprx_tanh),
                     reads=[pb], writes=[ugb])
            self.ws.release()
            if g < 4:
                s32, s32b = self.cq32[:, g, :], [self.b_cq[g]]
            else:
                s32, s32b = self.s32hi[:, g - 4, :], [self.b_rb[2 * (g - 4)], self.b_rb[2 * (g - 4) + 1]]
            for tq in range(2):
                pt, pb = self.ps[yrot[tq]], self.psb[yrot[tq]]
                fns = []
                for c4 in range(4):
                    c = tq * 4 + c4
                    fns.append(lambda e, pt=pt, c=c, c4=c4, g=g: e.matmul(pt[:, c4 * 128:(c4 + 1) * 128],
                                                                        self.vn[:, c, g * 128:(g + 1) * 128], self.swT16[:, g, :],
                                                                        start=True, stop=False))
                    fns.append(lambda e, pt=pt, c4=c4, g=g: e.matmul(pt[:, c4 * 128:(c4 + 1) * 128],
                                                                   self.ones32[0:1, :], self.sgub[0:1, g * 128:(g + 1) * 128],
                                                                   start=False, stop=True))
                k.mm(fns, reads=self.b_vn + [self.b_sw, self.b_ones, self.b_sgub], writes=[pb])
                k.op("dve", lambda e, pt=pt, tq=tq, s32=s32, ug=ug: e.tensor_tensor(s32[:, tq * 512:(tq + 1) * 512], pt[:],
                                                                                    ug[:, tq * 512:(tq + 1) * 512], ALU.mult),
                     reads=[pb, ugb], writes=s32b)
        chunks = [(self.cq32[:, g, :], [self.b_cq[g]]) for g in range(4)] + \
                 [(self.s32hi[:, g, :], [self.b_rb[2 * g], self.b_rb[2 * g + 1]]) for g in range(4)]
        self.tap("s32", chunks)
        self.rms_rstd(chunks, 1024, banks=(4, 5))
        for g in range(8):
            self.norm_apply(chunks[g][0], chunks[g][1], 96, g, self.RB[:, 8 + g, :], [self.b_rb[8 + g]])

    def l0_attn(self, h):
        k = self.k
        t0 = h * TH
        ar = np.arange
        rot = [4, 5, 6, 7]
        Vall, vb = self.hT, self.b_h
        if h == 1:
            k.dma("sp", [(Vall[:, 0:8, :], self.vd)], self.b_vio, reads=[self.b_vd], writes=vb[0:8])
        for hg in range(2):
            cols = np.concatenate([hd * 256 + 128 + ar(128) for hd in range(hg * 4, hg * 4 + 4)])
            spec = [("e_w_ukv", 0, kc * 128, cols) for kc in range(4)]
            sl, slb = self.ws.next(spec)
            for tc in range(8):
                pt, pb = self.bank(rot)
                k.mm([lambda e, kc=kc, pt=pt, tc=tc: e.matmul(pt[:], self.ckvn[:, kc, tc * 128:(tc + 1) * 128],
                                                             sl[:, kc * 512:(kc + 1) * 512], start=(kc == 0), stop=(kc == 3))
                      for kc in range(4)], reads=[slb, self.b_xs[6], self.b_xs[7]], writes=[pb])
                k.op("act", lambda e, pt=pt, tc=tc, hg=hg: e.activation(Vall[:, h * 8 + tc, hg * 512:(hg + 1) * 512], pt[:], AF.Copy),
                     reads=[pb], writes=[vb[h * 8 + tc]])
            self.ws.release()
        if h == 0:
            k.dma("sp", [(self.vd, Vall[:, 0:8, :])], self.b_vio, reads=vb[0:8], writes=[self.b_vd])
        sw = np.concatenate([ar(32, 64), ar(0, 32)])
        orot = [0, 1]
        srot = [2, 3]
        pti = 0
        for hd in range(8):
            i2 = hd % 2
            kh, khb = self.kh[i2], self.b_kh[i2]
            qn, qnb = self.qn[i2], self.b_qn[i2]
            qr, qrb = self.qr[i2], self.b_qr[i2]
            if h == 1:
                k.dma("sp", [(kh[:, 0:TH], self.kd[:, hd, :])], khb, reads=[self.b_kd], writes=[khb])
            spec = [("e_w_uq", 0, kc * 128, hd * 192 + ar(128)) for kc in range(4)] + \
                   [("e_w_uq", 0, kc * 128, hd * 192 + 128 + ar(64)) for kc in range(4)] + \
                   [("e_w_uq", 0, kc * 128, hd * 192 + 128 + sw) for kc in range(4)] + \
                   [("e_w_ukv", 0, kc * 128, hd * 256 + ar(128)) for kc in range(4)]
            sl, slb = self.ws.next(spec)
            for tg in range(2):
                pt, pb = self.bank(rot)
                k.mm([lambda e, kc=kc, pt=pt, tg=tg: e.matmul(pt[:], sl[:, kc * 128:(kc + 1) * 128],
                                                             self.cqn[:, kc, tg * 512:(tg + 1) * 512], start=(kc == 0), stop=(kc == 3))
                      for kc in range(4)], reads=[slb, self.b_xs[4], self.b_xs[5]], writes=[pb])
                k.op("act", lambda e, pt=pt, tg=tg, qn=qn: e.activation(qn[:, tg * 512:(tg + 1) * 512], pt[:], AF.Copy),
                     reads=[pb], writes=[qnb])
                pa, pab = self.bank(rot)
                pbk, pbb = self.bank(rot)
                for pp, ppb, base in ((pa, pab, 512), (pbk, pbb, 768)):
                    k.mm([lambda e, kc=kc, pp=pp, base=base, tg=tg: e.matmul(pp[0:64, :], sl[:, base + kc * 64:base + (kc + 1) * 64],
                                                                            self.cqn[:, kc, tg * 512:(tg + 1) * 512],
                                                                            start=(kc == 0), stop=(kc == 3)) for kc in range(4)],
                         reads=[slb, self.b_xs[4], self.b_xs[5]], writes=[ppb])
                self.rope_evac(pa, pab, pbk, pbb, tg, qr[:, tg * 512:(tg + 1) * 512], [qrb])
                pt, pb = self.bank(rot)
                k.mm([lambda e, kc=kc, pt=pt, tg=tg: e.matmul(pt[:], sl[:, 1024 + kc * 128:1024 + (kc + 1) * 128],
                                                             self.ckvn[:, kc, tg * 512:(tg + 1) * 512], start=(kc == 0), stop=(kc == 3))
                      for kc in range(4)], reads=[slb, self.b_xs[6], self.b_xs[7]], writes=[pb])
                k.op("act", lambda e, pt=pt, tg=tg, kh=kh: e.activation(kh[:, t0 + tg * 512:t0 + (tg + 1) * 512], pt[:], AF.Copy),
                     reads=[pb], writes=[khb])
            self.ws.release()
            if h == 0:
                k.dma("sp", [(self.kd[:, hd, :], kh[:, 0:TH])], khb, reads=[khb], writes=[self.b_kd])
            for tg in range(2):
                Q0 = t0 + tg * 512
                nkb = (Q0 + 512) // 128
                po, pob = self.bank(orot)
                psm, psmb = self.bank(srot)

                def c0_of(j):
                    r = j - Q0 // 128
                    return 128 * r if r > 0 else 0

                def score(j):
                    pt, pb = self.bank(rot)
                    c0 = c0_of(j)
                    k.mm([lambda e: e.matmul(pt[:, c0:512], kh[:, j * 128:(j + 1) * 128], qn[:, tg * 512 + c0:(tg + 1) * 512],
                                             start=True, stop=False),
                          lambda e: e.matmul(pt[:, c0:512], self.krT[:, j * 128:(j + 1) * 128], qr[:, tg * 512 + c0:(tg + 1) * 512],
                                             start=False, stop=True)],
                         reads=[khb, qnb, qrb, self.b_kr], writes=[pb])
                    return pt, pb

                pend_s = [score(jj) for jj in range(min(2, nkb))]
                for j in range(nkb):
                    pt, pb = pend_s.pop(0)
                    if j + 2 < nkb:
                        pend_s.append(score(j + 2))
                    c0 = c0_of(j)
                    r = j - Q0 // 128
                    P, Pb = self.PT[pti % 4], self.b_PT[pti % 4]
                    pti += 1
                    k.op("act", lambda e, pt=pt, P=P, c0=c0: e.activation(P[:, c0:512], pt[:, c0:512], AF.Exp, scale=SCALE),
                         reads=[pb], writes=[Pb])
                    if r >= 0:
                        k.op("dve", lambda e, P=P, c0=c0: e.tensor_tensor(P[:, c0:c0 + 128], P[:, c0:c0 + 128], self.tri16[:], ALU.mult),
                             reads=[Pb, self.b_tri], writes=[Pb])
                    k.mm([lambda e, P=P, c0=c0, j=j: e.matmul(po[:, c0:512], Vall[:, j, hd * 128:(hd + 1) * 128], P[:, c0:512],
                                                             start=(j == 0), stop=(j == nkb - 1))],
                         reads=[Pb, vb[j]], writes=[pob])
                    k.mm([lambda e, P=P, c0=c0, j=j: e.matmul(psm[:, c0:512], self.ones16[:], P[:, c0:512],
                                                             start=(j == 0), stop=(j == nkb - 1))],
                         reads=[Pb, self.b_ones], writes=[psmb])
                rs, rsb = self.T0[:, tg * 512:(tg + 1) * 512], self.b_T0[tg]
                k.op("dve", lambda e, rs=rs, psm=psm: e.reciprocal(rs, psm[:]), reads=[psmb], writes=[rsb])
                k.op("dve", lambda e, rs=rs, po=po, tg=tg, hd=hd: e.tensor_tensor(self.a32[:, hd, tg * 512:(tg + 1) * 512], po[:], rs, ALU.mult),
                     reads=[pob, rsb], writes=[self.b_a32[hd]])
        self.tap("a32", [(self.a32[:, c, :], [self.b_a32[c]]) for c in range(8)])
        self.rms_rstd([(self.a32[:, c, :], [self.b_a32[c]]) for c in range(8)], 1024, banks=(4, 5))
        for hd in range(8):
            self.norm_apply(self.a32[:, hd, :], [self.b_a32[hd]], 88, hd, self.RB[:, hd, :], [self.b_rb[hd]])

    def rms_mm(self, c, n, banks=(6, 7)):
        k = self.k
        sq, sqb = self.sq[c % 2], self.b_sq[c % 2]
        for tg in range(2):
            pt, pb = self.ps[banks[tg]], self.psb[banks[tg]]
            k.mm([lambda e: e.matmul(pt[:], self.ones16[:], sq[:, tg * 512:(tg + 1) * 512], start=(c == 0), stop=(c == n - 1))],
                 reads=[sqb, self.b_ones], writes=[pb])

    def rms_finish(self, nfeat, banks=(6, 7)):
        k = self.k
        for tg in range(2):
            pt, pb = self.ps[banks[tg]], self.psb[banks[tg]]
            k.op("act", lambda e: e.activation(self.rstd[:, tg * 512:(tg + 1) * 512], pt[:], AF.Sqrt, bias=EPS, scale=1.0 / nfeat),
                 reads=[pb], writes=[self.b_rstd])
        k.op("dve", lambda e: e.reciprocal(self.rstd[:], self.rstd[:]), reads=[self.b_rstd], writes=[self.b_rstd])

    def out_proj(self, wname, layer, xh=None, next_g=None):
        k = self.k
        prot = [0, 1, 2]
        ar = np.arange
        stg = [(self.T0, self.b_T0), (self.T1, self.b_T1)]

        def ld(j):
            T, tb = stg[j % 2]
            k.dma("sp", [(T[:], self.xT[j, :, xh * TH:(xh + 1) * TH])], tb[0], writes=tb)

        if xh is not None:
            ld(0)
        for j in range(16):
            if xh is not None and j + 1 < 16:
                ld(j + 1)
            spec = [(wname, layer, kc * 128, j * 128 + ar(128)) for kc in range(16)]
            sl, slb = self.ws.next(spec)
            p = prot[0]
            prot.append(prot.pop(0))
            for tg in range(2):
                pt, pb = self.ps[2 * p + tg], self.psb[2 * p + tg]
                k.mm([lambda e, kc=kc, pt=pt, tg=tg: e.matmul(pt[:], sl[:, kc * 128:(kc + 1) * 128],
                                                             self.RB[:, kc, tg * 512:(tg + 1) * 512], start=(kc == 0), stop=(kc == 15))
                      for kc in range(16)], reads=[slb], per=[[self.b_rb[kc]] for kc in range(16)], writes=[pb])
            pbs = [self.psb[2 * p], self.psb[2 * p + 1]]
            if xh is None:
                k.op("dve", lambda e, j=j, p=p: e.tensor_tensor(self.xs[:, j, :], self.xs[:, j, :], self.pp[p][:], ALU.add),
                     reads=pbs + [self.b_xs[j]], writes=[self.b_xs[j]])
            else:
                T, tb = stg[j % 2]
                k.op("dve", lambda e, j=j, p=p, T=T: e.tensor_tensor(self.xs[:, j, :], T[:], self.pp[p][:], ALU.add),
                     reads=pbs + tb, writes=[self.b_xs[j]])
            self.ws.release()
            if next_g is not None:
                if j >= 2:
                    self.rms_mm(j - 2, 16)
                k.op("act", lambda e, j=j: e.activation(self.hT[:, j, :], self.xs[:, j, :], AF.Copy, scale=self.gcol(next_g, j)),
                     reads=[self.b_xs[j], self.b_cst], writes=[self.b_h[j]])
                sq, sqb = self.sq[j % 2], self.b_sq[j % 2]
                k.op("act", lambda e, j=j, sq=sq: e.activation(sq[:], self.xs[:, j, :], AF.Square), reads=[self.b_xs[j]], writes=[sqb])
        if next_g is not None:
            self.rms_mm(14, 16)
            self.rms_mm(15, 16)
            self.rms_finish(D)

    def mlp(self, layer, next_g=None):
        k = self.k
        ar = np.arange
        prot = [0, 1, 2, 3]

        def pair():
            p = prot[0]
            prot.append(prot.pop(0))
            return p

        tmps = [(self.T0, self.b_T0), (self.T1, self.b_T1)]
        ti = [0]

        def m1(fb):
            ab = (fb % 4) * 4
            for fcl in range(4):
                spec = [("mlp_w1", layer, kc * 128, fb * 512 + fcl * 128 + ar(128)) for kc in range(16)]
                sl, slb = self.ws.next(spec)
                p = pair()
                for tg in range(2):
                    pt, pb = self.ps[2 * p + tg], self.psb[2 * p + tg]
                    k.mm([lambda e, kc=kc, pt=pt, tg=tg: e.matmul(pt[:], sl[:, kc * 128:(kc + 1) * 128],
                                                                 self.hT[:, kc, tg * 512:(tg + 1) * 512], start=(kc == 0), stop=(kc == 15))
                          for kc in range(16)], reads=[slb], per=[[self.b_h[kc]] for kc in range(16)], writes=[pb])
                self.ws.release()
                pbs = [self.psb[2 * p], self.psb[2 * p + 1]]
                tm, tmb = tmps[ti[0] % 2]
                ti[0] += 1
                c = ab + fcl
                k.op("act", lambda e, tm=tm, p=p: e.activation(tm[:], self.pp[p][:], AF.Relu), reads=pbs, writes=tmb)
                k.op("dve", lambda e, tm=tm: e.tensor_tensor(tm[:], tm[:], self.rstd[:], ALU.mult),
                     reads=tmb + [self.b_rstd], writes=tmb)
                k.op("act", lambda e, tm=tm, c=c: e.activation(self.RB[:, c, :], tm[:], AF.Square),
                     reads=tmb, writes=[self.b_rb[c]])

        def m2(fb):
            ab = (fb % 4) * 4
            last = (fb == 15)
            if last and 3 in prot:
                prot.remove(3)
            for dg in range(4):
                spec = [("mlp_w2", layer, fb * 512 + fcl * 128, dg * 512 + ar(512)) for fcl in range(4)]
                sl, slb = self.ws.next(spec)
                for dcl in range(4):
                    j = dg * 4 + dcl
                    p = pair()
                    for tg in range(2):
                        pt, pb = self.ps[2 * p + tg], self.psb[2 * p + tg]
                        k.mm([lambda e, fcl=fcl, pt=pt, tg=tg, dcl=dcl: e.matmul(pt[:], sl[:, fcl * 512 + dcl * 128:fcl * 512 + (dcl + 1) * 128],
                                                                                self.RB[:, ab + fcl, tg * 512:(tg + 1) * 512],
                                                                                start=(fcl == 0), stop=(fcl == 3)) for fcl in range(4)],
                             reads=[slb], per=[[self.b_rb[ab + f]] for f in range(4)], writes=[pb])
                    k.op("dve", lambda e, j=j, p=p: e.tensor_tensor(self.xs[:, j, :], self.xs[:, j, :], self.pp[p][:], ALU.add),
                         reads=[self.psb[2 * p], self.psb[2 * p + 1], self.b_xs[j]], writes=[self.b_xs[j]])
                    if last:
                        if j >= 2:
                            self.rms_mm(j - 2, 16)
                        if next_g is not None:
                            k.op("act", lambda e, j=j: e.activation(self.hT[:, j, :], self.xs[:, j, :], AF.Copy, scale=self.gcol(next_g, j)),
                                 reads=[self.b_xs[j], self.b_cst], writes=[self.b_h[j]])
                        sq, sqb = self.sq[j % 2], self.b_sq[j % 2]
                        k.op("act", lambda e, j=j, sq=sq: e.activation(sq[:], self.xs[:, j, :], AF.Square), reads=[self.b_xs[j]], writes=[sqb])
                self.ws.release()
            if last:
                self.rms_mm(14, 16)
                self.rms_mm(15, 16)
                self.rms_finish(D)

        m1(0)
        for fb in range(16):
            if fb + 1 < 16:
                m1(fb + 1)
            m2(fb)

    def l1_conv(self, h):
        k = self.k
        ar = np.arange
        rot = [0, 1, 2, 3, 4, 5, 6, 7]
        for j in range(16):
            banks = {}
            for name, cb in (("c", 2048), ("x", 4096), ("b", 0)):
                spec = [("o_w_in", 0, kc * 128, cb + j * 128 + ar(128)) for kc in range(16)]
                sl, slb = self.ws.next(spec)
                for tg in range(2):
                    pt, pb = self.bank(rot)
                    banks[(name, tg)] = (pt, pb)
                    k.mm([lambda e, kc=kc, pt=pt, tg=tg, sl=sl: e.matmul(pt[:], sl[:, kc * 128:(kc + 1) * 128],
                                                                        self.hT[:, kc, tg * 512:(tg + 1) * 512], start=(kc == 0), stop=(kc == 15))
                          for kc in range(16)], reads=[slb], per=[[self.b_h[kc]] for kc in range(16)], writes=[pb])
                    if name == "c":
                        k.op("dve", lambda e, pt=pt, tg=tg: e.tensor_tensor(self.T0[:, tg * 512:(tg + 1) * 512], pt[:],
                                                                           self.rstd[:, tg * 512:(tg + 1) * 512], ALU.mult),
                             reads=[pb, self.b_rstd], writes=[self.b_T0[tg]])
                    elif name == "x":
                        if tg == 0 and h == 1:
                            k.op("act", lambda e, j=j: e.activation(self.zb[:, 0:2], self.carry[:, j, :], AF.Copy),
                                 reads=[self.b_carry], writes=[self.b_zb])
                        k.op("dve", lambda e, pt=pt, tg=tg: e.tensor_tensor(self.zb[:, 2 + tg * 512:2 + (tg + 1) * 512], pt[:],
                                                                           self.rstd[:, tg * 512:(tg + 1) * 512], ALU.mult),
                             reads=[pb, self.b_rstd], writes=[self.b_zb])
                        k.op("dve", lambda e, tg=tg: e.tensor_tensor(self.zb[:, 2 + tg * 512:2 + (tg + 1) * 512],
                                                                    self.zb[:, 2 + tg * 512:2 + (tg + 1) * 512],
                                                                    self.T0[:, tg * 512:(tg + 1) * 512], ALU.mult),
                             reads=[self.b_zb, self.b_T0[tg]], writes=[self.b_zb])
                self.ws.release()
                if name == "x":
                    if h == 0:
                        k.op("act", lambda e, j=j: e.activation(self.carry[:, j, :], self.zb[:, 1024:1026], AF.Copy),
                             reads=[self.b_zb], writes=[self.b_carry])
                    k.op("dve", lambda e, j=j: e.tensor_scalar(self.T1[:], self.zb[:, 0:1024], self.gcol(104, j), None, ALU.mult),
                         reads=[self.b_zb, self.b_cst], writes=self.b_T1)
                    k.op("dve", lambda e, j=j: e.scalar_tensor_tensor(self.T1[:], self.zb[:, 1:1025], self.gcol(120, j), self.T1[:], ALU.mult, ALU.add),
                         reads=[self.b_zb, self.b_cst] + self.b_T1, writes=self.b_T1)
                    k.op("dve", lambda e, j=j: e.scalar_tensor_tensor(self.T1[:], self.zb[:, 2:1026], self.gcol(136, j), self.T1[:], ALU.mult, ALU.add),
                         reads=[self.b_zb, self.b_cst] + self.b_T1, writes=self.b_T1)
                    k.op("dve", lambda e: e.tensor_tensor(self.T1[:], self.T1[:], self.rstd[:], ALU.mult),
                         reads=self.b_T1 + [self.b_rstd], writes=self.b_T1)
            for tg in range(2):
                pt, pb = banks[("b", tg)]
                k.op("dve", lambda e, pt=pt, tg=tg, j=j: e.tensor_tensor(self.RB[:, j, tg * 512:(tg + 1) * 512], pt[:],
                                                                        self.T1[:, tg * 512:(tg + 1) * 512], ALU.mult),
                     reads=[pb, self.b_T1[tg]], writes=[self.b_rb[j]])


_CACHE = {}


def _get_prog(dbg=None):
    key = ("prog", dbg)
    if key not in _CACHE:
        _CACHE[key] = Prog(dbg=dbg, nt_hint=NT_TILES)
    return _CACHE[key]


def _build_wstream(specs, W):
    nt = len(specs)
    ws = np.zeros((nt, 128, 2048), np.float32)
    for t, spec in enumerate(specs):
        off = 0
        for (name, layer, r0, cols) in spec:
            n = len(cols)
            c0, c1 = int(cols[0]), int(cols[-1])
            m = W[name][layer]
            if c1 - c0 == n - 1:
                ws[t, :, off:off + n] = m[r0:r0 + 128, c0:c1 + 1]
            else:
                ws[t, :, off:off + n] = m[r0:r0 + 128][:, cols]
            off += n
        assert off <= 2048
    return ws


def _fm(v):
    return np.ascontiguousarray(np.asarray(v, np.float32).reshape(-1, 128).T)


def kernel(x, positions, e_norm_mix, e_w_in, e_q_norm, e_w_uq, e_kv_norm, e_w_ukv,
           e_v_norm, e_sgu_w, e_sgu_b, e_mla_out_norm, e_sgu_out_norm, e_w_out,
           o_norm_mix, o_w_in, o_conv_w, o_w_out, mlp_norm, mlp_w1, mlp_w2, final_norm, _dbg=None):
    prog = _get_prog(_dbg)
    W = {"e_w_in": np.asarray(e_w_in), "e_w_uq": np.asarray(e_w_uq), "e_w_ukv": np.asarray(e_w_ukv),
         "e_w_out": np.asarray(e_w_out), "o_w_in": np.asarray(o_w_in), "o_w_out": np.asarray(o_w_out),
         "mlp_w1": np.asarray(mlp_w1), "mlp_w2": np.asarray(mlp_w2)}
    wstream = _build_wstream(prog.ws.specs, W)
    cst = np.zeros((128, 192), np.float32)
    cst[:, 0:16] = _fm(e_norm_mix[0])
    cst[:, 16:32] = _fm(mlp_norm[0])
    cst[:, 32:48] = _fm(o_norm_mix[0])
    cst[:, 48:64] = _fm(mlp_norm[1])
    cst[:, 64:80] = _fm(final_norm)
    cst[:, 80:84] = _fm(e_q_norm[0])
    cst[:, 84:88] = _fm(e_kv_norm[0])
    cst[:, 88:96] = _fm(e_mla_out_norm[0])
    cst[:, 96:104] = _fm(e_sgu_out_norm[0])
    for kk in range(3):
        cst[:, 104 + 16 * kk:120 + 16 * kk] = _fm(np.asarray(o_conv_w)[0, kk])
    inv_freq = (np.float32(10000.0) ** (-np.arange(0, 64, 2, dtype=np.float32) / np.float32(64))).astype(np.float32)
    cst[0:64, 152] = np.concatenate([inv_freq, inv_freq])
    cst[0:32, 153] = -1.0
    cst[32:64, 153] = 1.0
    tri = (np.arange(128)[:, None] <= np.arange(128)[None, :]).astype(np.float32)
    swT = np.ascontiguousarray(np.asarray(e_sgu_w, np.float32)[0].transpose(2, 0, 1)).reshape(128, 1024)
    sgub = np.ascontiguousarray(np.asarray(e_sgu_b, np.float32)[0].reshape(1, 1024))
    gv = np.ascontiguousarray(np.asarray(e_v_norm, np.float32)[0].reshape(1, 1024))
    x = np.asarray(x, np.float32)
    positions = np.asarray(positions, np.int32)
    in_maps = []
    for c in range(NCORE):
        xT = np.ascontiguousarray(x[c].T).reshape(16, 128, SEQ)
        in_maps.append({"xT": xT, "pos": np.ascontiguousarray(positions[c].reshape(1, SEQ)), "wstream": wstream,
                        "cst": cst, "tri": tri, "swT": swT, "sgub": sgub, "gv": gv})
    res = run_bass_kernel_spmd(prog.nc, in_maps, core_ids=list(range(NCORE)))
    out = np.empty((NCORE, SEQ, D), np.float32)
    for c in range(NCORE):
        out[c] = res.results[c]["outT"].reshape(D, SEQ).T
    if _dbg:
        return out, [res.results[c]["dbg"] for c in range(NCORE)]
    return out
```
